# Optimizing a Trainium2 kernel written in Bass

```python
import math
import jax, jax.numpy as jnp
from jax import lax
import numpy as np

D_MODEL = 1024
BATCH = 8
SEQ = 2048
DEPTH = 1
DEC_BATCH = 128
DEC_SEQ = 4
PAST_LEN = 16384
PAGE_SIZE = 128

D_MIX = D_MODEL
D_A = D_MIX // 2
D_B = D_MIX - D_A
H_A = 4
DH_A = D_A // H_A
H_B = 4
DK_B = D_B // H_B
DV_B = D_B // H_B
CONV_W = 4
CHUNK = 64
D_FF = -(-8 * D_MODEL // (3 * 256)) * 256
EPS = 1e-6
SPLITS = [2 * D_A, D_A, D_A, H_A, H_A, D_B, D_B, D_B, D_B]
P_IN = sum(SPLITS)
F_A_OFF = 2 * D_A + D_A + D_A + H_A

kernel_name = "hymba_mlstm_hgrn2_decode_step"


def _rmsnorm(x, g):
    xf = x.astype(jnp.float32)
    y = xf * lax.rsqrt(jnp.mean(xf * xf, axis=-1, keepdims=True) + EPS)
    return (y * g.astype(jnp.float32)).astype(x.dtype)


def _headnorm(h, g, n_heads):
    B, T, W = h.shape
    hh = h.reshape(B, T, n_heads, W // n_heads)
    hh = hh * lax.rsqrt(jnp.mean(hh * hh, axis=-1, keepdims=True) + EPS)
    return hh.reshape(B, T, W) * g.astype(jnp.float32)


def _heads(x, n_heads):
    B, T, W = x.shape
    return x.reshape(B, T, n_heads, W // n_heads).transpose(0, 2, 1, 3).astype(jnp.float32)


def _to_chunks(x, L):
    B, H, T = x.shape[:3]
    return jnp.moveaxis(x.reshape(B, H, T // L, L, *x.shape[3:]), 2, 0)


def _from_chunks(y):
    nc, B, H, L = y.shape[:4]
    return jnp.moveaxis(y, 0, 2).reshape(B, H, nc * L, *y.shape[4:])


def _mlstm_chunk(carry, inp):
    C, n, m = carry
    q, k, v, ig, lf = inp
    L = q.shape[2]
    mask = jnp.tril(jnp.ones((L, L), dtype=bool))
    F = jnp.cumsum(lf, axis=-1)
    a = F + m[..., None]
    D = jnp.where(mask, F[..., :, None] - F[..., None, :] + ig[..., None, :], -jnp.inf)
    m_t = jnp.maximum(a, jnp.max(D, axis=-1))
    w_inter = jnp.exp(a - m_t)
    W = jnp.exp(D - m_t[..., None])
    s_qk = jnp.einsum('bhtk,bhsk->bhts', q, k) * W
    num = w_inter[..., None] * jnp.einsum('bhtk,bhkv->bhtv', q, C) + jnp.einsum('bhts,bhsv->bhtv', s_qk, v)
    den = w_inter * jnp.einsum('bhtk,bhk->bht', q, n) + jnp.sum(s_qk, axis=-1)
    h = num / jnp.maximum(jnp.abs(den), jnp.exp(-m_t))[..., None]
    decay = w_inter[..., -1]
    w_last = W[..., -1, :]
    C_new = decay[..., None, None] * C + jnp.einsum('bhs,bhsk,bhsv->bhkv', w_last, k, v)
    n_new = decay[..., None] * n + jnp.einsum('bhs,bhsk->bhk', w_last, k)
    return (C_new, n_new, m_t[..., -1]), h


def _hgrn_chunk(S, inp):
    q, k, lf, v = inp
    L = q.shape[2]
    mask = jnp.tril(jnp.ones((L, L), dtype=bool))
    b = jnp.cumsum(lf, axis=2)
    inter = jnp.einsum('bhtk,bhkv->bhtv', q * jnp.exp(b), S)
    dec = jnp.exp(jnp.where(mask[:, :, None], b[:, :, :, None, :] - b[:, :, None, :, :], -jnp.inf))
    A = jnp.einsum('bhtk,bhsk,bhtsk->bhts', q, k, dec)
    o = inter + jnp.einsum('bhts,bhsv->bhtv', A, v)
    bL = b[:, :, -1:, :]
    S_new = jnp.exp(bL[:, :, 0, :])[..., None] * S + jnp.einsum('bhsk,bhsv->bhkv', k * jnp.exp(bL - b), v)
    return S_new, o


def _mlstm_mixer(qk_raw, v_raw, o_raw, i_raw, f_raw, conv_buf, C0, n0, m0, conv_w, conv_b, norm_g):
    B, T, _ = qk_raw.shape
    xp = jnp.concatenate([conv_buf.astype(qk_raw.dtype), qk_raw], axis=1)
    conv = conv_b
    for j in range(CONV_W):
        conv = conv + conv_w[j] * xp[:, j:j + T]
    new_buf = xp[:, T:]
    qk = jax.nn.silu(conv)
    q = _heads(qk[..., :D_A], H_A) * (DH_A ** -0.5)
    k = _heads(qk[..., D_A:], H_A)
    v = _heads(v_raw, H_A)
    ig = i_raw.astype(jnp.float32).transpose(0, 2, 1)
    lf = jax.nn.log_sigmoid(f_raw.astype(jnp.float32)).transpose(0, 2, 1)
    L = math.gcd(T, CHUNK)
    carry0 = (C0.astype(jnp.float32), n0.astype(jnp.float32), m0.astype(jnp.float32))
    (C, n, m), h = lax.scan(_mlstm_chunk, carry0,
                            (_to_chunks(q, L), _to_chunks(k, L), _to_chunks(v, L), _to_chunks(ig, L), _to_chunks(lf, L)))
    h = _from_chunks(h).transpose(0, 2, 1, 3).reshape(B, T, D_A)
    out = jax.nn.sigmoid(o_raw.astype(jnp.float32)) * _headnorm(h, norm_g, H_A)
    return out, new_buf, C, n, m


def _hgrn_mixer(q_raw, f_raw, i_raw, g_raw, S0, lb, norm_g):
    B, T, _ = q_raw.shape
    fr = f_raw.astype(jnp.float32)
    f = lb + (1.0 - lb) * jax.nn.sigmoid(fr)
    lf = jnp.log(f)
    kk = (1.0 - lb) * jax.nn.sigmoid(-fr)
    q = _heads(jax.nn.silu(q_raw), H_B)
    k = _heads(kk, H_B)
    lf = _heads(lf, H_B)
    v = _heads(i_raw, H_B)
    L = math.gcd(T, CHUNK)
    S, o = lax.scan(_hgrn_chunk, S0.astype(jnp.float32),
                    (_to_chunks(q, L), _to_chunks(k, L), _to_chunks(lf, L), _to_chunks(v, L)))
    o = _from_chunks(o).transpose(0, 2, 1, 3).reshape(B, T, D_B)
    out = _headnorm(o, norm_g, H_B) * jax.nn.silu(g_raw.astype(jnp.float32))
    return out, S


def _layer(x, conv_buf, C0, n0, m0, S0, lb, norm_mix, w_in, b_in, conv_w, conv_b, mlstm_norm,
           hgrn_norm, w_out, norm_ffn, w_gate, w_up, w_down):
    h = _rmsnorm(x, norm_mix)
    p = jnp.einsum('btd,dp->btp', h, w_in) + b_in
    offs = [int(o) for o in np.cumsum(SPLITS)[:-1]]
    qk_a, v_a, o_a, i_a, f_a, q_b, f_b, i_b, g_b = jnp.split(p, offs, axis=-1)
    out_a, buf, C, n, m = _mlstm_mixer(qk_a, v_a, o_a, i_a, f_a, conv_buf, C0, n0, m0, conv_w, conv_b, mlstm_norm)
    out_b, S = _hgrn_mixer(q_b, f_b, i_b, g_b, S0, lb, hgrn_norm)
    mix = jnp.concatenate([out_a, out_b], axis=-1).astype(x.dtype)
    x = x + jnp.einsum('btm,md->btd', mix, w_out)
    h2 = _rmsnorm(x, norm_ffn)
    ff = jax.nn.silu(jnp.einsum('btd,df->btf', h2, w_gate)) * jnp.einsum('btd,df->btf', h2, w_up)
    x = x + jnp.einsum('btf,fd->btd', ff, w_down)
    return x, buf, C, n, m, S


def setup_inputs(seed: int = 0) -> dict:
    key = jax.random.key(seed)
    ks = jax.random.split(key, 24)
    f32 = jnp.float32
    nrm = lambda k, shape, s: jax.random.normal(k, shape, f32) * s
    b_in = nrm(ks[3], (DEPTH, P_IN), 0.02)
    b_in = b_in.at[:, F_A_OFF:F_A_OFF + H_A].add(3.0)
    return {
        "x_prompt": nrm(ks[0], (BATCH, SEQ, D_MODEL), 1.0),
        "x_sample": nrm(ks[1], (DEC_BATCH, DEC_SEQ, D_MODEL), 1.0),
        "state_conv": nrm(ks[4], (DEPTH, DEC_BATCH, CONV_W - 1, 2 * D_A), 1.0),
        "state_mlstm_C": nrm(ks[5], (DEPTH, DEC_BATCH, H_A, DH_A, DH_A), 0.3),
        "state_mlstm_n": nrm(ks[6], (DEPTH, DEC_BATCH, H_A, DH_A), 0.3),
        "state_mlstm_m": nrm(ks[7], (DEPTH, DEC_BATCH, H_A), 1.0),
        "state_hgrn_S": nrm(ks[8], (DEPTH, DEC_BATCH, H_B, DK_B, DV_B), 0.3),
        "norm_mix": 1.0 + nrm(ks[9], (DEPTH, D_MODEL), 0.02),
        "w_in": nrm(ks[2], (DEPTH, D_MODEL, P_IN), D_MODEL ** -0.5),
        "b_in": b_in,
        "conv_w": nrm(ks[10], (DEPTH, CONV_W, 2 * D_A), CONV_W ** -0.5),
        "conv_b": nrm(ks[11], (DEPTH, 2 * D_A), 0.02),
        "mlstm_norm": 1.0 + nrm(ks[12], (DEPTH, D_A), 0.02),
        "hgrn_lb_logits": nrm(ks[13], (DEPTH + 1, D_B), 0.5),
        "hgrn_norm": 1.0 + nrm(ks[14], (DEPTH, D_B), 0.02),
        "w_out": nrm(ks[15], (DEPTH, D_MIX, D_MODEL), D_MIX ** -0.5),
        "norm_ffn": 1.0 + nrm(ks[16], (DEPTH, D_MODEL), 0.02),
        "w_gate": nrm(ks[17], (DEPTH, D_MODEL, D_FF), D_MODEL ** -0.5),
        "w_up": nrm(ks[18], (DEPTH, D_MODEL, D_FF), D_MODEL ** -0.5),
        "w_down": nrm(ks[19], (DEPTH, D_FF, D_MODEL), D_FF ** -0.5),
        "norm_final": 1.0 + nrm(ks[20], (D_MODEL,), 0.02),
    }


def reference(x_prompt, x_sample, state_conv, state_mlstm_C, state_mlstm_n, state_mlstm_m, state_hgrn_S,
              norm_mix, w_in, b_in, conv_w, conv_b, mlstm_norm, hgrn_lb_logits, hgrn_norm, w_out,
              norm_ffn, w_gate, w_up, w_down, norm_final):
    lb_all = jnp.cumsum(jax.nn.softmax(hgrn_lb_logits.astype(jnp.float32), axis=0), axis=0)
    B = x_prompt.shape[0]
    xp, xs = x_prompt, x_sample
    p_states = [[], [], [], [], []]
    s_states = [[], [], [], [], []]
    for l in range(DEPTH):
        lw = (lb_all[l], norm_mix[l], w_in[l], b_in[l], conv_w[l], conv_b[l], mlstm_norm[l],
              hgrn_norm[l], w_out[l], norm_ffn[l], w_gate[l], w_up[l], w_down[l])
        zeros_buf = jnp.zeros((B, CONV_W - 1, 2 * D_A), xp.dtype)
        zeros_C = jnp.zeros((B, H_A, DH_A, DH_A), jnp.float32)
        zeros_n = jnp.zeros((B, H_A, DH_A), jnp.float32)
        zeros_m = jnp.zeros((B, H_A), jnp.float32)
        zeros_S = jnp.zeros((B, H_B, DK_B, DV_B), jnp.float32)
        xp, *sp = _layer(xp, zeros_buf, zeros_C, zeros_n, zeros_m, zeros_S, *lw)
        xs, *ss = _layer(xs, state_conv[l], state_mlstm_C[l], state_mlstm_n[l], state_mlstm_m[l],
                         state_hgrn_S[l], *lw)
        for j in range(5):
            p_states[j].append(sp[j])
            s_states[j].append(ss[j])
    y_prompt = _rmsnorm(xp, norm_final)
    y_sample = _rmsnorm(xs, norm_final)
    conv_p, C_p, n_p, m_p, S_p = [jnp.stack(s, axis=0) for s in p_states]
    conv_s, C_s, n_s, m_s, S_s = [jnp.stack(s, axis=0) for s in s_states]
    return (y_prompt, y_sample, conv_p, C_p, n_p, m_p, S_p, conv_s, C_s, n_s, m_s, S_s)
```

```python
import numpy as np
from contextlib import ExitStack
import concourse.bass as bass
import concourse.mybir as mybir
from concourse.bass_utils import run_bass_kernel_spmd

F32 = mybir.dt.float32
BF16 = mybir.dt.bfloat16
AF = mybir.ActivationFunctionType
ALU = mybir.AluOpType
AX = mybir.AxisListType

D = 1024
PIN = 4104
DFF = 2816
NFC = 22
T = 2048
NSEQ = 16
TSQ = 4
NS = NSEQ * TSQ
EPS = 1e-6
NCORES = 8
LNQ = float(np.log(128.0 ** -0.5))

C_QK, C_VA, C_OA, C_G, C_QB, C_FB, C_VB, C_GB = 0, 1024, 1536, 2048, 2056, 2568, 3080, 3592

K_ID, K_TRI, K_ONE, K_TRIS, K_BM, K_SEL, K_RP, K_RS, K_BMQ, K_NH, K_END = (
    0, 128, 256, 384, 512, 528, 530, 1042, 1106, 2130, 2131)


def make_consts():
    c = np.zeros((128, K_END), np.float32)
    c[:, K_ID:K_ID + 128] = np.eye(128)
    s = np.arange(128)
    c[:, K_TRI:K_TRI + 128] = (s[:, None] <= s[None, :])
    c[:, K_ONE:K_ONE + 128] = 1.0
    t64 = np.arange(64)
    c[:64, K_TRIS:K_TRIS + 64] = ((t64[:, None] // 4 == t64[None, :] // 4) & (t64[:, None] <= t64[None, :]))
    c[:64, K_BM:K_BM + 16] = (t64[:, None] // 4 == np.arange(16)[None, :])
    c[0, K_SEL] = 1.0
    c[1, K_SEL + 1] = 1.0
    r = np.ones(512, np.float32); r[::128] = 0.0
    c[:, K_RP:K_RP + 512] = r
    r = np.ones(64, np.float32); r[::4] = 0.0
    c[:, K_RS:K_RS + 64] = r
    bmq = (np.arange(16)[:, None] == (t64[None, :] // 4)).astype(np.float32).reshape(-1)
    c[:, K_BMQ:K_BMQ + 1024] = bmq
    c[:, K_NH] = -0.5
    return c


class Prog:
    ENG = ["pe", "act", "dve", "pool", "sp"]
    NDMA = {"sp": 64, "pool": 24}
    STRICT_SAME_ENGINE = False
    PRIO_INDEX_WEIGHT = 0.25

    def __init__(self, nc, stack):
        self.nc = nc
        self.sem = {e: stack.enter_context(nc.semaphore("s_" + e)) for e in self.ENG}
        self.dsem = {q: [stack.enter_context(nc.semaphore("d%s%d" % (q, i))) for i in range(n)] for q, n in self.NDMA.items()}
        self.cnt = {e: 0 for e in self.ENG}
        self.dcnt = {q: [0] * n for q, n in self.NDMA.items()}
        self.rr = {q: 0 for q in self.NDMA}
        self.waited = {e: {} for e in self.ENG}
        self.lastw = {}
        self.readers = {}
        self.ops = []
        self.dhist = {}
        self.reset_streams()

    def reset_streams(self):
        self.stream = {e: [] for e in self.ENG}

    def op(self, eng, fns, r=(), w=(), cost=0.3, tbl=None):
        if not isinstance(fns, (list, tuple)):
            fns = [fns]
        self.ops.append((eng, list(fns), tuple(r), tuple(w), False, cost, 0, tbl))

    def dma(self, q, fn, r=(), w=(), cost=3.0, ndesc=128):
        self.ops.append((q, [fn], tuple(r), tuple(w), True, cost, ndesc, None))

    def _schedule(self):
        import heapq
        ops = self.ops
        n = len(ops)
        succ = [[] for _ in range(n)]
        indeg = [0] * n
        lastw = {}
        readers = {}
        for i, (eng, fns, r, w, isd, cost, nd, tb) in enumerate(ops):
            deps = set()
            for k in r:
                if k in lastw:
                    deps.add(lastw[k])
            for k in w:
                if k in lastw:
                    deps.add(lastw[k])
                for x in readers.get(k, ()):
                    deps.add(x)
            deps.discard(i)
            for d in deps:
                succ[d].append(i)
            indeg[i] = len(deps)
            for k in r:
                readers.setdefault(k, []).append(i)
            for k in w:
                lastw[k] = i
                readers[k] = []
        rank = [0.0] * n
        for i in range(n - 1, -1, -1):
            m = 0.0
            for j in succ[i]:
                if rank[j] > m:
                    m = rank[j]
            rank[i] = m + ops[i][5]
        order_key = sorted(range(n), key=lambda i: (i * self.PRIO_INDEX_WEIGHT - rank[i], i))
        prio = [0] * n
        for pos, i in enumerate(order_key):
            prio[i] = pos
        inv = order_key
        ready = {e: [] for e in self.ENG}
        for i in range(n):
            if indeg[i] == 0:
                heapq.heappush(ready[ops[i][0]], prio[i])
        free = {e: 0.0 for e in self.ENG}
        events = []
        start = [0.0] * n
        t = 0.0
        done = 0
        dma_bw_free = 0.0
        cur_tbl = [None]
        while done < n:
            progressed = False
            for e in self.ENG:
                if free[e] <= t and ready[e]:
                    if e == "act" and cur_tbl[0] is not None:
                        cand = heapq.nsmallest(12, ready[e])
                        pick = None
                        for c in cand:
                            if ops[inv[c]][7] is None or ops[inv[c]][7] == cur_tbl[0]:
                                pick = c
                                break
                        if pick is None:
                            pick = cand[0]
                        ready[e].remove(pick)
                        heapq.heapify(ready[e])
                        i = inv[pick]
                    else:
                        i = inv[heapq.heappop(ready[e])]
                    eng, fns, r, w, isd, cost, nd, tb = ops[i]
                    if e == "act" and tb is not None:
                        if cur_tbl[0] is not None and cur_tbl[0] != tb:
                            cost = cost + 1.3
                        cur_tbl[0] = tb
                    start[i] = t
                    if isd:
                        occ = 0.15 if e == "sp" else 3.0
                        xfer = max(cost - 2.0, 0.0)
                        s0 = max(t, dma_bw_free)
                        dma_bw_free = s0 + xfer
                        fin = s0 + xfer + 2.0
                    else:
                        occ = cost
                        fin = t + cost
                    free[e] = t + occ
                    heapq.heappush(events, (fin, i))
                    progressed = True
            if progressed:
                continue
            cands = []
            if events:
                cands.append(events[0][0])
            for e in self.ENG:
                if ready[e] and free[e] > t:
                    cands.append(free[e])
            assert cands, "scheduler deadlock"
            t = max(t, min(cands))
            while events and events[0][0] <= t:
                fin, i = heapq.heappop(events)
                done += 1
                for j in succ[i]:
                    indeg[j] -= 1
                    if indeg[j] == 0:
                        heapq.heappush(ready[ops[j][0]], prio[j])
        order = sorted(range(n), key=lambda i: (start[i], i))
        self.est_time = t
        tot = {}
        for o in ops:
            tot[o[0]] = tot.get(o[0], 0.0) + (o[5] if not o[4] else 0.15)
        print('[sched] ops=%d est=%.1f us' % (n, t), {k: round(v) for k, v in tot.items()})
        return order

    def _semobj(self, k):
        return self.dsem[k[1]][k[2]] if isinstance(k, tuple) else self.sem[k]

    def _need(self, eng, ev):
        k, v, _ = ev
        if self.waited[eng].get(k, 0) >= v:
            return
        self.waited[eng][k] = v
        self.stream[eng].append(("wait", k, v))

    def _deps(self, eng, r, w, is_dma):
        for key in r:
            ev = self.lastw.get(key)
            if ev is not None:
                self._need(eng, ev)
        for key in w:
            ev = self.lastw.get(key)
            strict = is_dma or eng == "pool" or self.STRICT_SAME_ENGINE
            if ev is not None and (strict or ev[2] != eng):
                self._need(eng, ev)
            for rv in self.readers.get(key, []):
                if strict or rv[2] != eng:
                    self._need(eng, rv)

    def _commit(self, ev, r, w):
        for key in r:
            lst = self.readers.setdefault(key, [])
            if ev[2] != "dma":
                lst[:] = [x for x in lst if x[2] != ev[2]]
            lst.append(ev)
        for key in w:
            self.lastw[key] = ev
            self.readers[key] = []

    def _place(self, eng, fns, r, w, isd, nd=0):
        if not isd:
            self._deps(eng, r, w, False)
            self.cnt[eng] += 1
            ev = (eng, self.cnt[eng], eng)
            for f in fns[:-1]:
                self.stream[eng].append(("op", f, None, 0))
            self.stream[eng].append(("op", fns[-1], eng, 1))
            self._commit(ev, r, w)
        else:
            q = eng
            self._deps(q, r, w, True)
            hist_ = self.dhist.setdefault(q, [])
            while hist_ and (len(hist_) >= 4 or sum(x[1] for x in hist_) + nd > 3000):
                self._need(q, hist_.pop(0)[0])
            i = self.rr[q]
            self.rr[q] = (i + 1) % self.NDMA[q]
            k = ("d", q, i)
            if self.dcnt[q][i] > 0:
                self._need(q, (k, self.dcnt[q][i], "dma"))
            self.dcnt[q][i] += 16
            ev = (k, self.dcnt[q][i], "dma")
            self.stream[q].append(("op", fns[0], k, 16))
            hist_.append((ev, nd))
            self._commit(ev, r, w)

    def barrier(self, skip_queues=()):
        evs = [(e, self.cnt[e], e) for e in self.ENG if self.cnt[e] > 0]
        for q in self.NDMA:
            if q in skip_queues:
                continue
            evs += [(("d", q, i), self.dcnt[q][i], "dma") for i in range(self.NDMA[q]) if self.dcnt[q][i] > 0]
        for e in self.ENG:
            for ev in evs:
                if ev[2] != e:
                    self._need(e, ev)

    def finish_phase(self, reorder=True, skip_queues=()):
        order = self._schedule() if reorder else list(range(len(self.ops)))
        for i in order:
            eng, fns, r, w, isd, cost, nd, tb = self.ops[i]
            self._place(eng, fns, r, w, isd, nd)
        self.ops = []
        self.barrier(skip_queues)
        self.emit()

    def emit(self):
        nc = self.nc
        streams = self.stream
        P = self

        def run(eng_obj, name):
            for it in streams[name]:
                if it[0] == "wait":
                    eng_obj.wait_ge(P._semobj(it[1]), it[2])
                else:
                    ins = it[1](eng_obj)
                    if it[2] is not None:
                        ins.then_inc(P._semobj(it[2]), it[3])

        with nc.Block() as block:
            @block.tensor
            def _(e):
                run(e, "pe")

            @block.scalar
            def _(e):
                run(e, "act")

            @block.vector
            def _(e):
                run(e, "dve")

            @block.gpsimd
            def _(e):
                run(e, "pool")

            @block.sync
            def _(e):
                run(e, "sp")
        self.reset_streams()


def build_program():
    nc = bass.Bass("TRN2", target_bir_lowering=False)

    def din(name, shape):
        return nc.dram_tensor(name, list(shape), F32, kind="ExternalInput").ap()

    def dout(name, shape):
        return nc.dram_tensor(name, list(shape), F32, kind="ExternalOutput").ap()

    xp_d = din("xp", [T, D]); xs_d = din("xs", [NS, D]); sconv_d = din("sconv", [NSEQ * 3, D])
    sC_d = din("sC", [NSEQ * 4, 128, 128]); sn_d = din("sn", [NSEQ * 4, 128]); sm_d = din("sm", [NSEQ, 4])
    sS_d = din("sS", [NSEQ * 4, 128, 128])
    w_in_d = din("w_in", [D, PIN]); b_in_d = din("b_in", [1, PIN]); nmix_d = din("norm_mix", [D])
    cw_d = din("conv_w", [4, D]); cb_d = din("conv_b", [D]); gA_d = din("mlstm_norm", [1, 512])
    lbl_d = din("lb_logits", [2, 512]); gB_d = din("hgrn_norm", [1, 512]); w_out_d = din("w_out", [D, D])
    nffn_d = din("norm_ffn", [D]); wg_d = din("w_gate", [D, DFF]); wu_d = din("w_up", [D, DFF])
    wd_d = din("w_down", [DFF, D]); nfin_d = din("norm_final", [1, D]); cst_d = din("cst", [128, K_END]); pcol_d = din("pcol", [128, 80])

    yp_d = dout("y_prompt", [T, D]); ys_d = dout("y_sample", [NS, D])
    convp_d = dout("conv_p", [3, D]); Cp_d = dout("C_p", [4, 128, 128]); np_d = dout("n_p", [4, 128])
    mp_d = dout("m_p", [4, 1]); Sp_d = dout("S_p", [4, 128, 128])
    convs_d = dout("conv_s", [NSEQ * 3, D]); Cs_d = dout("C_s", [NSEQ * 4, 128, 128])
    ns_d = dout("n_s", [NSEQ * 4, 128]); ms_d = dout("m_s", [NSEQ, 4]); Ss_d = dout("S_s", [NSEQ * 4, 128, 128])
    x2_d = nc.dram_tensor("x2_scratch", [T + NS, D], F32, kind="Internal").ap()

    with ExitStack() as top:
        P = Prog(nc, top)
        nc_allow = top.enter_context(nc.allow_non_contiguous_dma(reason="small param layouts"))

        def sb(stack, name, shape, dt=F32):
            return stack.enter_context(nc.sbuf_tensor("sb_" + name, list(shape), dt))

        cst = sb(top, "cst", [128, K_END])
        ident_bf = sb(top, "ident_bf", [128, 128], BF16)
        ones2 = sb(top, "ones2", [2, 128], BF16)
        gfin_bc = sb(top, "gfin_bc", [128, D])
        pcol = sb(top, "pcol", [128, 80])
        gf8 = pcol[:, 8:16]
        psum = [top.enter_context(nc.psum_tensor("ps%d" % i, [128, 512], F32)) for i in range(8)]

        def psv(i, dt=F32):
            return psum[i][:] if dt == F32 else psum[i][:].bitcast(dt)

        ident = cst[:, K_ID:K_ID + 128]
        tri = cst[:, K_TRI:K_TRI + 128]
        onesf = cst[:, K_ONE:K_ONE + 128]
        triS = cst[0:64, K_TRIS:K_TRIS + 64]
        BM = cst[0:64, K_BM:K_BM + 16]

        def fsz(ap):
            n = 1
            for d in ap.shape[1:]:
                n *= int(d)
            return n

        def ecost(eng, ap, extra=0.0):
            n = fsz(ap)
            if eng == "act":
                return 0.22 + n * 0.00085 + extra
            if eng == "dve":
                return 0.12 + n * 0.00105 + extra
            return 0.35 + n * 0.0022 + extra

        def act(out, in_, func, r, w, scale=1.0, bias=0.0, accum=None):
            tb = "A" if func in (AF.Silu, AF.Tanh) else ("B" if func in (AF.Exp, AF.Ln) else None)
            if accum is None:
                P.op("act", lambda e: e.activation(out=out, in_=in_, func=func, scale=scale, bias=bias), r, w, ecost("act", out), tb)
            else:
                P.op("act", lambda e: e.activation(out=out, in_=in_, func=func, scale=scale, bias=bias,
                                                   accum_out=accum), r, w, ecost("act", out, 0.1), tb)

        def ts(eng, out, in0, s1, s2, op0, op1, r, w):
            if s2 is None:
                if eng == "pool":
                    assert op0 == ALU.mult
                    one_col = cst[0:int(out.shape[0]), K_ONE:K_ONE + 1]
                    P.op(eng, lambda e: e.tensor_scalar(out=out, in0=in0, scalar1=s1, scalar2=one_col, op0=ALU.mult, op1=ALU.mult),
                         tuple(r) + ("cst",), w, ecost(eng, out))
                else:
                    P.op(eng, lambda e: e.tensor_scalar(out=out, in0=in0, scalar1=s1, scalar2=None, op0=op0), r, w, ecost(eng, out))
            else:
                P.op(eng, lambda e: e.tensor_scalar(out=out, in0=in0, scalar1=s1, scalar2=s2, op0=op0, op1=op1), r, w, ecost(eng, out))

        def tt(eng, out, in0, in1, op, r, w):
            P.op(eng, lambda e: e.tensor_tensor(out=out, in0=in0, in1=in1, op=op), r, w, ecost(eng, out))

        def stt(out, in0, scalar, in1, op0, op1, r, w):
            P.op("dve", lambda e: e.scalar_tensor_tensor(out=out, in0=in0, scalar=scalar, in1=in1, op0=op0, op1=op1), r, w,
                 ecost("dve", out))

        def cp(eng, out, in_, r, w):
            if eng == "act":
                P.op("act", lambda e: e.copy(out=out, in_=in_), r, w, ecost("act", out))
            else:
                P.op(eng, lambda e: e.tensor_copy(out=out, in_=in_), r, w, ecost(eng, out))

        def mm(items, r, w):
            fns = []
            c = 0.05
            for (o, l, rh, st, sp_) in items:
                fns.append(lambda e, o=o, l=l, rh=rh, st=st, sp_=sp_: e.matmul(o, lhsT=l, rhs=rh, start=st, stop=sp_))
                c += (max(fsz(rh), 64) / 2400.0 + 0.012) * (4.0 if l.dtype == F32 else 1.0)
            P.op("pe", fns, r, w, c)

        def trp(items, idn, r, w):
            fns = []
            c = 0.05
            for (o, i) in items:
                fns.append(lambda e, o=o, i=i: e.transpose(o, i, idn))
                c += 0.07
            P.op("pe", fns, r, w, c)

        def ndesc_of(ap):
            tot = 1
            for d in ap.shape:
                tot *= int(d)
            last = ap.ap[-1]
            run = int(last[1]) if int(last[0]) == 1 else 1
            return max(1, tot // max(run, 1))

        def dma(q, out, in_, r, w, **kw):
            nbytes = out.shape[0] * fsz(out) * 4
            nd = max(ndesc_of(out), ndesc_of(in_))
            P.dma(q, lambda e: e.dma_start(out=out, in_=in_, **kw), r, w, 2.0 + nbytes / 150e3 + nd * 0.002, nd)

        def memset(eng, ap, val, w):
            P.op(eng, lambda e: e.memset(ap, val), (), w, ecost(eng, ap))

        dma("sp", cst[:], cst_d[:, :], [], ["cst"])
        cp("dve", ident_bf[:], ident, ["cst"], ["ident_bf"])
        memset("dve", ones2[:], 1.0, ["ones2"])
        for i in range(8):
            memset("dve", psum[i][:], 0.0, ["P%d" % i])
        dma("sp", gfin_bc[:], nfin_d.partition_broadcast(128), [], ["gfin_bc"])
        dma("sp", pcol[:], pcol_d[:, :], [], ["gf8", "g8", "bcol", "cw", "cb", "lbl"])

        with ExitStack() as A:
            win = sb(A, "win", [128, 8, PIN], BF16)
            wout = sb(A, "wout", [128, 8, D], BF16)
            def wload(blocks, with_out=False):
                for (c0, cn) in blocks:
                    for kc in range(8):
                        dma("pool", win[:, kc, c0:c0 + cn], w_in_d[kc * 128:(kc + 1) * 128, c0:c0 + cn], [], ["win%d_%d" % (c0, kc)],
                            max_dma_last_dim=4096)
                if with_out:
                    for kc in range(8):
                        dma("pool", wout[:, kc, :], w_out_d[kc * 128:(kc + 1) * 128, :], [], ["wout_%d" % kc], max_dma_last_dim=4096)
            wload([(C_QK, 1024)])

            def wk(c0):
                return ["win%d_%d" % (c0, kc) for kc in range(8)]
            WOUTK = ["wout_%d" % kc for kc in range(8)]

            g8 = pcol[:, 0:8]; bcol = pcol[:, 16:32]; bcolh = sb(A, "bcolh", [128, 16])
            cw = pcol[:, 32:64].rearrange("p (c j) -> p c j", j=4); cb = pcol[:, 64:72]
            lbl = pcol[:, 72:80].rearrange("p (l h) -> p l h", h=4); c0t = sb(A, "c0t", [128, 4]); c1t = sb(A, "c1t", [128, 4]); nc1t = sb(A, "nc1t", [128, 4])
            gA_bc = sb(A, "gA_bc", [128, 512]); gB_bc = sb(A, "gB_bc", [128, 512])
            bhl = sb(A, "bhl", [2, 2056], BF16)
            NTA = 256
            xin = [sb(A, "xin%d" % i, [128, D]) for i in range(2)]
            xres = [sb(A, "xres%d" % i, [128, D]) for i in range(2)]
            xn = [sb(A, "xn%d" % i, [128, D], BF16) for i in range(2)]
            ssA = sb(A, "ssA", [128, 8]); rsT = sb(A, "rsT", [128, 8])
            hT = sb(A, "hT", [128, 8, NTA], BF16)
            raw = [sb(A, "raw%d" % i, [128, NTA + 3]) for i in range(2)]
            cacc = [sb(A, "cacc%d" % i, [128, NTA]) for i in range(2)]
            qkT = sb(A, "qkT", [128, 8, NTA], BF16)
            qsh = [sb(A, "qsh%d" % i, [128, NTA]) for i in range(2)]; thh = [sb(A, "thh%d" % i, [128, NTA]) for i in range(2)]
            lfh = [sb(A, "lfh%d" % i, [128, NTA]) for i in range(2)]; bh = [sb(A, "bh%d" % i, [128, NTA]) for i in range(2)]
            eh = [sb(A, "eh%d" % i, [128, NTA]) for i in range(2)]
            qmid = sb(A, "qmid", [128, 4, NTA], BF16); kmid = sb(A, "kmid", [128, 4, NTA], BF16)
            e2 = sb(A, "e2", [128, 4, 16])
            vaug = sb(A, "vaug", [128, 4, 130], BF16); vsc = sb(A, "vsc", [128, 4, 130], BF16)
            go = sb(A, "go", [128, 512]); gt = sb(A, "gt", [128, 8])
            vB = sb(A, "vB", [128, 512], BF16); gg = sb(A, "gg", [128, 512])
            g1 = sb(A, "g1", [128, 4]); g2 = sb(A, "g2", [128, 4]); g3 = sb(A, "g3", [128, 4]); lfA = sb(A, "lfA", [128, 4])
            eq = sb(A, "eq", [128, 4]); ek = sb(A, "ek", [128, 4]); bet = sb(A, "bet", [128, 4])
            eDG = sb(A, "eDG", [128, 64])
            mrun = sb(A, "mrun", [4, NSEQ]); bmx = sb(A, "bmx", [4, NSEQ]); dG4 = sb(A, "dG4", [4, NSEQ])
            PT = sb(A, "PT", [128, 4, 128], BF16); ktok = sb(A, "ktok", [128, 4, 128], BF16)
            dd = sb(A, "dd", [128, 4]); rr = sb(A, "rr", [128, 4]); ssq = sb(A, "ssq", [128, 4]); sc = sb(A, "sc", [128, 4]); t4 = sb(A, "t4", [128, 4])
            junk2 = sb(A, "junk2", [128, 128], BF16)
            mix = sb(A, "mix", [128, D], BF16); mixT = sb(A, "mixT", [128, 8, 128], BF16)
            emS = sb(A, "emS", [128, 64]); mblk = sb(A, "mblk", [4, 4, NSEQ])
            scv = xin[0]; cso = xres[1]
            nS = sb(A, "nS", [128, 64]); nO = sb(A, "nO", [128, 64])
            LATE = []
            Ap = ExitStack()
            HT = [hT, sb(Ap, "hT_1", [128, 8, NTA], BF16)]; QKT = [qkT, sb(Ap, "qkT_1", [128, 8, NTA], BF16)]
            QMT = [qmid, sb(Ap, "qmid_1", [128, 4, NTA], BF16)]; KMT = [kmid, sb(Ap, "kmid_1", [128, 4, NTA], BF16)]
            BMt = [sb(Ap, "bm_%d" % i, [128, 4, 16]) for i in range(2)]; NBMt = [sb(Ap, "nbm_%d" % i, [128, 4, 16]) for i in range(2)]
            E1t = [sb(Ap, "e1_%d" % i, [128, 4, 16]) for i in range(2)]; E2t = [e2, sb(Ap, "e2_1", [128, 4, 16])]
            hist = sb(Ap, "hist", [128, 8, 3])
            Chat = sb(Ap, "Chat", [128, 4, 130]); Chb = sb(Ap, "Chb", [128, 4, 130], BF16)
            S32 = sb(Ap, "S32", [128, 4, 128]); Smb = sb(Ap, "Smb", [128, 4, 128], BF16); Stmp = sb(Ap, "Stmp", [128, 4, 128])

            dma("sp", gA_bc[:], gA_d.partition_broadcast(128), [], ["gA_bc"])
            dma("sp", gB_bc[:], gB_d.partition_broadcast(128), [], ["gB_bc"])
            ts("dve", bcolh[:], bcol[:], 0.5, None, ALU.mult, None, ["bcol"], ["bcolh"])
            ts("dve", gA_bc[:], gA_bc[:], 0.5, None, ALU.mult, None, ["gA_bc"], ["gA_bc"])
            tt("dve", c0t[:], lbl[:, 0, :], lbl[:, 1, :], ALU.subtract, ["lbl"], ["c0t"])
            act(c1t[:], c0t[:], AF.Tanh, ["c0t"], ["c1t"], scale=0.5)
            ts("dve", c0t[:], c1t[:], 0.25, 0.75, ALU.mult, ALU.add, ["c1t"], ["c0t"])
            ts("dve", nc1t[:], c1t[:], 0.25, -0.25, ALU.mult, ALU.add, ["c1t"], ["nc1t"])
            ts("dve", c1t[:], nc1t[:], -1.0, None, ALU.mult, None, ["nc1t", "c1t"], ["c1t"])
            def bofs(col):
                return col - 1024 if col < 2056 else col - 3080 + 1032
            for (c0_, cn_) in [(1024, 1024), (2048, 8), (3080, 1024)]:
                bfull = xin[0][0:2, 0:cn_]; bhif = xin[1][0:2, 0:cn_]; bhi = xn[0][0:2, 0:cn_]
                dma("sp", bfull, b_in_d[:, c0_:c0_ + cn_].partition_broadcast(2), [], ["xin0"])
                cp("dve", bhi, bfull, ["xin0"], ["xn0"])
                cp("dve", bhif, bhi, ["xn0"], ["xin1"])
                tt("dve", bfull, bfull, bhif, ALU.subtract, ["xin0", "xin1"], ["xin0"])
                ts("dve", bhif, bhif, cst[0:2, K_SEL:K_SEL + 1], None, ALU.mult, None, ["xin1", "cst"], ["xin1"])
                stt(bhl[:, bofs(c0_):bofs(c0_) + cn_], bfull, cst[0:2, K_SEL + 1:K_SEL + 2], bhif, ALU.mult, ALU.add,
                    ["xin0", "xin1", "cst"], ["bhl"])

            memset("dve", vaug[:], 1.0, ["vaug"])
            memset("dve", Chat[:], 0.0, ["Chat"])
            memset("dve", S32[:], 0.0, ["S32"])
            memset("dve", mrun[:], 0.0, ["mrun"])
            memset("pool", hist[:], 0.0, ["hist"])

            def bcast_hb(dst, func_scale, src_key):
                mm([(psum[3][:, 0:64], onesf[0:4, :], mblk[:].rearrange("k h b -> k (h b)"), True, True)], ["mblk", "cst"], ["P3"])
                act(dst[:, 0:64].rearrange("p (b h) -> p h b", h=4), psum[3][:, 0:64].rearrange("p (h b) -> p h b", h=4),
                    AF.Exp, ["P3"], [src_key], scale=func_scale)

            def phaseA_super(x_src, row0, NT, TS, sample, stage="XY"):
                nsub = NT // TS
                sp_ = 0 if sample else (row0 // NT) % 2
                hT, qkT, qmid, kmid = HT[sp_], QKT[sp_], QMT[sp_], KMT[sp_]
                e2 = E2t[sp_]
                if not sample:
                    bm, nbm, e1 = BMt[sp_], NBMt[sp_], E1t[sp_]
                kHT, kQK, kQM, kKM = "hT%d" % sp_, "qkT%d" % sp_, "qmid%d" % sp_, "kmid%d" % sp_
                kBM, kNBM, kE1, kE2 = "bm%d" % sp_, "nbm%d" % sp_, "e1%d" % sp_, "e2%d" % sp_
                if "X" in stage:
                    for j in range(nsub):
                        xi = xin[j % 2]; xb_ = xn[j % 2]
                        rows = slice(row0 + j * TS, row0 + (j + 1) * TS)
                        dma("sp", xi[0:TS, :], x_src[rows, :], [], ["xin%d" % (j % 2)])
                        act(xb_[0:TS, :], xi[0:TS, :], AF.Square, ["xin%d" % (j % 2)], ["xn%d" % (j % 2), "ssA"], accum=ssA[0:TS, j:j + 1])
                        ts("dve", rsT[0:TS, j:j + 1], ssA[0:TS, j:j + 1], 1.0 / D, EPS, ALU.mult, ALU.add, ["ssA"], ["rsT"])
                        tt("pool", ssA[0:TS, j:j + 1], rsT[0:TS, j:j + 1], cst[0:TS, K_NH:K_NH + 1], ALU.pow, ["rsT", "cst"], ["ssA"])
                        act(xb_[0:TS, :], xi[0:TS, :], AF.Copy, ["ssA", "xin%d" % (j % 2)], ["xn%d" % (j % 2)], scale=ssA[0:TS, j:j + 1])
                        pT = psv(2, BF16).rearrange("p (c t) -> p c t", c=8)
                        trp([(pT[:, c, 0:TS], xb_[0:TS, c * 128:(c + 1) * 128]) for c in range(8)], ident_bf[0:TS, 0:TS],
                            ["xn%d" % (j % 2), "ident_bf"], ["P2"])
                        tt("dve", hT[:, :, j * TS:(j + 1) * TS], pT[:, :, 0:TS],
                           g8[:].unsqueeze(2).broadcast_to([128, 8, TS]), ALU.mult, ["P2", "g8"], [kHT])

                    if sample:
                        dma("sp", scv[0:48, :], sconv_d[:, :], [], ["xin0"])
                        for hf_ in range(2):
                            dma("sp", CS32[:, 8 * hf_:8 * hf_ + 8, 0:128], sC_d.rearrange("(b h) k v -> h k b v", h=4)[0][:, 8 * hf_:8 * hf_ + 8, :],
                                [], ["CS32_%d" % hf_])
                        for hf_ in range(2):
                            dma("sp", SSh[:, 8 * hf_:8 * hf_ + 8, :], sS_d.rearrange("(b h) k v -> h k b v", h=4)[0][:, 8 * hf_:8 * hf_ + 8, :],
                                [], ["SSh_%d" % hf_])
                        dma("sp", nS[:], sn_d.rearrange("g k -> k g"), [], ["nS"])
                    for c in range(8):
                        pb = c % 2
                        col = C_QK + c * 128
                        mm([(psum[pb][:, 0:NT], win[:, kc, col:col + 128], hT[:, kc, 0:NT], kc == 0, kc == 7) for kc in range(8)],
                           wk(C_QK) + [kHT], ["P%d" % pb])
                        rw = raw[c % 2]; rk = "raw%d" % (c % 2)
                        ca = cacc[c % 2]; ck = "cacc%d" % (c % 2)
                        if sample:
                            rv = rw[:, 0:112].rearrange("p (b j) -> p b j", j=7)
                            trp([(psum[3][:, 0:48], scv[0:48, c * 128:(c + 1) * 128])], cst[0:48, K_ID:K_ID + 48], ["xin0", "cst"], ["P3"])
                            cp("dve", rv[:, :, 0:3], psum[3][:, 0:48].rearrange("p (b j) -> p b j", j=3), ["P3"], [rk])
                            act(rv[:, :, 3:7], psum[pb][:, 0:NT].rearrange("p (b t) -> p b t", t=4), AF.Identity,
                                ["P%d" % pb, "bcol"], [rk], bias=bcol[:, c:c + 1])
                            src = lambda j: rv[:, :, j:j + 4]
                            cav = ca[:, 0:NT].rearrange("p (b t) -> p b t", t=4)
                        else:
                            cp("pool", rw[:, 0:3], hist[:, c, :], ["hist"], [rk])
                            act(rw[:, 3:3 + NT], psum[pb][:, 0:NT], AF.Identity, ["P%d" % pb, "bcol"], [rk], bias=bcol[:, c:c + 1])
                            src = lambda j: rw[:, j:j + NT]
                            cav = ca[:, 0:NT]
                        ts("dve", cav, src(0), cw[:, c, 0:1], cb[:, c:c + 1], ALU.mult, ALU.add, [rk, "cw", "cb"], [ck])
                        for j in (1, 2, 3):
                            stt(cav, src(j), cw[:, c, j:j + 1], cav, ALU.mult, ALU.add, [rk, "cw", ck], [ck])
                        act(qkT[:, c, 0:NT], ca[:, 0:NT], AF.Silu, [ck], [kQK])
                        if sample:
                            cp("dve", scvT[:, :].rearrange("p (b j) -> p b j", j=3), rv[:, :, 4:7], [rk], ["scvT"])
                            trp([(psum[3][0:48, 0:128], scvT[:, 0:48])], ident, ["scvT", "cst"], ["P3"])
                            cp("act", cso[0:48, c * 128:(c + 1) * 128], psum[3][0:48, 0:128], ["P3"], ["xres1"])
                        else:
                            cp("pool", hist[:, c, :], rw[:, NT:NT + 3], [rk], ["hist"])
                    if sample:
                        LATE.append(lambda: dma("sp", convs_d[:, :], cso[0:48, :], ["xres1"], ["o_convs"]))

                    rvec = cst[:, K_RS:K_RS + 64] if sample else cst[:, K_RP:K_RP + 512]
                    for h in range(4):
                        q_ = qsh[h % 2]; t_ = thh[h % 2]; l_ = lfh[h % 2]; b_ = bh[h % 2]; e_ = eh[h % 2]
                        qk_, tk_, lk_, bk_, ekk_ = ["%s%d" % (n, h % 2) for n in ("qsh", "thh", "lfh", "bh", "eh")]
                        colq = C_QB + h * 128; colf = C_FB + h * 128
                        mm([(psum[0][:, 0:NT], win[:, kc, colq:colq + 128], hT[:, kc, 0:NT], kc == 0, kc == 7) for kc in range(8)],
                           wk(C_QB) + [kHT], ["P0"])
                        act(q_[:, 0:NT], psum[0][:, 0:NT], AF.Silu, ["P0", "bcol"], [qk_], bias=bcol[:, 8 + h:9 + h])
                        mm([(psum[1][:, 0:NT], win[:, kc, colf:colf + 128], hT[:, kc, 0:NT], kc == 0, kc == 7) for kc in range(8)],
                           wk(C_QB) + [kHT], ["P1"])
                        act(t_[:, 0:NT], psum[1][:, 0:NT], AF.Tanh, ["P1", "bcolh"], [tk_], scale=0.5, bias=bcolh[:, 12 + h:13 + h])
                        act(l_[:, 0:NT], t_[:, 0:NT], AF.Ln, [tk_, "c0t", "c1t"], [lk_], scale=c1t[:, h:h + 1], bias=c0t[:, h:h + 1])
                        ts("pool", t_[:, 0:NT], t_[:, 0:NT], nc1t[:, h:h + 1], c1t[:, h:h + 1], ALU.mult, ALU.add, [tk_, "nc1t", "c1t"], [tk_])
                        P.op("dve", lambda e, b_=b_, l_=l_: e.tensor_tensor_scan(out=b_[:, 0:NT], data0=rvec[:, 0:NT], data1=l_[:, 0:NT],
                                                                                 initial=0.0, op0=ALU.mult, op1=ALU.add), [lk_, "cst"], [bk_])
                        if sample:
                            act(e_[:, 0:NT], b_[:, 0:NT], AF.Exp, [bk_], [ekk_])
                            tt("pool", qmid[:, h, 0:NT], q_[:, 0:NT], e_[:, 0:NT], ALU.mult, [qk_, ekk_], [kQM])
                            act(e_[:, 0:NT], b_[:, 0:NT], AF.Exp, [bk_], [ekk_], scale=-1.0)
                            tt("dve", kmid[:, h, 0:NT], t_[:, 0:NT], e_[:, 0:NT], ALU.mult, [tk_, ekk_], [kKM])
                            act(e2[:, h, 0:NSEQ], b_[:, 3:NT:4], AF.Exp, [bk_], [kE2])
                        else:
                            cp("dve", bm[:, h, 0:nsub], b_[:, 63:NT:128], [bk_], [kBM])
                            ts("dve", nbm[:, h, 0:nsub], b_[:, 63:NT:128], -1.0, None, ALU.mult, None, [bk_], [kNBM])
                            for j in range(nsub):
                                act(e_[:, j * TS:(j + 1) * TS], b_[:, j * TS:(j + 1) * TS], AF.Exp, [bk_, kNBM], [ekk_], bias=nbm[:, h, j:j + 1])
                            tt("pool", qmid[:, h, 0:NT], q_[:, 0:NT], e_[:, 0:NT], ALU.mult, [qk_, ekk_], [kQM])
                            for j in range(nsub):
                                act(e_[:, j * TS:(j + 1) * TS], b_[:, j * TS:(j + 1) * TS], AF.Exp, [bk_, kBM], [ekk_], scale=-1.0,
                                    bias=bm[:, h, j:j + 1])
                            tt("dve", kmid[:, h, 0:NT], t_[:, 0:NT], e_[:, 0:NT], ALU.mult, [tk_, ekk_], [kKM])
                            act(e1[:, h, 0:nsub], bm[:, h, 0:nsub], AF.Exp, [kBM], [kE1])
                            tt("dve", e2[:, h, 0:nsub], b_[:, 127:NT:128], bm[:, h, 0:nsub], ALU.subtract, [bk_, kBM], [kE2])
                            act(e2[:, h, 0:nsub], e2[:, h, 0:nsub], AF.Exp, [kE2], [kE2])

                if "Y" in stage:
                    if (not sample) and row0 == 0:
                        wload([(C_VA, 1032), (C_VB, 1024)], with_out=True)
                    for j in range(nsub):
                        js = slice(j * TS, (j + 1) * TS)
                        rows = slice(row0 + j * TS, row0 + (j + 1) * TS)
                        first = (not sample) and row0 == 0 and j == 0
                        xr = xres[j % 2]; xrk = "xres%d" % (j % 2)
                        dma("sp", xr[0:TS, :], x_src[rows, :], [], [xrk])

                        def tokmm(pb, col, n, wkey):
                            items = [(psum[pb][0:TS, 0:n], hT[:, kc, js], win[:, kc, col:col + n], kc == 0, False) for kc in range(8)]
                            items.append((psum[pb][0:TS, 0:n], ones2[0:2, 0:TS], bhl[0:2, bofs(col):bofs(col) + n], False, True))
                            mm(items, wkey + [kHT, "bhl", "ones2"], ["P%d" % pb])

                        wA = wk(C_VA); wB = wk(C_VB)
                        tokmm(0, C_VA, 512, wA)
                        cp("act", vaug[0:TS, :, 0:128], psum[0][0:TS, :].rearrange("p (h v) -> p h v", h=4), ["P0"], ["vaug"])
                        tokmm(1, C_OA, 512, wA)
                        act(go[0:TS, :], psum[1][0:TS, :], AF.Tanh, ["P1"], ["go"], scale=0.5)
                        stt(go[0:TS, :], go[0:TS, :], 1.0, gA_bc[0:TS, :], ALU.add, ALU.mult, ["go", "gA_bc"], ["go"])
                        tokmm(0, C_G, 8, wA)
                        cp("dve", gt[0:TS, :], psum[0][0:TS, 0:8], ["P0"], ["gt"])
                        tokmm(1, C_VB, 512, wB)
                        cp("act", vB[0:TS, :], psum[1][0:TS, :], ["P1"], ["vB"])
                        tokmm(0, C_GB, 512, wB)
                        act(gg[0:TS, :], psum[0][0:TS, :], AF.Silu, ["P0"], ["gg"])
                        tt("dve", gg[0:TS, :], gg[0:TS, :], gB_bc[0:TS, :], ALU.mult, ["gg", "gB_bc"], ["gg"])

                        fr = gt[0:TS, 4:8]; ig = gt[0:TS, 0:4]
                        act(g1[0:TS, :], fr, AF.Abs, ["gt"], ["g1"])
                        act(g2[0:TS, :], g1[0:TS, :], AF.Exp, ["g1"], ["g2"], scale=-1.0)
                        act(g2[0:TS, :], g2[0:TS, :], AF.Ln, ["g2"], ["g2"], bias=1.0)
                        ts("dve", g3[0:TS, :], fr, 0.0, None, ALU.min, None, ["gt"], ["g3"])
                        tt("dve", lfA[0:TS, :], g3[0:TS, :], g2[0:TS, :], ALU.subtract, ["g3", "g2"], ["lfA"])
                        cum = triS if sample else tri[0:TS, 0:TS]
                        mm([(psum[3][0:TS, 0:4], cum, lfA[0:TS, :], True, True)], ["lfA", "cst"], ["P3"])
                        act(eq[0:TS, :], psum[3][0:TS, 0:4], AF.Exp, ["P3"], ["eq"], bias=LNQ)
                        tt("dve", bet[0:TS, :], ig, psum[3][0:TS, 0:4], ALU.subtract, ["gt", "P3"], ["bet"])
                        act(ek[0:TS, :], bet[0:TS, :], AF.Exp, ["bet"], ["ek"])
                        if sample:
                            tt("dve", lfm[:], lfA[0:64, :].unsqueeze(1).broadcast_to([64, NSEQ, 4]),
                               BM.unsqueeze(2).broadcast_to([64, NSEQ, 4]), ALU.mult, ["lfA", "cst"], ["lfm"])
                            mm([(psum[3][:, 64:128], onesf[0:64, :], lfm[:].rearrange("s b h -> s (b h)"), True, True)],
                               ["lfm", "cst"], ["P3"])
                            act(eDG[:, 0:64], psum[3][:, 64:128], AF.Exp, ["P3"], ["eDG"])
                        else:
                            mm([(psum[3][:, 64:68], onesf[0:TS, :], lfA[0:TS, :], True, True)], ["lfA", "cst"], ["P3"])
                            act(eDG[:, 0:4], psum[3][:, 64:68], AF.Exp, ["P3"], ["eDG"])
                        trp([(psum[3][0:4, 128:128 + TS], bet[0:TS, :])], cst[0:TS, K_ID:K_ID + TS], ["bet", "cst"], ["P3"])
                        trp([(psum[3][0:4, 256:256 + TS], lfA[0:TS, :])], cst[0:TS, K_ID:K_ID + TS], ["lfA", "cst"], ["P3"])
                        if sample:
                            P.op("dve", lambda e: e.tensor_reduce(out=bmx[:, :], in_=psum[3][0:4, 128:192].rearrange("h (b t) -> h b t", t=4),
                                                                   axis=AX.X, op=ALU.max), ["P3"], ["bmx"])
                            P.op("dve", lambda e: e.tensor_reduce(out=dG4[:, :], in_=psum[3][0:4, 256:320].rearrange("h (b t) -> h b t", t=4),
                                                                   axis=AX.X, op=ALU.add), ["P3"], ["dG4"])
                            tt("dve", mfin[:], m0T[:], bmx[:], ALU.max, ["m0T", "bmx"], ["mfin"])
                            tt("dve", mfin[:], mfin[:], dG4[:], ALU.add, ["mfin", "dG4"], ["mfin"])
                            LATE.append(lambda: dma("sp", ms_d.rearrange("b h -> h b"), mfin[:], ["mfin"], ["o_ms"]))
                            tt("dve", mblk[:], mfin[:].unsqueeze(1).broadcast_to([4, 4, NSEQ]),
                               cst[0:4, K_ID:K_ID + 4].unsqueeze(2).broadcast_to([4, 4, NSEQ]), ALU.mult, ["mfin", "cst"], ["mblk"])
                            bcast_hb(emF, -1.0, "emF")
                            tt("dve", emF[:, 0:64], emF[:, 0:64], eDG[:, 0:64], ALU.mult, ["emF", "eDG"], ["emF"])
                        else:
                            P.op("dve", lambda e: e.tensor_reduce(out=bmx[:, 0:1], in_=psum[3][0:4, 128:128 + TS], axis=AX.X, op=ALU.max),
                                 ["P3"], ["bmx"])
                            P.op("dve", lambda e: e.tensor_reduce(out=dG4[:, 0:1], in_=psum[3][0:4, 256:256 + TS], axis=AX.X, op=ALU.add),
                                 ["P3"], ["dG4"])
                            tt("dve", mrun[:, 0:1], mrun[:, 0:1], bmx[:, 0:1], ALU.max, ["mrun", "bmx"], ["mrun"])
                            tt("dve", mrun[:, 0:1], mrun[:, 0:1], dG4[:, 0:1], ALU.add, ["mrun", "dG4"], ["mrun"])
                        for h in range(4):
                            ts("pool", vsc[0:TS, h, :], vaug[0:TS, h, :], ek[0:TS, h:h + 1], None, ALU.mult, None, ["vaug", "ek"], ["vsc"])

                        maskA = triS if sample else tri[0:TS, 0:TS]
                        pS = psum[3][:].rearrange("p (h t) -> p h t", h=4)
                        mm([(pS[0:TS, h, 0:TS], qkT[:, 4 + h, js], qkT[:, h, js], True, True) for h in range(4)], [kQK], ["P3"])
                        tt("dve", PT[0:TS, :, 0:TS], pS[0:TS, :, 0:TS], maskA.unsqueeze(1).broadcast_to([TS, 4, TS]), ALU.mult,
                           ["P3", "cst"], ["PT"])
                        pTk = psv(2, BF16)[:, 0:512].rearrange("p (h k) -> p h k", h=4)
                        trp([(pTk[0:TS, h, :], qkT[:, 4 + h, js]) for h in range(4)], ident_bf[:], [kQK, "ident_bf"], ["P2"])
                        cp("act", ktok[0:TS, :, :], pTk[0:TS, :, :], ["P2"], ["ktok"])
                        pH = [psum[4][:].rearrange("p (h x) -> p h x", h=2), psum[5][:].rearrange("p (h x) -> p h x", h=2)]
                        pU = [psum[6][:].rearrange("p (h x) -> p h x", h=2), psum[7][:].rearrange("p (h x) -> p h x", h=2)]
                        if sample:
                            Cs_v = Cs_d.rearrange("(b h) k v -> h k b v", h=4)
                            ns_v = ns_d.rearrange("(b h) k -> h k b", h=4)
                            sC_v = sC_d.rearrange("(b h) k v -> h k b v", h=4)
                            sn_v = sn_d.rearrange("(b h) k -> h k b", h=4)
                            def c_load(h_, hf_):
                                dma("sp", CS32[:, 8 * hf_:8 * hf_ + 8, 0:128], sC_v[h_][:, 8 * hf_:8 * hf_ + 8, :], [], ["CS32_%d" % hf_])
                            for h in range(4):
                                tt("dve", qm[:], qkT[:, h, 0:64].unsqueeze(1).broadcast_to([128, NSEQ, 64]),
                                   cst[:, K_BMQ:K_BMQ + 1024].rearrange("p (b t) -> p b t", b=NSEQ), ALU.mult, [kQK, "cst"], ["qm"])
                                tt("pool", vm[:], vsc[0:64, h, :].unsqueeze(1).broadcast_to([64, NSEQ, 130]),
                                   BM.unsqueeze(2).broadcast_to([64, NSEQ, 130]), ALU.mult, ["vsc", "cst"], ["vm"])
                                for hf in range(2):
                                    bs = slice(8 * hf, 8 * hf + 8)
                                    gs = slice(h + 32 * hf, 32 * hf + 32, 4)
                                    ck_ = "CS32_%d" % hf; cbk_ = "CSb_%d" % hf
                                    cp("pool", CS32[:, bs, 128], nS[:, gs], ["nS"], [ck_])
                                    tt("dve", CS32[:, bs, 0:129], CS32[:, bs, 0:129], emS[:, gs].unsqueeze(2).broadcast_to([128, 8, 129]),
                                       ALU.mult, [ck_, "emS"], [ck_])
                                    cp("act", CSb[:, bs, 0:129], CS32[:, bs, 0:129], [ck_], [cbk_])
                                    items = []
                                    if hf == 0:
                                        items.append((pH[h // 2][0:TS, h % 2, 0:129], PT[0:TS, h, 0:TS], vsc[0:TS, h, 0:129], True, False))
                                    for b in range(8 * hf, 8 * hf + 8):
                                        items.append((pH[h // 2][0:TS, h % 2, 0:129], qm[:, b, :], CSb[:, b, 0:129], False, b == NSEQ - 1))
                                    mm(items, ["PT", "vsc", "qm", cbk_], ["P%d" % (4 + h // 2)])
                                    for b0 in range(8 * hf, 8 * hf + 8, 2):
                                        pu = pU[(b0 // 2) % 2]; puk = "P%d" % (6 + (b0 // 2) % 2)
                                        mm([(pu[:, bb, 0:129], ktok[0:64, h, :], vm[:, b0 + bb, 0:129], True, True) for bb in range(2)],
                                           ["ktok", "vm"], [puk])
                                        tt("dve", CS32[:, b0:b0 + 2, 0:129], CS32[:, b0:b0 + 2, 0:129], pu[:, :, 0:129], ALU.add, [ck_, puk], [ck_])
                                    tt("dve", CS32[:, bs, 0:129], CS32[:, bs, 0:129], emF[:, gs].unsqueeze(2).broadcast_to([128, 8, 129]),
                                       ALU.mult, [ck_, "emF"], [ck_])
                                    cp("pool", nO[:, gs], CS32[:, bs, 128], [ck_], ["nO"])
                                    dma("sp", Cs_v[h][:, bs, :], CS32[:, bs, 0:128], [ck_], ["o_Cs"])
                                    if h < 3:
                                        c_load(h + 1, hf)
                            LATE.append(lambda: dma("sp", ns_d.rearrange("g k -> k g"), nO[:], ["nO"], ["o_ns"]))
                        else:
                            for h in range(4):
                                items = [(pH[h // 2][0:TS, h % 2, 0:129], PT[0:TS, h, 0:TS], vsc[0:TS, h, 0:129], True, first)]
                                if not first:
                                    items.append((pH[h // 2][0:TS, h % 2, 0:129], qkT[:, h, js], Chb[:, h, 0:129], False, True))
                                mm(items, ["PT", "vsc", kQK, "Chb"], ["P%d" % (4 + h // 2)])
                        for hp in range(2):
                            tt("dve", dd[0:TS, 2 * hp:2 * hp + 2], pH[hp][0:TS, :, 128], eq[0:TS, 2 * hp:2 * hp + 2], ALU.mult,
                               ["P%d" % (4 + hp), "eq"], ["dd"])
                        act(dd[0:TS, :], dd[0:TS, :], AF.Abs, ["dd"], ["dd"])
                        ts("dve", dd[0:TS, :], dd[0:TS, :], 1.0, None, ALU.max, None, ["dd"], ["dd"])
                        P.op("dve", lambda e: e.reciprocal(out=rr[0:TS, :], in_=dd[0:TS, :]), ["dd"], ["rr"])
                        tt("dve", rr[0:TS, :], rr[0:TS, :], eq[0:TS, :], ALU.mult, ["rr", "eq"], ["rr"])
                        for h in range(4):
                            act(junk2[0:TS, :], pH[h // 2][0:TS, h % 2, 0:128], AF.Square, ["P%d" % (4 + h // 2)], ["junk2", "ssq"],
                                accum=ssq[0:TS, h:h + 1])
                        tt("dve", t4[0:TS, :], rr[0:TS, :], rr[0:TS, :], ALU.mult, ["rr"], ["t4"])
                        tt("dve", t4[0:TS, :], t4[0:TS, :], ssq[0:TS, :], ALU.mult, ["t4", "ssq"], ["t4"])
                        ts("dve", t4[0:TS, :], t4[0:TS, :], 1.0 / 128, EPS, ALU.mult, ALU.add, ["t4"], ["t4"])
                        tt("pool", sc[0:TS, :], t4[0:TS, :], cst[0:TS, K_NH:K_NH + 1].broadcast_to([TS, 4]), ALU.pow, ["t4", "cst"], ["sc"])
                        tt("dve", sc[0:TS, :], sc[0:TS, :], rr[0:TS, :], ALU.mult, ["sc", "rr"], ["sc"])
                        for h in range(4):
                            stt(mix[0:TS, h * 128:(h + 1) * 128], pH[h // 2][0:TS, h % 2, 0:128], sc[0:TS, h:h + 1],
                                go[0:TS, h * 128:(h + 1) * 128], ALU.mult, ALU.mult, ["P%d" % (4 + h // 2), "sc", "go"], ["mix"])
                        if not sample:
                            for h in range(4):
                                mm([(pU[h // 2][:, h % 2, 0:129], ktok[0:TS, h, :], vsc[0:TS, h, 0:129], True, True)],
                                   ["ktok", "vsc"], ["P%d" % (6 + h // 2)])
                            for h in range(4):
                                tt("dve", Chat[:, h, 0:129], Chat[:, h, 0:129], pU[h // 2][:, h % 2, 0:129], ALU.add,
                                   ["Chat", "P%d" % (6 + h // 2)], ["Chat"])
                                ts("dve", Chat[:, h, 0:130], Chat[:, h, 0:130], eDG[:, h:h + 1], None, ALU.mult, None, ["Chat", "eDG"], ["Chat"])
                            cp("act", Chb[:], Chat[:], ["Chat"], ["Chb"])

                        pA = psum[3][:].rearrange("p (h t) -> p h t", h=4)
                        if sample:
                            mm([(pA[0:64, h, 0:64], kmid[:, h, 0:64], qmid[:, h, 0:64], True, True) for h in range(4)], [kKM, kQM], ["P3"])
                        else:
                            items = []
                            for h in range(4):
                                items.append((pA[0:64, h, 0:128], kmid[:, h, j * 128:j * 128 + 64], qmid[:, h, js], True, True))
                                items.append((pA[64:128, h, 64:128], kmid[:, h, j * 128 + 64:j * 128 + 128], qmid[:, h, j * 128 + 64:j * 128 + 128], True, True))
                            mm(items, [kKM, kQM], ["P3"])
                        tt("dve", PT[0:TS, :, 0:TS], pA[0:TS, :, 0:TS], maskA.unsqueeze(1).broadcast_to([TS, 4, TS]), ALU.mult,
                           ["P3", "cst"], ["PT"])
                        trp([(pTk[0:TS, h, :], kmid[:, h, js]) for h in range(4)], ident_bf[:], [kKM, "ident_bf"], ["P2"])
                        cp("act", ktok[0:TS, :, :], pTk[0:TS, :, :], ["P2"], ["ktok"])
                        pO = psum[4][:].rearrange("p (h v) -> p h v", h=4)
                        if sample:
                            Ss_v = Ss_d.rearrange("(b h) k v -> h k b v", h=4)
                            sS_v = sS_d.rearrange("(b h) k v -> h k b v", h=4)
                            def s_load(h_, hf_):
                                dma("sp", SSh[:, 8 * hf_:8 * hf_ + 8, :], sS_v[h_][:, 8 * hf_:8 * hf_ + 8, :], [], ["SSh_%d" % hf_])
                            for h in range(4):
                                tt("dve", qm[:], qmid[:, h, 0:64].unsqueeze(1).broadcast_to([128, NSEQ, 64]),
                                   cst[:, K_BMQ:K_BMQ + 1024].rearrange("p (b t) -> p b t", b=NSEQ), ALU.mult, [kQM, "cst"], ["qm"])
                                tt("pool", vm[:, :, 0:128], vB[0:64, h * 128:(h + 1) * 128].unsqueeze(1).broadcast_to([64, NSEQ, 128]),
                                   BM.unsqueeze(2).broadcast_to([64, NSEQ, 128]), ALU.mult, ["vB", "cst"], ["vm"])
                                for hf in range(2):
                                    bs = slice(8 * hf, 8 * hf + 8)
                                    sk_ = "SSh_%d" % hf; sbk_ = "SSb_%d" % hf
                                    cp("act", SSb[:, bs, :], SSh[:, bs, :], [sk_], [sbk_])
                                    items = []
                                    if hf == 0:
                                        items.append((pO[0:TS, h, :], PT[0:TS, h, 0:TS], vB[0:TS, h * 128:(h + 1) * 128], True, False))
                                    for b in range(8 * hf, 8 * hf + 8):
                                        items.append((pO[0:TS, h, :], qm[:, b, :], SSb[:, b, :], False, b == NSEQ - 1))
                                    mm(items, ["PT", "vB", "qm", sbk_], ["P4"])
                                    for b0 in range(8 * hf, 8 * hf + 8, 4):
                                        pbk = 6 + (b0 // 4) % 2
                                        pu = psum[pbk][:].rearrange("p (b v) -> p b v", b=4)
                                        mm([(pu[:, bb, :], ktok[0:64, h, :], vm[:, b0 + bb, 0:128], True, True) for bb in range(4)],
                                           ["ktok", "vm"], ["P%d" % pbk])
                                        tt("dve", SSh[:, b0:b0 + 4, :], SSh[:, b0:b0 + 4, :], pu[:, :, :], ALU.add, [sk_, "P%d" % pbk], [sk_])
                                    tt("dve", SSh[:, bs, :], SSh[:, bs, :], e2[:, h, 8 * hf:8 * hf + 8].unsqueeze(2).broadcast_to([128, 8, 128]), ALU.mult,
                                       [sk_, kE2], [sk_])
                                    dma("sp", Ss_v[h][:, bs, :], SSh[:, bs, :], [sk_], ["o_Ss"])
                                    if h < 3:
                                        s_load(h + 1, hf)
                        else:
                            if not first:
                                for h in range(4):
                                    act(Smb[:, h, :], S32[:, h, :], AF.Copy, ["S32", kE1], ["Smb"], scale=e1[:, h, j:j + 1])
                            for h in range(4):
                                items = [(pO[0:TS, h, :], PT[0:TS, h, 0:TS], vB[0:TS, h * 128:(h + 1) * 128], True, first)]
                                if not first:
                                    items.append((pO[0:TS, h, :], qmid[:, h, js], Smb[:, h, :], False, True))
                                mm(items, ["PT", "vB", kQM, "Smb"], ["P4"])
                        for h in range(4):
                            act(junk2[0:TS, :], pO[0:TS, h, :], AF.Square, ["P4"], ["junk2", "ssq"], accum=ssq[0:TS, h:h + 1])
                        ts("dve", t4[0:TS, :], ssq[0:TS, :], 1.0 / 128, EPS, ALU.mult, ALU.add, ["ssq"], ["t4"])
                        tt("pool", sc[0:TS, :], t4[0:TS, :], cst[0:TS, K_NH:K_NH + 1].broadcast_to([TS, 4]), ALU.pow, ["t4", "cst"], ["sc"])
                        for h in range(4):
                            stt(mix[0:TS, 512 + h * 128:512 + (h + 1) * 128], pO[0:TS, h, :], sc[0:TS, h:h + 1],
                                gg[0:TS, h * 128:(h + 1) * 128], ALU.mult, ALU.mult, ["P4", "sc", "gg"], ["mix"])
                        if not sample:
                            pUB = psum[6][:].rearrange("p (h v) -> p h v", h=4)
                            mm([(pUB[:, h, :], ktok[0:TS, h, :], vB[0:TS, h * 128:(h + 1) * 128], True, True) for h in range(4)],
                               ["ktok", "vB"], ["P6"])
                            for h in range(4):
                                stt(Stmp[:, h, :], S32[:, h, :], e1[:, h, j:j + 1], pUB[:, h, :], ALU.mult, ALU.add, ["S32", kE1, "P6"], ["Stmp"])
                                ts("dve", S32[:, h, :], Stmp[:, h, :], e2[:, h, j:j + 1], None, ALU.mult, None, ["Stmp", kE2], ["S32"])

                        pTm = psv(2, BF16).rearrange("p (c t) -> p c t", c=8)
                        trp([(pTm[:, c, 0:TS], mix[0:TS, c * 128:(c + 1) * 128]) for c in range(8)], ident_bf[0:TS, 0:TS],
                            ["mix", "ident_bf"], ["P2"])
                        cp("act", mixT[:, :, 0:TS], pTm[:, :, 0:TS], ["P2"], ["mixT"])
                        for dh in range(2):
                            mm([(psum[dh][0:TS, :], mixT[:, kc, 0:TS], wout[:, kc, dh * 512:(dh + 1) * 512], kc == 0, kc == 7) for kc in range(8)],
                               ["mixT"] + WOUTK, ["P%d" % dh])
                            tt("dve", xr[0:TS, dh * 512:(dh + 1) * 512], xr[0:TS, dh * 512:(dh + 1) * 512], psum[dh][0:TS, :], ALU.add,
                               [xrk, "P%d" % dh], [xrk])
                        x2rows = slice((T if sample else 0) + row0 + j * TS, (T if sample else 0) + row0 + (j + 1) * TS)
                        dma("sp", x2_d[x2rows, :], xr[0:TS, :], [xrk], ["x2d"])

            P.finish_phase(reorder=False, skip_queues=("pool",))
            wload([(C_QB, 1024)])
            NST = T // NTA
            for st_ in range(NST):
                phaseA_super(xp_d, st_ * NTA, NTA, 128, False, "XY")
            P.finish_phase(reorder=True)
            for j_ in range(3):
                dma("sp", convp_d[j_].rearrange("(c p) -> p c", p=128), hist[:, :, j_], ["hist"], ["o_convp"])
            dma("sp", Sp_d.rearrange("h k v -> k h v"), S32[:], ["S32"], ["o_Sp"])
            dma("sp", mp_d[:, :], mrun[:, 0:1], ["mrun"], ["o_mp"])
            tt("dve", mblk[:, :, 0:1], mrun[:, 0:1].unsqueeze(1).broadcast_to([4, 4, 1]),
               cst[0:4, K_ID:K_ID + 4].unsqueeze(2), ALU.mult, ["mrun", "cst"], ["mblk"])
            mm([(psum[3][:, 0:4], onesf[0:4, :], mblk[:, :, 0], True, True)], ["mblk", "cst"], ["P3"])
            act(emS[:, 0:4], psum[3][:, 0:4], AF.Exp, ["P3"], ["emS"], scale=-1.0)
            for h in range(4):
                ts("dve", Chat[:, h, 0:129], Chat[:, h, 0:129], emS[:, h:h + 1], None, ALU.mult, None, ["Chat", "emS"], ["Chat"])
            dma("sp", Cp_d.rearrange("h k v -> k h v"), Chat[:, :, 0:128], ["Chat"], ["o_Cp"])
            dma("sp", np_d.rearrange("h k -> k h"), Chat[:, :, 128], ["Chat"], ["o_np"])

            P.finish_phase(reorder=False)
            Ap.close()
            As = ExitStack()
            CS32 = sb(As, "CS32", [128, NSEQ, 130]); CSb = sb(As, "CSb", [128, NSEQ, 130], BF16)
            SSh = sb(As, "SSh", [128, NSEQ, 128]); SSb = sb(As, "SSb", [128, NSEQ, 128], BF16)
            qm = sb(As, "qm", [128, NSEQ, 64], BF16); vm = sb(As, "vm", [64, NSEQ, 130], BF16)
            scvT = sb(As, "scvT", [128, 48]); lfm = sb(As, "lfm", [64, NSEQ, 4])
            m0T = sb(As, "m0T", [4, NSEQ]); emF = sb(As, "emF", [128, 64]); mfin = sb(As, "mfin", [4, NSEQ])
            dma("sp", m0T[:], sm_d.rearrange("b h -> h b"), [], ["m0T"])
            tt("dve", mblk[:], m0T[:].unsqueeze(1).broadcast_to([4, 4, NSEQ]),
               cst[0:4, K_ID:K_ID + 4].unsqueeze(2).broadcast_to([4, 4, NSEQ]), ALU.mult, ["m0T", "cst"], ["mblk"])
            bcast_hb(emS, 1.0, "emS")
            phaseA_super(xs_d, 0, 64, 64, True)
            for f_ in LATE:
                f_()

            P.finish_phase(reorder=False)
            As.close()

        with ExitStack() as B:
            wg = sb(B, "wg", [128, 8, DFF], BF16); wu = sb(B, "wu", [128, 8, DFF], BF16); wd = sb(B, "wd", [128, NFC, D], BF16)
            FCB = 6
            NCB = (NFC + FCB - 1) // FCB
            for cb in range(NCB):
                c0_, c1_ = cb * FCB * 128, min(DFF, (cb + 1) * FCB * 128)
                for kc in range(8):
                    dma("pool", wg[:, kc, c0_:c1_], wg_d[kc * 128:(kc + 1) * 128, c0_:c1_], [], ["wg_%d_%d" % (cb, kc)], max_dma_last_dim=4096)
                for kc in range(8):
                    dma("pool", wu[:, kc, c0_:c1_], wu_d[kc * 128:(kc + 1) * 128, c0_:c1_], [], ["wu_%d_%d" % (cb, kc)], max_dma_last_dim=4096)
            for fc in range(NFC):
                dma("pool", wd[:, fc, :], wd_d[fc * 128:(fc + 1) * 128, :], [], ["wd_%d" % fc], max_dma_last_dim=4096)
            WGK = lambda fc: ["wg_%d_%d" % (fc // FCB, kc) for kc in range(8)]
            WUK = lambda fc: ["wu_%d_%d" % (fc // FCB, kc) for kc in range(8)]
            WDK = ["wd_%d" % fc for fc in range(NFC)]
            NTB = 512
            xa = [sb(B, "xa%d" % i, [128, D]) for i in range(2)]
            xrb = [sb(B, "xrb%d" % i, [128, D]) for i in range(2)]
            junkB = sb(B, "junkB", [128, D], BF16)
            xnB = [sb(B, "xnB%d" % i, [128, D], BF16) for i in range(2)]
            ssB = sb(B, "ssB", [128, 8]); rsB = sb(B, "rsB", [128, 8])
            h2T = sb(B, "h2T", [128, 8, NTB], BF16)
            ffT = sb(B, "ffT", [128, NFC, NTB], BF16)
            sgt = [sb(B, "sgt%d" % i, [128, NTB]) for i in range(2)]

            def phaseB_super(row0, NT, TS, y_dst, yrow0):
                nsub = NT // TS
                for j in range(nsub):
                    xi = xa[j % 2]; xb_ = xnB[j % 2]
                    rows = slice(row0 + j * TS, row0 + (j + 1) * TS)
                    dma("sp", xi[0:TS, :], x2_d[rows, :], ["x2d"], ["xa%d" % (j % 2)])
                    act(junkB[0:TS, :], xi[0:TS, :], AF.Square, ["xa%d" % (j % 2)], ["junkB", "ssB"], accum=ssB[0:TS, j:j + 1])
                    ts("dve", rsB[0:TS, j:j + 1], ssB[0:TS, j:j + 1], 1.0 / D, EPS, ALU.mult, ALU.add, ["ssB"], ["rsB"])
                    tt("pool", ssB[0:TS, j:j + 1], rsB[0:TS, j:j + 1], cst[0:TS, K_NH:K_NH + 1], ALU.pow, ["rsB", "cst"], ["ssB"])
                    act(xb_[0:TS, :], xi[0:TS, :], AF.Copy, ["ssB", "xa%d" % (j % 2)], ["xnB%d" % (j % 2)], scale=ssB[0:TS, j:j + 1])
                    pT = psv(6, BF16).rearrange("p (c t) -> p c t", c=8)
                    trp([(pT[:, c, 0:TS], xb_[0:TS, c * 128:(c + 1) * 128]) for c in range(8)], ident_bf[0:TS, 0:TS],
                        ["xnB%d" % (j % 2), "ident_bf"], ["P6"])
                    tt("dve", h2T[:, :, j * TS:(j + 1) * TS], pT[:, :, 0:TS],
                       gf8[:].unsqueeze(2).broadcast_to([128, 8, TS]), ALU.mult, ["P6", "gf8"], ["h2T"])
                for fc in range(NFC):
                    pg = 2 * (fc % 2); pu = pg + 1
                    mm([(psum[pg][:, 0:NT], wg[:, kc, fc * 128:(fc + 1) * 128], h2T[:, kc, 0:NT], kc == 0, kc == 7) for kc in range(8)],
                       WGK(fc) + ["h2T"], ["P%d" % pg])
                    mm([(psum[pu][:, 0:NT], wu[:, kc, fc * 128:(fc + 1) * 128], h2T[:, kc, 0:NT], kc == 0, kc == 7) for kc in range(8)],
                       WUK(fc) + ["h2T"], ["P%d" % pu])
                    s_ = sgt[fc % 2]; sk = "sgt%d" % (fc % 2)
                    act(s_[:, 0:NT], psum[pg][:, 0:NT], AF.Silu, ["P%d" % pg], [sk])
                    tt("dve", ffT[:, fc, 0:NT], s_[:, 0:NT], psum[pu][:, 0:NT], ALU.mult, [sk, "P%d" % pu], ["ffT"])
                for j in range(nsub):
                    xr = xrb[j % 2]; xrk = "xrb%d" % (j % 2)
                    rows = slice(row0 + j * TS, row0 + (j + 1) * TS)
                    dma("sp", xr[0:TS, :], x2_d[rows, :], ["x2d"], [xrk])
                    for dh in range(2):
                        pb = 4 + dh
                        mm([(psum[pb][0:TS, :], ffT[:, fc, j * TS:(j + 1) * TS], wd[:, fc, dh * 512:(dh + 1) * 512], fc == 0, fc == NFC - 1)
                            for fc in range(NFC)], ["ffT"] + WDK, ["P%d" % pb])
                        tt("dve", xr[0:TS, dh * 512:(dh + 1) * 512], xr[0:TS, dh * 512:(dh + 1) * 512], psum[pb][0:TS, :], ALU.add,
                           [xrk, "P%d" % pb], [xrk])
                    act(junkB[0:TS, :], xr[0:TS, :], AF.Square, [xrk], ["junkB", "ssB"], accum=ssB[0:TS, 4 + j:5 + j])
                    ts("dve", rsB[0:TS, 4 + j:5 + j], ssB[0:TS, 4 + j:5 + j], 1.0 / D, EPS, ALU.mult, ALU.add, ["ssB"], ["rsB"])
                    tt("pool", ssB[0:TS, 4 + j:5 + j], rsB[0:TS, 4 + j:5 + j], cst[0:TS, K_NH:K_NH + 1], ALU.pow, ["rsB", "cst"], ["ssB"])
                    stt(xr[0:TS, :], xr[0:TS, :], ssB[0:TS, 4 + j:5 + j], gfin_bc[0:TS, :], ALU.mult, ALU.mult, [xrk, "ssB", "gfin_bc"], [xrk])
                    orow = slice(yrow0 + j * TS, yrow0 + (j + 1) * TS)
                    dma("sp", y_dst[orow, :], xr[0:TS, :], [xrk], ["o_y"])

            for st_ in range(T // NTB):
                phaseB_super(st_ * NTB, NTB, 128, yp_d, st_ * NTB)
            phaseB_super(T, 64, 64, ys_d, 0)
            P.finish_phase()
    return nc


_NC_CACHE = {}


def kernel(**inputs):
    f = lambda a: np.ascontiguousarray(np.asarray(a, dtype=np.float32))
    if "nc" not in _NC_CACHE:
        _NC_CACHE["nc"] = build_program()
    nc = _NC_CACHE["nc"]
    cst = make_consts()
    shared = {
        "w_in": f(inputs["w_in"][0]), "b_in": f(inputs["b_in"][0]).reshape(1, PIN), "norm_mix": f(inputs["norm_mix"][0]),
        "conv_w": f(inputs["conv_w"][0]), "conv_b": f(inputs["conv_b"][0]), "mlstm_norm": f(inputs["mlstm_norm"][0]).reshape(1, 512),
        "lb_logits": f(inputs["hgrn_lb_logits"]), "hgrn_norm": f(inputs["hgrn_norm"][0]).reshape(1, 512),
        "w_out": f(inputs["w_out"][0]), "norm_ffn": f(inputs["norm_ffn"][0]), "w_gate": f(inputs["w_gate"][0]),
        "w_up": f(inputs["w_up"][0]), "w_down": f(inputs["w_down"][0]), "norm_final": f(inputs["norm_final"]).reshape(1, D),
        "cst": cst,
    }
    colp = lambda v: f(v).reshape(-1, 128).T
    b_in0 = f(inputs["b_in"][0])
    cwp = np.stack([colp(inputs["conv_w"][0][j]) for j in range(4)], axis=2).reshape(128, 32)
    lbp = np.stack([colp(inputs["hgrn_lb_logits"][l]) for l in range(2)], axis=1).reshape(128, 8)
    shared["pcol"] = np.ascontiguousarray(np.concatenate([
        colp(inputs["norm_mix"][0]), colp(inputs["norm_ffn"][0]), colp(b_in0[C_QK:C_QK + 1024]), colp(b_in0[C_QB:C_QB + 1024]),
        cwp, colp(inputs["conv_b"][0]), lbp], axis=1).astype(np.float32))
    in_maps = []
    for c in range(NCORES):
        sl = slice(c * NSEQ, (c + 1) * NSEQ)
        m = dict(shared)
        m["xp"] = f(inputs["x_prompt"][c])
        m["xs"] = f(inputs["x_sample"][sl]).reshape(NS, D)
        m["sconv"] = f(inputs["state_conv"][0, sl]).reshape(NSEQ * 3, D)
        m["sC"] = f(inputs["state_mlstm_C"][0, sl]).reshape(NSEQ * 4, 128, 128)
        m["sn"] = f(inputs["state_mlstm_n"][0, sl]).reshape(NSEQ * 4, 128)
        m["sm"] = f(inputs["state_mlstm_m"][0, sl]).reshape(NSEQ, 4)
        m["sS"] = f(inputs["state_hgrn_S"][0, sl]).reshape(NSEQ * 4, 128, 128)
        in_maps.append(m)
    res = run_bass_kernel_spmd(nc, in_maps, core_ids=list(range(NCORES)))
    R = res.results
    cat = lambda k, shp: np.stack([np.asarray(R[c][k], dtype=np.float32).reshape(shp) for c in range(NCORES)], axis=0)
    y_prompt = cat("y_prompt", (T, D))
    y_sample = cat("y_sample", (NSEQ, TSQ, D)).reshape(NCORES * NSEQ, TSQ, D)
    conv_p = cat("conv_p", (3, D))[None]
    C_p = cat("C_p", (4, 128, 128))[None]
    n_p = cat("n_p", (4, 128))[None]
    m_p = cat("m_p", (4,))[None]
    S_p = cat("S_p", (4, 128, 128))[None]
    conv_s = cat("conv_s", (NSEQ, 3, D)).reshape(NCORES * NSEQ, 3, D)[None]
    C_s = cat("C_s", (NSEQ, 4, 128, 128)).reshape(NCORES * NSEQ, 4, 128, 128)[None]
    n_s = cat("n_s", (NSEQ, 4, 128)).reshape(NCORES * NSEQ, 4, 128)[None]
    m_s = cat("m_s", (NSEQ, 4)).reshape(NCORES * NSEQ, 4)[None]
    S_s = cat("S_s", (NSEQ, 4, 128, 128)).reshape(NCORES * NSEQ, 4, 128, 128)[None]
    return (y_prompt, y_sample, conv_p, C_p, n_p, m_p, S_p, conv_s, C_s, n_s, m_s, S_s)
```

```python
import numpy as np
from contextlib import ExitStack
import concourse.bass as bass
import concourse.mybir as mybir
from concourse.bass_utils import run_bass_kernel_spmd

F32 = mybir.dt.float32
BF16 = mybir.dt.bfloat16
AF = mybir.ActivationFunctionType
ALU = mybir.AluOpType
AX = mybir.AxisListType

D = 1024
PIN = 4104
DFF = 2816
NFC = 22
T = 2048
NSEQ = 16
TSQ = 4
NS = NSEQ * TSQ
EPS = 1e-6
NCORES = 8
LNQ = float(np.log(128.0 ** -0.5))

C_QK, C_VA, C_OA, C_G, C_QB, C_FB, C_VB, C_GB = 0, 1024, 1536, 2048, 2056, 2568, 3080, 3592

K_ID, K_TRI, K_ONE, K_TRIS, K_BM, K_SEL, K_RP, K_RS, K_BMQ, K_NH, K_END = (
    0, 128, 256, 384, 512, 528, 530, 1042, 1106, 2130, 2131)


def make_consts():
    c = np.zeros((128, K_END), np.float32)
    c[:, K_ID:K_ID + 128] = np.eye(128)
    s = np.arange(128)
    c[:, K_TRI:K_TRI + 128] = (s[:, None] <= s[None, :])
    c[:, K_ONE:K_ONE + 128] = 1.0
    t64 = np.arange(64)
    c[:64, K_TRIS:K_TRIS + 64] = ((t64[:, None] // 4 == t64[None, :] // 4) & (t64[:, None] <= t64[None, :]))
    c[:64, K_BM:K_BM + 16] = (t64[:, None] // 4 == np.arange(16)[None, :])
    c[0, K_SEL] = 1.0
    c[1, K_SEL + 1] = 1.0
    r = np.ones(512, np.float32); r[::128] = 0.0
    c[:, K_RP:K_RP + 512] = r
    r = np.ones(64, np.float32); r[::4] = 0.0
    c[:, K_RS:K_RS + 64] = r
    bmq = (np.arange(16)[:, None] == (t64[None, :] // 4)).astype(np.float32).reshape(-1)
    c[:, K_BMQ:K_BMQ + 1024] = bmq
    c[:, K_NH] = -0.5
    return c


class Prog:
    ENG = ["pe", "act", "dve", "pool", "sp"]
    NDMA = {"sp": 64, "pool": 24}
    STRICT_SAME_ENGINE = False
    PRIO_INDEX_WEIGHT = 0.25

    def __init__(self, nc, stack):
        self.nc = nc
        self.sem = {e: stack.enter_context(nc.semaphore("s_" + e)) for e in self.ENG}
        self.dsem = {q: [stack.enter_context(nc.semaphore("d%s%d" % (q, i))) for i in range(n)] for q, n in self.NDMA.items()}
        self.cnt = {e: 0 for e in self.ENG}
        self.dcnt = {q: [0] * n for q, n in self.NDMA.items()}
        self.rr = {q: 0 for q in self.NDMA}
        self.waited = {e: {} for e in self.ENG}
        self.lastw = {}
        self.readers = {}
        self.ops = []
        self.dhist = {}
        self.reset_streams()

    def reset_streams(self):
        self.stream = {e: [] for e in self.ENG}

    def op(self, eng, fns, r=(), w=(), cost=0.3, tbl=None):
        if not isinstance(fns, (list, tuple)):
            fns = [fns]
        self.ops.append((eng, list(fns), tuple(r), tuple(w), False, cost, 0, tbl))

    def dma(self, q, fn, r=(), w=(), cost=3.0, ndesc=128):
        self.ops.append((q, [fn], tuple(r), tuple(w), True, cost, ndesc, None))

    def _schedule(self):
        import heapq
        ops = self.ops
        n = len(ops)
        succ = [[] for _ in range(n)]
        indeg = [0] * n
        lastw = {}
        readers = {}
        for i, (eng, fns, r, w, isd, cost, nd, tb) in enumerate(ops):
            deps = set()
            for k in r:
                if k in lastw:
                    deps.add(lastw[k])
            for k in w:
                if k in lastw:
                    deps.add(lastw[k])
                for x in readers.get(k, ()):
                    deps.add(x)
            deps.discard(i)
            for d in deps:
                succ[d].append(i)
            indeg[i] = len(deps)
            for k in r:
                readers.setdefault(k, []).append(i)
            for k in w:
                lastw[k] = i
                readers[k] = []
        rank = [0.0] * n
        for i in range(n - 1, -1, -1):
            m = 0.0
            for j in succ[i]:
                if rank[j] > m:
                    m = rank[j]
            rank[i] = m + ops[i][5]
        order_key = sorted(range(n), key=lambda i: (i * self.PRIO_INDEX_WEIGHT - rank[i], i))
        prio = [0] * n
        for pos, i in enumerate(order_key):
            prio[i] = pos
        inv = order_key
        ready = {e: [] for e in self.ENG}
        for i in range(n):
            if indeg[i] == 0:
                heapq.heappush(ready[ops[i][0]], prio[i])
        free = {e: 0.0 for e in self.ENG}
        events = []
        start = [0.0] * n
        t = 0.0
        done = 0
        dma_bw_free = 0.0
        cur_tbl = [None]
        while done < n:
            progressed = False
            for e in self.ENG:
                if free[e] <= t and ready[e]:
                    if e == "act" and cur_tbl[0] is not None:
                        cand = heapq.nsmallest(12, ready[e])
                        pick = None
                        for c in cand:
                            if ops[inv[c]][7] is None or ops[inv[c]][7] == cur_tbl[0]:
                                pick = c
                                break
                        if pick is None:
                            pick = cand[0]
                        ready[e].remove(pick)
                        heapq.heapify(ready[e])
                        i = inv[pick]
                    else:
                        i = inv[heapq.heappop(ready[e])]
                    eng, fns, r, w, isd, cost, nd, tb = ops[i]
                    if e == "act" and tb is not None:
                        if cur_tbl[0] is not None and cur_tbl[0] != tb:
                            cost = cost + 1.3
                        cur_tbl[0] = tb
                    start[i] = t
                    if isd:
                        occ = 0.15 if e == "sp" else 3.0
                        xfer = max(cost - 2.0, 0.0)
                        s0 = max(t, dma_bw_free)
                        dma_bw_free = s0 + xfer
                        fin = s0 + xfer + 2.0
                    else:
                        occ = cost
                        fin = t + cost
                    free[e] = t + occ
                    heapq.heappush(events, (fin, i))
                    progressed = True
            if progressed:
                continue
            cands = []
            if events:
                cands.append(events[0][0])
            for e in self.ENG:
                if ready[e] and free[e] > t:
                    cands.append(free[e])
            assert cands, "scheduler deadlock"
            t = max(t, min(cands))
            while events and events[0][0] <= t:
                fin, i = heapq.heappop(events)
                done += 1
                for j in succ[i]:
                    indeg[j] -= 1
                    if indeg[j] == 0:
                        heapq.heappush(ready[ops[j][0]], prio[j])
        order = sorted(range(n), key=lambda i: (start[i], i))
        self.est_time = t
        tot = {}
        for o in ops:
            tot[o[0]] = tot.get(o[0], 0.0) + (o[5] if not o[4] else 0.15)
        print('[sched] ops=%d est=%.1f us' % (n, t), {k: round(v) for k, v in tot.items()})
        return order

    def _semobj(self, k):
        return self.dsem[k[1]][k[2]] if isinstance(k, tuple) else self.sem[k]

    def _need(self, eng, ev):
        k, v, _ = ev
        if self.waited[eng].get(k, 0) >= v:
            return
        self.waited[eng][k] = v
        self.stream[eng].append(("wait", k, v))

    def _deps(self, eng, r, w, is_dma):
        for key in r:
            ev = self.lastw.get(key)
            if ev is not None:
                self._need(eng, ev)
        for key in w:
            ev = self.lastw.get(key)
            strict = is_dma or eng == "pool" or self.STRICT_SAME_ENGINE
            if ev is not None and (strict or ev[2] != eng):
                self._need(eng, ev)
            for rv in self.readers.get(key, []):
                if strict or rv[2] != eng:
                    self._need(eng, rv)

    def _commit(self, ev, r, w):
        for key in r:
            lst = self.readers.setdefault(key, [])
            if ev[2] != "dma":
                lst[:] = [x for x in lst if x[2] != ev[2]]
            lst.append(ev)
        for key in w:
            self.lastw[key] = ev
            self.readers[key] = []

    def _place(self, eng, fns, r, w, isd, nd=0):
        if not isd:
            self._deps(eng, r, w, False)
            self.cnt[eng] += 1
            ev = (eng, self.cnt[eng], eng)
            for f in fns[:-1]:
                self.stream[eng].append(("op", f, None, 0))
            self.stream[eng].append(("op", fns[-1], eng, 1))
            self._commit(ev, r, w)
        else:
            q = eng
            self._deps(q, r, w, True)
            hist_ = self.dhist.setdefault(q, [])
            while hist_ and (len(hist_) >= 4 or sum(x[1] for x in hist_) + nd > 3000):
                self._need(q, hist_.pop(0)[0])
            i = self.rr[q]
            self.rr[q] = (i + 1) % self.NDMA[q]
            k = ("d", q, i)
            if self.dcnt[q][i] > 0:
                self._need(q, (k, self.dcnt[q][i], "dma"))
            self.dcnt[q][i] += 16
            ev = (k, self.dcnt[q][i], "dma")
            self.stream[q].append(("op", fns[0], k, 16))
            hist_.append((ev, nd))
            self._commit(ev, r, w)

    def barrier(self, skip_queues=()):
        evs = [(e, self.cnt[e], e) for e in self.ENG if self.cnt[e] > 0]
        for q in self.NDMA:
            if q in skip_queues:
                continue
            evs += [(("d", q, i), self.dcnt[q][i], "dma") for i in range(self.NDMA[q]) if self.dcnt[q][i] > 0]
        for e in self.ENG:
            for ev in evs:
                if ev[2] != e:
                    self._need(e, ev)

    def finish_phase(self, reorder=True, skip_queues=()):
        order = self._schedule() if reorder else list(range(len(self.ops)))
        for i in order:
            eng, fns, r, w, isd, cost, nd, tb = self.ops[i]
            self._place(eng, fns, r, w, isd, nd)
        self.ops = []
        self.barrier(skip_queues)
        self.emit()

    def emit(self):
        nc = self.nc
        streams = self.stream
        P = self

        def run(eng_obj, name):
            for it in streams[name]:
                if it[0] == "wait":
                    eng_obj.wait_ge(P._semobj(it[1]), it[2])
                else:
                    ins = it[1](eng_obj)
                    if it[2] is not None:
                        ins.then_inc(P._semobj(it[2]), it[3])

        with nc.Block() as block:
            @block.tensor
            def _(e):
                run(e, "pe")

            @block.scalar
            def _(e):
                run(e, "act")

            @block.vector
            def _(e):
                run(e, "dve")

            @block.gpsimd
            def _(e):
                run(e, "pool")

            @block.sync
            def _(e):
                run(e, "sp")
        self.reset_streams()


def build_program():
    nc = bass.Bass("TRN2", target_bir_lowering=False)

    def din(name, shape):
        return nc.dram_tensor(name, list(shape), F32, kind="ExternalInput").ap()

    def dout(name, shape):
        return nc.dram_tensor(name, list(shape), F32, kind="ExternalOutput").ap()

    xp_d = din("xp", [T, D]); xs_d = din("xs", [NS, D]); sconv_d = din("sconv", [NSEQ * 3, D])
    sC_d = din("sC", [NSEQ * 4, 128, 128]); sn_d = din("sn", [NSEQ * 4, 128]); sm_d = din("sm", [NSEQ, 4])
    sS_d = din("sS", [NSEQ * 4, 128, 128])
    w_in_d = din("w_in", [D, PIN]); b_in_d = din("b_in", [1, PIN]); nmix_d = din("norm_mix", [D])
    cw_d = din("conv_w", [4, D]); cb_d = din("conv_b", [D]); gA_d = din("mlstm_norm", [1, 512])
    lbl_d = din("lb_logits", [2, 512]); gB_d = din("hgrn_norm", [1, 512]); w_out_d = din("w_out", [D, D])
    nffn_d = din("norm_ffn", [D]); wg_d = din("w_gate", [D, DFF]); wu_d = din("w_up", [D, DFF])
    wd_d = din("w_down", [DFF, D]); nfin_d = din("norm_final", [1, D]); cst_d = din("cst", [128, K_END]); pcol_d = din("pcol", [128, 80])

    yp_d = dout("y_prompt", [T, D]); ys_d = dout("y_sample", [NS, D])
    convp_d = dout("conv_p", [3, D]); Cp_d = dout("C_p", [4, 128, 128]); np_d = dout("n_p", [4, 128])
    mp_d = dout("m_p", [4, 1]); Sp_d = dout("S_p", [4, 128, 128])
    convs_d = dout("conv_s", [NSEQ * 3, D]); Cs_d = dout("C_s", [NSEQ * 4, 128, 128])
    ns_d = dout("n_s", [NSEQ * 4, 128]); ms_d = dout("m_s", [NSEQ, 4]); Ss_d = dout("S_s", [NSEQ * 4, 128, 128])
    x2_d = nc.dram_tensor("x2_scratch", [T + NS, D], F32, kind="Internal").ap()

    with ExitStack() as top:
        P = Prog(nc, top)
        nc_allow = top.enter_context(nc.allow_non_contiguous_dma(reason="small param layouts"))

        def sb(stack, name, shape, dt=F32):
            return stack.enter_context(nc.sbuf_tensor("sb_" + name, list(shape), dt))

        cst = sb(top, "cst", [128, K_END])
        ident_bf = sb(top, "ident_bf", [128, 128], BF16)
        ones2 = sb(top, "ones2", [2, 128], BF16)
        gfin_bc = sb(top, "gfin_bc", [128, D])
        pcol = sb(top, "pcol", [128, 80])
        gf8 = pcol[:, 8:16]
        psum = [top.enter_context(nc.psum_tensor("ps%d" % i, [128, 512], F32)) for i in range(8)]

        def psv(i, dt=F32):
            return psum[i][:] if dt == F32 else psum[i][:].bitcast(dt)

        ident = cst[:, K_ID:K_ID + 128]
        tri = cst[:, K_TRI:K_TRI + 128]
        onesf = cst[:, K_ONE:K_ONE + 128]
        triS = cst[0:64, K_TRIS:K_TRIS + 64]
        BM = cst[0:64, K_BM:K_BM + 16]

        def fsz(ap):
            n = 1
            for d in ap.shape[1:]:
                n *= int(d)
            return n

        def ecost(eng, ap, extra=0.0):
            n = fsz(ap)
            if eng == "act":
                return 0.22 + n * 0.00085 + extra
            if eng == "dve":
                return 0.12 + n * 0.00105 + extra
            return 0.35 + n * 0.0022 + extra

        def act(out, in_, func, r, w, scale=1.0, bias=0.0, accum=None):
            tb = "A" if func in (AF.Silu, AF.Tanh) else ("B" if func in (AF.Exp, AF.Ln) else None)
            if accum is None:
                P.op("act", lambda e: e.activation(out=out, in_=in_, func=func, scale=scale, bias=bias), r, w, ecost("act", out), tb)
            else:
                P.op("act", lambda e: e.activation(out=out, in_=in_, func=func, scale=scale, bias=bias,
                                                   accum_out=accum), r, w, ecost("act", out, 0.1), tb)

        def ts(eng, out, in0, s1, s2, op0, op1, r, w):
            if s2 is None:
                if eng == "pool":
                    assert op0 == ALU.mult
                    one_col = cst[0:int(out.shape[0]), K_ONE:K_ONE + 1]
                    P.op(eng, lambda e: e.tensor_scalar(out=out, in0=in0, scalar1=s1, scalar2=one_col, op0=ALU.mult, op1=ALU.mult),
                         tuple(r) + ("cst",), w, ecost(eng, out))
                else:
                    P.op(eng, lambda e: e.tensor_scalar(out=out, in0=in0, scalar1=s1, scalar2=None, op0=op0), r, w, ecost(eng, out))
            else:
                P.op(eng, lambda e: e.tensor_scalar(out=out, in0=in0, scalar1=s1, scalar2=s2, op0=op0, op1=op1), r, w, ecost(eng, out))

        def tt(eng, out, in0, in1, op, r, w):
            P.op(eng, lambda e: e.tensor_tensor(out=out, in0=in0, in1=in1, op=op), r, w, ecost(eng, out))

        def stt(out, in0, scalar, in1, op0, op1, r, w):
            P.op("dve", lambda e: e.scalar_tensor_tensor(out=out, in0=in0, scalar=scalar, in1=in1, op0=op0, op1=op1), r, w,
                 ecost("dve", out))

        def cp(eng, out, in_, r, w):
            if eng == "act":
                P.op("act", lambda e: e.copy(out=out, in_=in_), r, w, ecost("act", out))
            else:
                P.op(eng, lambda e: e.tensor_copy(out=out, in_=in_), r, w, ecost(eng, out))

        def mm(items, r, w):
            fns = []
            c = 0.05
            for (o, l, rh, st, sp_) in items:
                fns.append(lambda e, o=o, l=l, rh=rh, st=st, sp_=sp_: e.matmul(o, lhsT=l, rhs=rh, start=st, stop=sp_))
                c += (max(fsz(rh), 64) / 2400.0 + 0.012) * (4.0 if l.dtype == F32 else 1.0)
            P.op("pe", fns, r, w, c)

        def trp(items, idn, r, w):
            fns = []
            c = 0.05
            for (o, i) in items:
                fns.append(lambda e, o=o, i=i: e.transpose(o, i, idn))
                c += 0.07
            P.op("pe", fns, r, w, c)

        def ndesc_of(ap):
            tot = 1
            for d in ap.shape:
                tot *= int(d)
            last = ap.ap[-1]
            run = int(last[1]) if int(last[0]) == 1 else 1
            return max(1, tot // max(run, 1))

        def dma(q, out, in_, r, w, **kw):
            nbytes = out.shape[0] * fsz(out) * 4
            nd = max(ndesc_of(out), ndesc_of(in_))
            P.dma(q, lambda e: e.dma_start(out=out, in_=in_, **kw), r, w, 2.0 + nbytes / 150e3 + nd * 0.002, nd)

        def memset(eng, ap, val, w):
            P.op(eng, lambda e: e.memset(ap, val), (), w, ecost(eng, ap))

        dma("sp", cst[:], cst_d[:, :], [], ["cst"])
        cp("dve", ident_bf[:], ident, ["cst"], ["ident_bf"])
        memset("dve", ones2[:], 1.0, ["ones2"])
        for i in range(8):
            memset("dve", psum[i][:], 0.0, ["P%d" % i])
        dma("sp", gfin_bc[:], nfin_d.partition_broadcast(128), [], ["gfin_bc"])
        dma("sp", pcol[:], pcol_d[:, :], [], ["gf8", "g8", "bcol", "cw", "cb", "lbl"])

        with ExitStack() as A:
            win = sb(A, "win", [128, 8, PIN], BF16)
            wout = sb(A, "wout", [128, 8, D], BF16)
            def wload(blocks, with_out=False):
                for (c0, cn) in blocks:
                    for kc in range(8):
                        dma("pool", win[:, kc, c0:c0 + cn], w_in_d[kc * 128:(kc + 1) * 128, c0:c0 + cn], [], ["win%d_%d" % (c0, kc)],
                            max_dma_last_dim=4096)
                if with_out:
                    for kc in range(8):
                        dma("pool", wout[:, kc, :], w_out_d[kc * 128:(kc + 1) * 128, :], [], ["wout_%d" % kc], max_dma_last_dim=4096)
            wload([(C_QK, 1024)])

            def wk(c0):
                return ["win%d_%d" % (c0, kc) for kc in range(8)]
            WOUTK = ["wout_%d" % kc for kc in range(8)]

            g8 = pcol[:, 0:8]; bcol = pcol[:, 16:32]; bcolh = sb(A, "bcolh", [128, 16])
            cw = pcol[:, 32:64].rearrange("p (c j) -> p c j", j=4); cb = pcol[:, 64:72]
            lbl = pcol[:, 72:80].rearrange("p (l h) -> p l h", h=4); c0t = sb(A, "c0t", [128, 4]); c1t = sb(A, "c1t", [128, 4]); nc1t = sb(A, "nc1t", [128, 4])
            gA_bc = sb(A, "gA_bc", [128, 512]); gB_bc = sb(A, "gB_bc", [128, 512])
            bhl = sb(A, "bhl", [2, 2056], BF16)
            NTA = 256
            xin = [sb(A, "xin%d" % i, [128, D]) for i in range(2)]
            xres = [sb(A, "xres%d" % i, [128, D]) for i in range(2)]
            xn = [sb(A, "xn%d" % i, [128, D], BF16) for i in range(2)]
            ssA = sb(A, "ssA", [128, 8]); rsT = sb(A, "rsT", [128, 8])
            hT = sb(A, "hT", [128, 8, NTA], BF16)
            raw = [sb(A, "raw%d" % i, [128, NTA + 3]) for i in range(2)]
            cacc = [sb(A, "cacc%d" % i, [128, NTA]) for i in range(2)]
            qkT = sb(A, "qkT", [128, 8, NTA], BF16)
            qsh = [sb(A, "qsh%d" % i, [128, NTA]) for i in range(2)]; thh = [sb(A, "thh%d" % i, [128, NTA]) for i in range(2)]
            lfh = [sb(A, "lfh%d" % i, [128, NTA]) for i in range(2)]; bh = [sb(A, "bh%d" % i, [128, NTA]) for i in range(2)]
            eh = [sb(A, "eh%d" % i, [128, NTA]) for i in range(2)]
            qmid = sb(A, "qmid", [128, 4, NTA], BF16); kmid = sb(A, "kmid", [128, 4, NTA], BF16)
            e2 = sb(A, "e2", [128, 4, 16])
            vaug = sb(A, "vaug", [128, 4, 130], BF16); vsc = sb(A, "vsc", [128, 4, 130], BF16)
            go = sb(A, "go", [128, 512]); gt = sb(A, "gt", [128, 8])
            vB = sb(A, "vB", [128, 512], BF16); gg = sb(A, "gg", [128, 512])
            g1 = sb(A, "g1", [128, 4]); g2 = sb(A, "g2", [128, 4]); g3 = sb(A, "g3", [128, 4]); lfA = sb(A, "lfA", [128, 4])
            eq = sb(A, "eq", [128, 4]); ek = sb(A, "ek", [128, 4]); bet = sb(A, "bet", [128, 4])
            eDG = sb(A, "eDG", [128, 64])
            mrun = sb(A, "mrun", [4, NSEQ]); bmx = sb(A, "bmx", [4, NSEQ]); dG4 = sb(A, "dG4", [4, NSEQ])
            PT = sb(A, "PT", [128, 4, 128], BF16); ktok = sb(A, "ktok", [128, 4, 128], BF16)
            dd = sb(A, "dd", [128, 4]); rr = sb(A, "rr", [128, 4]); ssq = sb(A, "ssq", [128, 4]); sc = sb(A, "sc", [128, 4]); t4 = sb(A, "t4", [128, 4])
            junk2 = sb(A, "junk2", [128, 128], BF16)
            mix = sb(A, "mix", [128, D], BF16); mixT = sb(A, "mixT", [128, 8, 128], BF16)
            emS = sb(A, "emS", [128, 64]); mblk = sb(A, "mblk", [4, 4, NSEQ])
            scv = xin[0]; cso = xres[1]
            nS = sb(A, "nS", [128, 64]); nO = sb(A, "nO", [128, 64])
            LATE = []
            hist = sb(A, "hist", [128, 8, 3]); Chat = sb(A, "Chat", [128, 4, 130]); S32 = sb(A, "S32", [128, 4, 128])
            Ap = ExitStack()
            HT = [hT, sb(Ap, "hT_1", [128, 8, NTA], BF16)]; QKT = [qkT, sb(Ap, "qkT_1", [128, 8, NTA], BF16)]
            QMT = [qmid, sb(Ap, "qmid_1", [128, 4, NTA], BF16)]; KMT = [kmid, sb(Ap, "kmid_1", [128, 4, NTA], BF16)]
            BMt = [sb(Ap, "bm_%d" % i, [128, 4, 16]) for i in range(2)]; NBMt = [sb(Ap, "nbm_%d" % i, [128, 4, 16]) for i in range(2)]
            E1t = [sb(Ap, "e1_%d" % i, [128, 4, 16]) for i in range(2)]; E2t = [e2, sb(Ap, "e2_1", [128, 4, 16])]
            Chb = sb(Ap, "Chb", [128, 4, 130], BF16)
            Smb = sb(Ap, "Smb", [128, 4, 128], BF16); Stmp = sb(Ap, "Stmp", [128, 4, 128])

            dma("sp", gA_bc[:], gA_d.partition_broadcast(128), [], ["gA_bc"])
            dma("sp", gB_bc[:], gB_d.partition_broadcast(128), [], ["gB_bc"])
            ts("dve", bcolh[:], bcol[:], 0.5, None, ALU.mult, None, ["bcol"], ["bcolh"])
            ts("dve", gA_bc[:], gA_bc[:], 0.5, None, ALU.mult, None, ["gA_bc"], ["gA_bc"])
            tt("dve", c0t[:], lbl[:, 0, :], lbl[:, 1, :], ALU.subtract, ["lbl"], ["c0t"])
            act(c1t[:], c0t[:], AF.Tanh, ["c0t"], ["c1t"], scale=0.5)
            ts("dve", c0t[:], c1t[:], 0.25, 0.75, ALU.mult, ALU.add, ["c1t"], ["c0t"])
            ts("dve", nc1t[:], c1t[:], 0.25, -0.25, ALU.mult, ALU.add, ["c1t"], ["nc1t"])
            ts("dve", c1t[:], nc1t[:], -1.0, None, ALU.mult, None, ["nc1t", "c1t"], ["c1t"])
            def bofs(col):
                return col - 1024 if col < 2056 else col - 3080 + 1032
            for (c0_, cn_) in [(1024, 1024), (2048, 8), (3080, 1024)]:
                bfull = xin[0][0:2, 0:cn_]; bhif = xin[1][0:2, 0:cn_]; bhi = xn[0][0:2, 0:cn_]
                dma("sp", bfull, b_in_d[:, c0_:c0_ + cn_].partition_broadcast(2), [], ["xin0"])
                cp("dve", bhi, bfull, ["xin0"], ["xn0"])
                cp("dve", bhif, bhi, ["xn0"], ["xin1"])
                tt("dve", bfull, bfull, bhif, ALU.subtract, ["xin0", "xin1"], ["xin0"])
                ts("dve", bhif, bhif, cst[0:2, K_SEL:K_SEL + 1], None, ALU.mult, None, ["xin1", "cst"], ["xin1"])
                stt(bhl[:, bofs(c0_):bofs(c0_) + cn_], bfull, cst[0:2, K_SEL + 1:K_SEL + 2], bhif, ALU.mult, ALU.add,
                    ["xin0", "xin1", "cst"], ["bhl"])

            memset("dve", vaug[:], 1.0, ["vaug"])
            memset("dve", Chat[:], 0.0, ["Chat"])
            memset("dve", S32[:], 0.0, ["S32"])
            memset("dve", mrun[:], 0.0, ["mrun"])
            memset("pool", hist[:], 0.0, ["hist"])

            def bcast_hb(dst, func_scale, src_key):
                mm([(psum[3][:, 0:64], onesf[0:4, :], mblk[:].rearrange("k h b -> k (h b)"), True, True)], ["mblk", "cst"], ["P3"])
                act(dst[:, 0:64].rearrange("p (b h) -> p h b", h=4), psum[3][:, 0:64].rearrange("p (h b) -> p h b", h=4),
                    AF.Exp, ["P3"], [src_key], scale=func_scale)

            def phaseA_super(x_src, row0, NT, TS, sample, stage="XY"):
                nsub = NT // TS
                sp_ = 0 if sample else (row0 // NT) % 2
                hT, qkT, qmid, kmid = HT[sp_], QKT[sp_], QMT[sp_], KMT[sp_]
                e2 = E2t[sp_]
                if not sample:
                    bm, nbm, e1 = BMt[sp_], NBMt[sp_], E1t[sp_]
                kHT, kQK, kQM, kKM = "hT%d" % sp_, "qkT%d" % sp_, "qmid%d" % sp_, "kmid%d" % sp_
                kBM, kNBM, kE1, kE2 = "bm%d" % sp_, "nbm%d" % sp_, "e1%d" % sp_, "e2%d" % sp_
                if "X" in stage:
                    for j in range(nsub):
                        xi = xin[j % 2]; xb_ = xn[j % 2]
                        rows = slice(row0 + j * TS, row0 + (j + 1) * TS)
                        dma("sp", xi[0:TS, :], x_src[rows, :], [], ["xin%d" % (j % 2)])
                        act(xb_[0:TS, :], xi[0:TS, :], AF.Square, ["xin%d" % (j % 2)], ["xn%d" % (j % 2), "ssA"], accum=ssA[0:TS, j:j + 1])
                        ts("dve", rsT[0:TS, j:j + 1], ssA[0:TS, j:j + 1], 1.0 / D, EPS, ALU.mult, ALU.add, ["ssA"], ["rsT"])
                        tt("pool", ssA[0:TS, j:j + 1], rsT[0:TS, j:j + 1], cst[0:TS, K_NH:K_NH + 1], ALU.pow, ["rsT", "cst"], ["ssA"])
                        act(xb_[0:TS, :], xi[0:TS, :], AF.Copy, ["ssA", "xin%d" % (j % 2)], ["xn%d" % (j % 2)], scale=ssA[0:TS, j:j + 1])
                        pT = psv(2, BF16).rearrange("p (c t) -> p c t", c=8)
                        trp([(pT[:, c, 0:TS], xb_[0:TS, c * 128:(c + 1) * 128]) for c in range(8)], ident_bf[0:TS, 0:TS],
                            ["xn%d" % (j % 2), "ident_bf"], ["P2"])
                        tt("dve", hT[:, :, j * TS:(j + 1) * TS], pT[:, :, 0:TS],
                           g8[:].unsqueeze(2).broadcast_to([128, 8, TS]), ALU.mult, ["P2", "g8"], [kHT])

                    if sample:
                        dma("sp", scv[0:48, :], sconv_d[:, :], [], ["xin0"])
                        for hf_ in range(2):
                            dma("sp", CS32[:, 8 * hf_:8 * hf_ + 8, 0:128], sC_d.rearrange("(b h) k v -> h k b v", h=4)[0][:, 8 * hf_:8 * hf_ + 8, :],
                                [], ["CS32_%d" % hf_])
                        for hf_ in range(2):
                            dma("sp", SSh[:, 8 * hf_:8 * hf_ + 8, :], sS_d.rearrange("(b h) k v -> h k b v", h=4)[0][:, 8 * hf_:8 * hf_ + 8, :],
                                [], ["SSh_%d" % hf_])
                        dma("sp", nS[:], sn_d.rearrange("g k -> k g"), [], ["nS"])
                    for c in range(8):
                        pb = c % 2
                        col = C_QK + c * 128
                        mm([(psum[pb][:, 0:NT], win[:, kc, col:col + 128], hT[:, kc, 0:NT], kc == 0, kc == 7) for kc in range(8)],
                           wk(C_QK) + [kHT], ["P%d" % pb])
                        rw = raw[c % 2]; rk = "raw%d" % (c % 2)
                        ca = cacc[c % 2]; ck = "cacc%d" % (c % 2)
                        if sample:
                            rv = rw[:, 0:112].rearrange("p (b j) -> p b j", j=7)
                            trp([(psum[3][:, 0:48], scv[0:48, c * 128:(c + 1) * 128])], cst[0:48, K_ID:K_ID + 48], ["xin0", "cst"], ["P3"])
                            cp("dve", rv[:, :, 0:3], psum[3][:, 0:48].rearrange("p (b j) -> p b j", j=3), ["P3"], [rk])
                            act(rv[:, :, 3:7], psum[pb][:, 0:NT].rearrange("p (b t) -> p b t", t=4), AF.Identity,
                                ["P%d" % pb, "bcol"], [rk], bias=bcol[:, c:c + 1])
                            src = lambda j: rv[:, :, j:j + 4]
                            cav = ca[:, 0:NT].rearrange("p (b t) -> p b t", t=4)
                        else:
                            cp("pool", rw[:, 0:3], hist[:, c, :], ["hist"], [rk])
                            act(rw[:, 3:3 + NT], psum[pb][:, 0:NT], AF.Identity, ["P%d" % pb, "bcol"], [rk], bias=bcol[:, c:c + 1])
                            src = lambda j: rw[:, j:j + NT]
                            cav = ca[:, 0:NT]
                        ts("dve", cav, src(0), cw[:, c, 0:1], cb[:, c:c + 1], ALU.mult, ALU.add, [rk, "cw", "cb"], [ck])
                        for j in (1, 2, 3):
                            stt(cav, src(j), cw[:, c, j:j + 1], cav, ALU.mult, ALU.add, [rk, "cw", ck], [ck])
                        act(qkT[:, c, 0:NT], ca[:, 0:NT], AF.Silu, [ck], [kQK])
                        if sample:
                            cp("dve", scvT[:, :].rearrange("p (b j) -> p b j", j=3), rv[:, :, 4:7], [rk], ["scvT"])
                            trp([(psum[3][0:48, 0:128], scvT[:, 0:48])], ident, ["scvT", "cst"], ["P3"])
                            cp("act", cso[0:48, c * 128:(c + 1) * 128], psum[3][0:48, 0:128], ["P3"], ["xres1"])
                        else:
                            cp("pool", hist[:, c, :], rw[:, NT:NT + 3], [rk], ["hist"])
                    if sample:
                        LATE.append(lambda: dma("sp", convs_d[:, :], cso[0:48, :], ["xres1"], ["o_convs"]))

                    rvec = cst[:, K_RS:K_RS + 64] if sample else cst[:, K_RP:K_RP + 512]
                    for h in range(4):
                        q_ = qsh[h % 2]; t_ = thh[h % 2]; l_ = lfh[h % 2]; b_ = bh[h % 2]; e_ = eh[h % 2]
                        qk_, tk_, lk_, bk_, ekk_ = ["%s%d" % (n, h % 2) for n in ("qsh", "thh", "lfh", "bh", "eh")]
                        colq = C_QB + h * 128; colf = C_FB + h * 128
                        mm([(psum[0][:, 0:NT], win[:, kc, colq:colq + 128], hT[:, kc, 0:NT], kc == 0, kc == 7) for kc in range(8)],
                           wk(C_QB) + [kHT], ["P0"])
                        act(q_[:, 0:NT], psum[0][:, 0:NT], AF.Silu, ["P0", "bcol"], [qk_], bias=bcol[:, 8 + h:9 + h])
                        mm([(psum[1][:, 0:NT], win[:, kc, colf:colf + 128], hT[:, kc, 0:NT], kc == 0, kc == 7) for kc in range(8)],
                           wk(C_QB) + [kHT], ["P1"])
                        act(t_[:, 0:NT], psum[1][:, 0:NT], AF.Tanh, ["P1", "bcolh"], [tk_], scale=0.5, bias=bcolh[:, 12 + h:13 + h])
                        act(l_[:, 0:NT], t_[:, 0:NT], AF.Ln, [tk_, "c0t", "c1t"], [lk_], scale=c1t[:, h:h + 1], bias=c0t[:, h:h + 1])
                        ts("pool", t_[:, 0:NT], t_[:, 0:NT], nc1t[:, h:h + 1], c1t[:, h:h + 1], ALU.mult, ALU.add, [tk_, "nc1t", "c1t"], [tk_])
                        P.op("dve", lambda e, b_=b_, l_=l_: e.tensor_tensor_scan(out=b_[:, 0:NT], data0=rvec[:, 0:NT], data1=l_[:, 0:NT],
                                                                                 initial=0.0, op0=ALU.mult, op1=ALU.add), [lk_, "cst"], [bk_])
                        if sample:
                            act(e_[:, 0:NT], b_[:, 0:NT], AF.Exp, [bk_], [ekk_])
                            tt("pool", qmid[:, h, 0:NT], q_[:, 0:NT], e_[:, 0:NT], ALU.mult, [qk_, ekk_], [kQM])
                            act(e_[:, 0:NT], b_[:, 0:NT], AF.Exp, [bk_], [ekk_], scale=-1.0)
                            tt("dve", kmid[:, h, 0:NT], t_[:, 0:NT], e_[:, 0:NT], ALU.mult, [tk_, ekk_], [kKM])
                            act(e2[:, h, 0:NSEQ], b_[:, 3:NT:4], AF.Exp, [bk_], [kE2])
                        else:
                            cp("dve", bm[:, h, 0:nsub], b_[:, 63:NT:128], [bk_], [kBM])
                            ts("dve", nbm[:, h, 0:nsub], b_[:, 63:NT:128], -1.0, None, ALU.mult, None, [bk_], [kNBM])
                            for j in range(nsub):
                                act(e_[:, j * TS:(j + 1) * TS], b_[:, j * TS:(j + 1) * TS], AF.Exp, [bk_, kNBM], [ekk_], bias=nbm[:, h, j:j + 1])
                            tt("pool", qmid[:, h, 0:NT], q_[:, 0:NT], e_[:, 0:NT], ALU.mult, [qk_, ekk_], [kQM])
                            for j in range(nsub):
                                act(e_[:, j * TS:(j + 1) * TS], b_[:, j * TS:(j + 1) * TS], AF.Exp, [bk_, kBM], [ekk_], scale=-1.0,
                                    bias=bm[:, h, j:j + 1])
                            tt("dve", kmid[:, h, 0:NT], t_[:, 0:NT], e_[:, 0:NT], ALU.mult, [tk_, ekk_], [kKM])
                            act(e1[:, h, 0:nsub], bm[:, h, 0:nsub], AF.Exp, [kBM], [kE1])
                            tt("dve", e2[:, h, 0:nsub], b_[:, 127:NT:128], bm[:, h, 0:nsub], ALU.subtract, [bk_, kBM], [kE2])
                            act(e2[:, h, 0:nsub], e2[:, h, 0:nsub], AF.Exp, [kE2], [kE2])

                if "Y" in stage:
                    if (not sample) and row0 == 0:
                        wload([(C_VA, 1032), (C_VB, 1024)], with_out=True)
                    for j in range(nsub):
                        js = slice(j * TS, (j + 1) * TS)
                        rows = slice(row0 + j * TS, row0 + (j + 1) * TS)
                        first = (not sample) and row0 == 0 and j == 0
                        xr = xres[j % 2]; xrk = "xres%d" % (j % 2)
                        dma("sp", xr[0:TS, :], x_src[rows, :], [], [xrk])

                        def tokmm(pb, col, n, wkey):
                            items = [(psum[pb][0:TS, 0:n], hT[:, kc, js], win[:, kc, col:col + n], kc == 0, False) for kc in range(8)]
                            items.append((psum[pb][0:TS, 0:n], ones2[0:2, 0:TS], bhl[0:2, bofs(col):bofs(col) + n], False, True))
                            mm(items, wkey + [kHT, "bhl", "ones2"], ["P%d" % pb])

                        wA = wk(C_VA); wB = wk(C_VB)
                        tokmm(0, C_VA, 512, wA)
                        cp("act", vaug[0:TS, :, 0:128], psum[0][0:TS, :].rearrange("p (h v) -> p h v", h=4), ["P0"], ["vaug"])
                        tokmm(1, C_OA, 512, wA)
                        act(go[0:TS, :], psum[1][0:TS, :], AF.Tanh, ["P1"], ["go"], scale=0.5)
                        stt(go[0:TS, :], go[0:TS, :], 1.0, gA_bc[0:TS, :], ALU.add, ALU.mult, ["go", "gA_bc"], ["go"])
                        tokmm(0, C_G, 8, wA)
                        cp("dve", gt[0:TS, :], psum[0][0:TS, 0:8], ["P0"], ["gt"])
                        tokmm(1, C_VB, 512, wB)
                        cp("act", vB[0:TS, :], psum[1][0:TS, :], ["P1"], ["vB"])
                        tokmm(0, C_GB, 512, wB)
                        act(gg[0:TS, :], psum[0][0:TS, :], AF.Silu, ["P0"], ["gg"])
                        tt("dve", gg[0:TS, :], gg[0:TS, :], gB_bc[0:TS, :], ALU.mult, ["gg", "gB_bc"], ["gg"])

                        fr = gt[0:TS, 4:8]; ig = gt[0:TS, 0:4]
                        act(g1[0:TS, :], fr, AF.Abs, ["gt"], ["g1"])
                        act(g2[0:TS, :], g1[0:TS, :], AF.Exp, ["g1"], ["g2"], scale=-1.0)
                        act(g2[0:TS, :], g2[0:TS, :], AF.Ln, ["g2"], ["g2"], bias=1.0)
                        ts("dve", g3[0:TS, :], fr, 0.0, None, ALU.min, None, ["gt"], ["g3"])
                        tt("dve", lfA[0:TS, :], g3[0:TS, :], g2[0:TS, :], ALU.subtract, ["g3", "g2"], ["lfA"])
                        cum = triS if sample else tri[0:TS, 0:TS]
                        mm([(psum[3][0:TS, 0:4], cum, lfA[0:TS, :], True, True)], ["lfA", "cst"], ["P3"])
                        act(eq[0:TS, :], psum[3][0:TS, 0:4], AF.Exp, ["P3"], ["eq"], bias=LNQ)
                        tt("dve", bet[0:TS, :], ig, psum[3][0:TS, 0:4], ALU.subtract, ["gt", "P3"], ["bet"])
                        act(ek[0:TS, :], bet[0:TS, :], AF.Exp, ["bet"], ["ek"])
                        if sample:
                            tt("dve", lfm[:], lfA[0:64, :].unsqueeze(1).broadcast_to([64, NSEQ, 4]),
                               BM.unsqueeze(2).broadcast_to([64, NSEQ, 4]), ALU.mult, ["lfA", "cst"], ["lfm"])
                            mm([(psum[3][:, 64:128], onesf[0:64, :], lfm[:].rearrange("s b h -> s (b h)"), True, True)],
                               ["lfm", "cst"], ["P3"])
                            act(eDG[:, 0:64], psum[3][:, 64:128], AF.Exp, ["P3"], ["eDG"])
                        else:
                            mm([(psum[3][:, 64:68], onesf[0:TS, :], lfA[0:TS, :], True, True)], ["lfA", "cst"], ["P3"])
                            act(eDG[:, 0:4], psum[3][:, 64:68], AF.Exp, ["P3"], ["eDG"])
                        trp([(psum[3][0:4, 128:128 + TS], bet[0:TS, :])], cst[0:TS, K_ID:K_ID + TS], ["bet", "cst"], ["P3"])
                        trp([(psum[3][0:4, 256:256 + TS], lfA[0:TS, :])], cst[0:TS, K_ID:K_ID + TS], ["lfA", "cst"], ["P3"])
                        if sample:
                            P.op("dve", lambda e: e.tensor_reduce(out=bmx[:, :], in_=psum[3][0:4, 128:192].rearrange("h (b t) -> h b t", t=4),
                                                                   axis=AX.X, op=ALU.max), ["P3"], ["bmx"])
                            P.op("dve", lambda e: e.tensor_reduce(out=dG4[:, :], in_=psum[3][0:4, 256:320].rearrange("h (b t) -> h b t", t=4),
                                                                   axis=AX.X, op=ALU.add), ["P3"], ["dG4"])
                            tt("dve", mfin[:], m0T[:], bmx[:], ALU.max, ["m0T", "bmx"], ["mfin"])
                            tt("dve", mfin[:], mfin[:], dG4[:], ALU.add, ["mfin", "dG4"], ["mfin"])
                            LATE.append(lambda: dma("sp", ms_d.rearrange("b h -> h b"), mfin[:], ["mfin"], ["o_ms"]))
                            tt("dve", mblk[:], mfin[:].unsqueeze(1).broadcast_to([4, 4, NSEQ]),
                               cst[0:4, K_ID:K_ID + 4].unsqueeze(2).broadcast_to([4, 4, NSEQ]), ALU.mult, ["mfin", "cst"], ["mblk"])
                            bcast_hb(emF, -1.0, "emF")
                            tt("dve", emF[:, 0:64], emF[:, 0:64], eDG[:, 0:64], ALU.mult, ["emF", "eDG"], ["emF"])
                        else:
                            P.op("dve", lambda e: e.tensor_reduce(out=bmx[:, 0:1], in_=psum[3][0:4, 128:128 + TS], axis=AX.X, op=ALU.max),
                                 ["P3"], ["bmx"])
                            P.op("dve", lambda e: e.tensor_reduce(out=dG4[:, 0:1], in_=psum[3][0:4, 256:256 + TS], axis=AX.X, op=ALU.add),
                                 ["P3"], ["dG4"])
                            tt("dve", mrun[:, 0:1], mrun[:, 0:1], bmx[:, 0:1], ALU.max, ["mrun", "bmx"], ["mrun"])
                            tt("dve", mrun[:, 0:1], mrun[:, 0:1], dG4[:, 0:1], ALU.add, ["mrun", "dG4"], ["mrun"])
                        for h in range(4):
                            ts("pool", vsc[0:TS, h, :], vaug[0:TS, h, :], ek[0:TS, h:h + 1], None, ALU.mult, None, ["vaug", "ek"], ["vsc"])

                        maskA = triS if sample else tri[0:TS, 0:TS]
                        pS = psum[3][:].rearrange("p (h t) -> p h t", h=4)
                        mm([(pS[0:TS, h, 0:TS], qkT[:, 4 + h, js], qkT[:, h, js], True, True) for h in range(4)], [kQK], ["P3"])
                        tt("dve", PT[0:TS, :, 0:TS], pS[0:TS, :, 0:TS], maskA.unsqueeze(1).broadcast_to([TS, 4, TS]), ALU.mult,
                           ["P3", "cst"], ["PT"])
                        pTk = psv(2, BF16)[:, 0:512].rearrange("p (h k) -> p h k", h=4)
                        trp([(pTk[0:TS, h, :], qkT[:, 4 + h, js]) for h in range(4)], ident_bf[:], [kQK, "ident_bf"], ["P2"])
                        cp("act", ktok[0:TS, :, :], pTk[0:TS, :, :], ["P2"], ["ktok"])
                        pH = [psum[4][:].rearrange("p (h x) -> p h x", h=2), psum[5][:].rearrange("p (h x) -> p h x", h=2)]
                        pU = [psum[6][:].rearrange("p (h x) -> p h x", h=2), psum[7][:].rearrange("p (h x) -> p h x", h=2)]
                        if sample:
                            Cs_v = Cs_d.rearrange("(b h) k v -> h k b v", h=4)
                            ns_v = ns_d.rearrange("(b h) k -> h k b", h=4)
                            sC_v = sC_d.rearrange("(b h) k v -> h k b v", h=4)
                            sn_v = sn_d.rearrange("(b h) k -> h k b", h=4)
                            def c_load(h_, hf_):
                                dma("sp", CS32[:, 8 * hf_:8 * hf_ + 8, 0:128], sC_v[h_][:, 8 * hf_:8 * hf_ + 8, :], [], ["CS32_%d" % hf_])
                            for h in range(4):
                                tt("dve", qm[:], qkT[:, h, 0:64].unsqueeze(1).broadcast_to([128, NSEQ, 64]),
                                   cst[:, K_BMQ:K_BMQ + 1024].rearrange("p (b t) -> p b t", b=NSEQ), ALU.mult, [kQK, "cst"], ["qm"])
                                tt("pool", vm[:], vsc[0:64, h, :].unsqueeze(1).broadcast_to([64, NSEQ, 130]),
                                   BM.unsqueeze(2).broadcast_to([64, NSEQ, 130]), ALU.mult, ["vsc", "cst"], ["vm"])
                                for hf in range(2):
                                    bs = slice(8 * hf, 8 * hf + 8)
                                    gs = slice(h + 32 * hf, 32 * hf + 32, 4)
                                    ck_ = "CS32_%d" % hf; cbk_ = "CSb_%d" % hf
                                    cp("pool", CS32[:, bs, 128], nS[:, gs], ["nS"], [ck_])
                                    tt("dve", CS32[:, bs, 0:129], CS32[:, bs, 0:129], emS[:, gs].unsqueeze(2).broadcast_to([128, 8, 129]),
                                       ALU.mult, [ck_, "emS"], [ck_])
                                    cp("act", CSb[:, bs, 0:129], CS32[:, bs, 0:129], [ck_], [cbk_])
                                    items = []
                                    if hf == 0:
                                        items.append((pH[h // 2][0:TS, h % 2, 0:129], PT[0:TS, h, 0:TS], vsc[0:TS, h, 0:129], True, False))
                                    for b in range(8 * hf, 8 * hf + 8):
                                        items.append((pH[h // 2][0:TS, h % 2, 0:129], qm[:, b, :], CSb[:, b, 0:129], False, b == NSEQ - 1))
                                    mm(items, ["PT", "vsc", "qm", cbk_], ["P%d" % (4 + h // 2)])
                                    for b0 in range(8 * hf, 8 * hf + 8, 2):
                                        pu = pU[(b0 // 2) % 2]; puk = "P%d" % (6 + (b0 // 2) % 2)
                                        mm([(pu[:, bb, 0:129], ktok[0:64, h, :], vm[:, b0 + bb, 0:129], True, True) for bb in range(2)],
                                           ["ktok", "vm"], [puk])
                                        tt("dve", CS32[:, b0:b0 + 2, 0:129], CS32[:, b0:b0 + 2, 0:129], pu[:, :, 0:129], ALU.add, [ck_, puk], [ck_])
                                    tt("dve", CS32[:, bs, 0:129], CS32[:, bs, 0:129], emF[:, gs].unsqueeze(2).broadcast_to([128, 8, 129]),
                                       ALU.mult, [ck_, "emF"], [ck_])
                                    cp("pool", nO[:, gs], CS32[:, bs, 128], [ck_], ["nO"])
                                    dma("sp", Cs_v[h][:, bs, :], CS32[:, bs, 0:128], [ck_], ["o_Cs"])
                                    if h < 3:
                                        c_load(h + 1, hf)
                            LATE.append(lambda: dma("sp", ns_d.rearrange("g k -> k g"), nO[:], ["nO"], ["o_ns"]))
                        else:
                            for h in range(4):
                                items = [(pH[h // 2][0:TS, h % 2, 0:129], PT[0:TS, h, 0:TS], vsc[0:TS, h, 0:129], True, first)]
                                if not first:
                                    items.append((pH[h // 2][0:TS, h % 2, 0:129], qkT[:, h, js], Chb[:, h, 0:129], False, True))
                                mm(items, ["PT", "vsc", kQK, "Chb"], ["P%d" % (4 + h // 2)])
                        for hp in range(2):
                            tt("dve", dd[0:TS, 2 * hp:2 * hp + 2], pH[hp][0:TS, :, 128], eq[0:TS, 2 * hp:2 * hp + 2], ALU.mult,
                               ["P%d" % (4 + hp), "eq"], ["dd"])
                        act(dd[0:TS, :], dd[0:TS, :], AF.Abs, ["dd"], ["dd"])
                        ts("dve", dd[0:TS, :], dd[0:TS, :], 1.0, None, ALU.max, None, ["dd"], ["dd"])
                        P.op("dve", lambda e: e.reciprocal(out=rr[0:TS, :], in_=dd[0:TS, :]), ["dd"], ["rr"])
                        tt("dve", rr[0:TS, :], rr[0:TS, :], eq[0:TS, :], ALU.mult, ["rr", "eq"], ["rr"])
                        for h in range(4):
                            act(junk2[0:TS, :], pH[h // 2][0:TS, h % 2, 0:128], AF.Square, ["P%d" % (4 + h // 2)], ["junk2", "ssq"],
                                accum=ssq[0:TS, h:h + 1])
                        tt("dve", t4[0:TS, :], rr[0:TS, :], rr[0:TS, :], ALU.mult, ["rr"], ["t4"])
                        tt("dve", t4[0:TS, :], t4[0:TS, :], ssq[0:TS, :], ALU.mult, ["t4", "ssq"], ["t4"])
                        ts("dve", t4[0:TS, :], t4[0:TS, :], 1.0 / 128, EPS, ALU.mult, ALU.add, ["t4"], ["t4"])
                        tt("pool", sc[0:TS, :], t4[0:TS, :], cst[0:TS, K_NH:K_NH + 1].broadcast_to([TS, 4]), ALU.pow, ["t4", "cst"], ["sc"])
                        tt("dve", sc[0:TS, :], sc[0:TS, :], rr[0:TS, :], ALU.mult, ["sc", "rr"], ["sc"])
                        for h in range(4):
                            stt(mix[0:TS, h * 128:(h + 1) * 128], pH[h // 2][0:TS, h % 2, 0:128], sc[0:TS, h:h + 1],
                                go[0:TS, h * 128:(h + 1) * 128], ALU.mult, ALU.mult, ["P%d" % (4 + h // 2), "sc", "go"], ["mix"])
                        if not sample:
                            for h in range(4):
                                mm([(pU[h // 2][:, h % 2, 0:129], ktok[0:TS, h, :], vsc[0:TS, h, 0:129], True, True)],
                                   ["ktok", "vsc"], ["P%d" % (6 + h // 2)])
                            for h in range(4):
                                tt("dve", Chat[:, h, 0:129], Chat[:, h, 0:129], pU[h // 2][:, h % 2, 0:129], ALU.add,
                                   ["Chat", "P%d" % (6 + h // 2)], ["Chat"])
                                ts("dve", Chat[:, h, 0:130], Chat[:, h, 0:130], eDG[:, h:h + 1], None, ALU.mult, None, ["Chat", "eDG"], ["Chat"])
                            cp("act", Chb[:], Chat[:], ["Chat"], ["Chb"])

                        pA = psum[3][:].rearrange("p (h t) -> p h t", h=4)
                        if sample:
                            mm([(pA[0:64, h, 0:64], kmid[:, h, 0:64], qmid[:, h, 0:64], True, True) for h in range(4)], [kKM, kQM], ["P3"])
                        else:
                            items = []
                            for h in range(4):
                                items.append((pA[0:64, h, 0:128], kmid[:, h, j * 128:j * 128 + 64], qmid[:, h, js], True, True))
                                items.append((pA[64:128, h, 64:128], kmid[:, h, j * 128 + 64:j * 128 + 128], qmid[:, h, j * 128 + 64:j * 128 + 128], True, True))
                            mm(items, [kKM, kQM], ["P3"])
                        tt("dve", PT[0:TS, :, 0:TS], pA[0:TS, :, 0:TS], maskA.unsqueeze(1).broadcast_to([TS, 4, TS]), ALU.mult,
                           ["P3", "cst"], ["PT"])
                        trp([(pTk[0:TS, h, :], kmid[:, h, js]) for h in range(4)], ident_bf[:], [kKM, "ident_bf"], ["P2"])
                        cp("act", ktok[0:TS, :, :], pTk[0:TS, :, :], ["P2"], ["ktok"])
                        pO = psum[4][:].rearrange("p (h v) -> p h v", h=4)
                        if sample:
                            Ss_v = Ss_d.rearrange("(b h) k v -> h k b v", h=4)
                            sS_v = sS_d.rearrange("(b h) k v -> h k b v", h=4)
                            def s_load(h_, hf_):
                                dma("sp", SSh[:, 8 * hf_:8 * hf_ + 8, :], sS_v[h_][:, 8 * hf_:8 * hf_ + 8, :], [], ["SSh_%d" % hf_])
                            for h in range(4):
                                tt("dve", qm[:], qmid[:, h, 0:64].unsqueeze(1).broadcast_to([128, NSEQ, 64]),
                                   cst[:, K_BMQ:K_BMQ + 1024].rearrange("p (b t) -> p b t", b=NSEQ), ALU.mult, [kQM, "cst"], ["qm"])
                                tt("pool", vm[:, :, 0:128], vB[0:64, h * 128:(h + 1) * 128].unsqueeze(1).broadcast_to([64, NSEQ, 128]),
                                   BM.unsqueeze(2).broadcast_to([64, NSEQ, 128]), ALU.mult, ["vB", "cst"], ["vm"])
                                for hf in range(2):
                                    bs = slice(8 * hf, 8 * hf + 8)
                                    sk_ = "SSh_%d" % hf; sbk_ = "SSb_%d" % hf
                                    cp("act", SSb[:, bs, :], SSh[:, bs, :], [sk_], [sbk_])
                                    items = []
                                    if hf == 0:
                                        items.append((pO[0:TS, h, :], PT[0:TS, h, 0:TS], vB[0:TS, h * 128:(h + 1) * 128], True, False))
                                    for b in range(8 * hf, 8 * hf + 8):
                                        items.append((pO[0:TS, h, :], qm[:, b, :], SSb[:, b, :], False, b == NSEQ - 1))
                                    mm(items, ["PT", "vB", "qm", sbk_], ["P4"])
                                    for b0 in range(8 * hf, 8 * hf + 8, 4):
                                        pbk = 6 + (b0 // 4) % 2
                                        pu = psum[pbk][:].rearrange("p (b v) -> p b v", b=4)
                                        mm([(pu[:, bb, :], ktok[0:64, h, :], vm[:, b0 + bb, 0:128], True, True) for bb in range(4)],
                                           ["ktok", "vm"], ["P%d" % pbk])
                                        tt("dve", SSh[:, b0:b0 + 4, :], SSh[:, b0:b0 + 4, :], pu[:, :, :], ALU.add, [sk_, "P%d" % pbk], [sk_])
                                    tt("dve", SSh[:, bs, :], SSh[:, bs, :], e2[:, h, 8 * hf:8 * hf + 8].unsqueeze(2).broadcast_to([128, 8, 128]), ALU.mult,
                                       [sk_, kE2], [sk_])
                                    dma("sp", Ss_v[h][:, bs, :], SSh[:, bs, :], [sk_], ["o_Ss"])
                                    if h < 3:
                                        s_load(h + 1, hf)
                        else:
                            if not first:
                                for h in range(4):
                                    act(Smb[:, h, :], S32[:, h, :], AF.Copy, ["S32", kE1], ["Smb"], scale=e1[:, h, j:j + 1])
                            for h in range(4):
                                items = [(pO[0:TS, h, :], PT[0:TS, h, 0:TS], vB[0:TS, h * 128:(h + 1) * 128], True, first)]
                                if not first:
                                    items.append((pO[0:TS, h, :], qmid[:, h, js], Smb[:, h, :], False, True))
                                mm(items, ["PT", "vB", kQM, "Smb"], ["P4"])
                        for h in range(4):
                            act(junk2[0:TS, :], pO[0:TS, h, :], AF.Square, ["P4"], ["junk2", "ssq"], accum=ssq[0:TS, h:h + 1])
                        ts("dve", t4[0:TS, :], ssq[0:TS, :], 1.0 / 128, EPS, ALU.mult, ALU.add, ["ssq"], ["t4"])
                        tt("pool", sc[0:TS, :], t4[0:TS, :], cst[0:TS, K_NH:K_NH + 1].broadcast_to([TS, 4]), ALU.pow, ["t4", "cst"], ["sc"])
                        for h in range(4):
                            stt(mix[0:TS, 512 + h * 128:512 + (h + 1) * 128], pO[0:TS, h, :], sc[0:TS, h:h + 1],
                                gg[0:TS, h * 128:(h + 1) * 128], ALU.mult, ALU.mult, ["P4", "sc", "gg"], ["mix"])
                        if not sample:
                            pUB = psum[6][:].rearrange("p (h v) -> p h v", h=4)
                            mm([(pUB[:, h, :], ktok[0:TS, h, :], vB[0:TS, h * 128:(h + 1) * 128], True, True) for h in range(4)],
                               ["ktok", "vB"], ["P6"])
                            for h in range(4):
                                stt(Stmp[:, h, :], S32[:, h, :], e1[:, h, j:j + 1], pUB[:, h, :], ALU.mult, ALU.add, ["S32", kE1, "P6"], ["Stmp"])
                                ts("dve", S32[:, h, :], Stmp[:, h, :], e2[:, h, j:j + 1], None, ALU.mult, None, ["Stmp", kE2], ["S32"])

                        pTm = psv(2, BF16).rearrange("p (c t) -> p c t", c=8)
                        trp([(pTm[:, c, 0:TS], mix[0:TS, c * 128:(c + 1) * 128]) for c in range(8)], ident_bf[0:TS, 0:TS],
                            ["mix", "ident_bf"], ["P2"])
                        cp("act", mixT[:, :, 0:TS], pTm[:, :, 0:TS], ["P2"], ["mixT"])
                        for dh in range(2):
                            mm([(psum[dh][0:TS, :], mixT[:, kc, 0:TS], wout[:, kc, dh * 512:(dh + 1) * 512], kc == 0, kc == 7) for kc in range(8)],
                               ["mixT"] + WOUTK, ["P%d" % dh])
                            tt("dve", xr[0:TS, dh * 512:(dh + 1) * 512], xr[0:TS, dh * 512:(dh + 1) * 512], psum[dh][0:TS, :], ALU.add,
                               [xrk, "P%d" % dh], [xrk])
                        x2rows = slice((T if sample else 0) + row0 + j * TS, (T if sample else 0) + row0 + (j + 1) * TS)
                        dma("sp", x2_d[x2rows, :], xr[0:TS, :], [xrk], ["x2d"])

            P.finish_phase(reorder=False, skip_queues=("pool",))
            wload([(C_QB, 1024)])
            NST = T // NTA
            for st_ in range(NST):
                phaseA_super(xp_d, st_ * NTA, NTA, 128, False, "XY")
            P.finish_phase(reorder=True)
            Ap.close()
            As = ExitStack()
            CS32 = sb(As, "CS32", [128, NSEQ, 130]); CSb = sb(As, "CSb", [128, NSEQ, 130], BF16)
            SSh = sb(As, "SSh", [128, NSEQ, 128]); SSb = sb(As, "SSb", [128, NSEQ, 128], BF16)
            qm = sb(As, "qm", [128, NSEQ, 64], BF16); vm = sb(As, "vm", [64, NSEQ, 130], BF16)
            scvT = sb(As, "scvT", [128, 48]); lfm = sb(As, "lfm", [64, NSEQ, 4])
            m0T = sb(As, "m0T", [4, NSEQ]); emF = sb(As, "emF", [128, 64]); mfin = sb(As, "mfin", [4, NSEQ])
            for j_ in range(3):
                LATE.append(lambda j_=j_: dma("sp", convp_d[j_].rearrange("(c p) -> p c", p=128), hist[:, :, j_], ["hist"], ["o_convp"]))
            LATE.append(lambda: dma("sp", Sp_d.rearrange("h k v -> k h v"), S32[:], ["S32"], ["o_Sp"]))
            LATE.append(lambda: dma("sp", mp_d[:, :], mrun[:, 0:1], ["mrun"], ["o_mp"]))
            tt("dve", mblk[:, :, 0:1], mrun[:, 0:1].unsqueeze(1).broadcast_to([4, 4, 1]),
               cst[0:4, K_ID:K_ID + 4].unsqueeze(2), ALU.mult, ["mrun", "cst"], ["mblk"])
            mm([(psum[3][:, 0:4], onesf[0:4, :], mblk[:, :, 0], True, True)], ["mblk", "cst"], ["P3"])
            act(emS[:, 0:4], psum[3][:, 0:4], AF.Exp, ["P3"], ["emS"], scale=-1.0)
            for h in range(4):
                ts("dve", Chat[:, h, 0:129], Chat[:, h, 0:129], emS[:, h:h + 1], None, ALU.mult, None, ["Chat", "emS"], ["Chat"])
            LATE.append(lambda: dma("sp", Cp_d.rearrange("h k v -> k h v"), Chat[:, :, 0:128], ["Chat"], ["o_Cp"]))
            LATE.append(lambda: dma("sp", np_d.rearrange("h k -> k h"), Chat[:, :, 128], ["Chat"], ["o_np"]))
            dma("sp", m0T[:], sm_d.rearrange("b h -> h b"), [], ["m0T"])
            tt("dve", mblk[:], m0T[:].unsqueeze(1).broadcast_to([4, 4, NSEQ]),
               cst[0:4, K_ID:K_ID + 4].unsqueeze(2).broadcast_to([4, 4, NSEQ]), ALU.mult, ["m0T", "cst"], ["mblk"])
            bcast_hb(emS, 1.0, "emS")
            phaseA_super(xs_d, 0, 64, 64, True)
            for f_ in LATE:
                f_()

            P.finish_phase(reorder=False)
            As.close()

        with ExitStack() as B:
            wg = sb(B, "wg", [128, 8, DFF], BF16); wu = sb(B, "wu", [128, 8, DFF], BF16); wd = sb(B, "wd", [128, NFC, D], BF16)
            FCB = 6
            NCB = (NFC + FCB - 1) // FCB
            for cb in range(NCB):
                c0_, c1_ = cb * FCB * 128, min(DFF, (cb + 1) * FCB * 128)
                for kc in range(8):
                    dma("pool", wg[:, kc, c0_:c1_], wg_d[kc * 128:(kc + 1) * 128, c0_:c1_], [], ["wg_%d_%d" % (cb, kc)], max_dma_last_dim=4096)
                for kc in range(8):
                    dma("pool", wu[:, kc, c0_:c1_], wu_d[kc * 128:(kc + 1) * 128, c0_:c1_], [], ["wu_%d_%d" % (cb, kc)], max_dma_last_dim=4096)
            for fc in range(NFC):
                dma("pool", wd[:, fc, :], wd_d[fc * 128:(fc + 1) * 128, :], [], ["wd_%d" % fc], max_dma_last_dim=4096)
            WGK = lambda fc: ["wg_%d_%d" % (fc // FCB, kc) for kc in range(8)]
            WUK = lambda fc: ["wu_%d_%d" % (fc // FCB, kc) for kc in range(8)]
            WDK = ["wd_%d" % fc for fc in range(NFC)]
            NTB = 512
            xa = [sb(B, "xa%d" % i, [128, D]) for i in range(2)]
            xrb = [sb(B, "xrb%d" % i, [128, D]) for i in range(2)]
            junkB = sb(B, "junkB", [128, D], BF16)
            xnB = [sb(B, "xnB%d" % i, [128, D], BF16) for i in range(2)]
            ssB = sb(B, "ssB", [128, 8]); rsB = sb(B, "rsB", [128, 8])
            h2T = sb(B, "h2T", [128, 8, NTB], BF16)
            ffT = sb(B, "ffT", [128, NFC, NTB], BF16)
            sgt = [sb(B, "sgt%d" % i, [128, NTB]) for i in range(2)]

            def phaseB_super(row0, NT, TS, y_dst, yrow0):
                nsub = NT // TS
                for j in range(nsub):
                    xi = xa[j % 2]; xb_ = xnB[j % 2]
                    rows = slice(row0 + j * TS, row0 + (j + 1) * TS)
                    dma("sp", xi[0:TS, :], x2_d[rows, :], ["x2d"], ["xa%d" % (j % 2)])
                    act(junkB[0:TS, :], xi[0:TS, :], AF.Square, ["xa%d" % (j % 2)], ["junkB", "ssB"], accum=ssB[0:TS, j:j + 1])
                    ts("dve", rsB[0:TS, j:j + 1], ssB[0:TS, j:j + 1], 1.0 / D, EPS, ALU.mult, ALU.add, ["ssB"], ["rsB"])
                    tt("pool", ssB[0:TS, j:j + 1], rsB[0:TS, j:j + 1], cst[0:TS, K_NH:K_NH + 1], ALU.pow, ["rsB", "cst"], ["ssB"])
                    act(xb_[0:TS, :], xi[0:TS, :], AF.Copy, ["ssB", "xa%d" % (j % 2)], ["xnB%d" % (j % 2)], scale=ssB[0:TS, j:j + 1])
                    pT = psv(6, BF16).rearrange("p (c t) -> p c t", c=8)
                    trp([(pT[:, c, 0:TS], xb_[0:TS, c * 128:(c + 1) * 128]) for c in range(8)], ident_bf[0:TS, 0:TS],
                        ["xnB%d" % (j % 2), "ident_bf"], ["P6"])
                    tt("dve", h2T[:, :, j * TS:(j + 1) * TS], pT[:, :, 0:TS],
                       gf8[:].unsqueeze(2).broadcast_to([128, 8, TS]), ALU.mult, ["P6", "gf8"], ["h2T"])
                for fc in range(NFC):
                    pg = 2 * (fc % 2); pu = pg + 1
                    mm([(psum[pg][:, 0:NT], wg[:, kc, fc * 128:(fc + 1) * 128], h2T[:, kc, 0:NT], kc == 0, kc == 7) for kc in range(8)],
                       WGK(fc) + ["h2T"], ["P%d" % pg])
                    mm([(psum[pu][:, 0:NT], wu[:, kc, fc * 128:(fc + 1) * 128], h2T[:, kc, 0:NT], kc == 0, kc == 7) for kc in range(8)],
                       WUK(fc) + ["h2T"], ["P%d" % pu])
                    s_ = sgt[fc % 2]; sk = "sgt%d" % (fc % 2)
                    act(s_[:, 0:NT], psum[pg][:, 0:NT], AF.Silu, ["P%d" % pg], [sk])
                    tt("dve", ffT[:, fc, 0:NT], s_[:, 0:NT], psum[pu][:, 0:NT], ALU.mult, [sk, "P%d" % pu], ["ffT"])
                for j in range(nsub):
                    xr = xrb[j % 2]; xrk = "xrb%d" % (j % 2)
                    rows = slice(row0 + j * TS, row0 + (j + 1) * TS)
                    dma("sp", xr[0:TS, :], x2_d[rows, :], ["x2d"], [xrk])
                    for dh in range(2):
                        pb = 4 + dh
                        mm([(psum[pb][0:TS, :], ffT[:, fc, j * TS:(j + 1) * TS], wd[:, fc, dh * 512:(dh + 1) * 512], fc == 0, fc == NFC - 1)
                            for fc in range(NFC)], ["ffT"] + WDK, ["P%d" % pb])
                        tt("dve", xr[0:TS, dh * 512:(dh + 1) * 512], xr[0:TS, dh * 512:(dh + 1) * 512], psum[pb][0:TS, :], ALU.add,
                           [xrk, "P%d" % pb], [xrk])
                    act(junkB[0:TS, :], xr[0:TS, :], AF.Square, [xrk], ["junkB", "ssB"], accum=ssB[0:TS, 4 + j:5 + j])
                    ts("dve", rsB[0:TS, 4 + j:5 + j], ssB[0:TS, 4 + j:5 + j], 1.0 / D, EPS, ALU.mult, ALU.add, ["ssB"], ["rsB"])
                    tt("pool", ssB[0:TS, 4 + j:5 + j], rsB[0:TS, 4 + j:5 + j], cst[0:TS, K_NH:K_NH + 1], ALU.pow, ["rsB", "cst"], ["ssB"])
                    stt(xr[0:TS, :], xr[0:TS, :], ssB[0:TS, 4 + j:5 + j], gfin_bc[0:TS, :], ALU.mult, ALU.mult, [xrk, "ssB", "gfin_bc"], [xrk])
                    orow = slice(yrow0 + j * TS, yrow0 + (j + 1) * TS)
                    dma("sp", y_dst[orow, :], xr[0:TS, :], [xrk], ["o_y"])

            for st_ in range(T // NTB):
                phaseB_super(st_ * NTB, NTB, 128, yp_d, st_ * NTB)
            phaseB_super(T, 64, 64, ys_d, 0)
            P.finish_phase()
    return nc


_NC_CACHE = {}


def kernel(**inputs):
    f = lambda a: np.ascontiguousarray(np.asarray(a, dtype=np.float32))
    if "nc" not in _NC_CACHE:
        _NC_CACHE["nc"] = build_program()
    nc = _NC_CACHE["nc"]
    cst = make_consts()
    shared = {
        "w_in": f(inputs["w_in"][0]), "b_in": f(inputs["b_in"][0]).reshape(1, PIN), "norm_mix": f(inputs["norm_mix"][0]),
        "conv_w": f(inputs["conv_w"][0]), "conv_b": f(inputs["conv_b"][0]), "mlstm_norm": f(inputs["mlstm_norm"][0]).reshape(1, 512),
        "lb_logits": f(inputs["hgrn_lb_logits"]), "hgrn_norm": f(inputs["hgrn_norm"][0]).reshape(1, 512),
        "w_out": f(inputs["w_out"][0]), "norm_ffn": f(inputs["norm_ffn"][0]), "w_gate": f(inputs["w_gate"][0]),
        "w_up": f(inputs["w_up"][0]), "w_down": f(inputs["w_down"][0]), "norm_final": f(inputs["norm_final"]).reshape(1, D),
        "cst": cst,
    }
    colp = lambda v: f(v).reshape(-1, 128).T
    b_in0 = f(inputs["b_in"][0])
    cwp = np.stack([colp(inputs["conv_w"][0][j]) for j in range(4)], axis=2).reshape(128, 32)
    lbp = np.stack([colp(inputs["hgrn_lb_logits"][l]) for l in range(2)], axis=1).reshape(128, 8)
    shared["pcol"] = np.ascontiguousarray(np.concatenate([
        colp(inputs["norm_mix"][0]), colp(inputs["norm_ffn"][0]), colp(b_in0[C_QK:C_QK + 1024]), colp(b_in0[C_QB:C_QB + 1024]),
        cwp, colp(inputs["conv_b"][0]), lbp], axis=1).astype(np.float32))
    in_maps = []
    for c in range(NCORES):
        sl = slice(c * NSEQ, (c + 1) * NSEQ)
        m = dict(shared)
        m["xp"] = f(inputs["x_prompt"][c])
        m["xs"] = f(inputs["x_sample"][sl]).reshape(NS, D)
        m["sconv"] = f(inputs["state_conv"][0, sl]).reshape(NSEQ * 3, D)
        m["sC"] = f(inputs["state_mlstm_C"][0, sl]).reshape(NSEQ * 4, 128, 128)
        m["sn"] = f(inputs["state_mlstm_n"][0, sl]).reshape(NSEQ * 4, 128)
        m["sm"] = f(inputs["state_mlstm_m"][0, sl]).reshape(NSEQ, 4)
        m["sS"] = f(inputs["state_hgrn_S"][0, sl]).reshape(NSEQ * 4, 128, 128)
        in_maps.append(m)
    res = run_bass_kernel_spmd(nc, in_maps, core_ids=list(range(NCORES)))
    R = res.results
    cat = lambda k, shp: np.stack([np.asarray(R[c][k], dtype=np.float32).reshape(shp) for c in range(NCORES)], axis=0)
    y_prompt = cat("y_prompt", (T, D))
    y_sample = cat("y_sample", (NSEQ, TSQ, D)).reshape(NCORES * NSEQ, TSQ, D)
    conv_p = cat("conv_p", (3, D))[None]
    C_p = cat("C_p", (4, 128, 128))[None]
    n_p = cat("n_p", (4, 128))[None]
    m_p = cat("m_p", (4,))[None]
    S_p = cat("S_p", (4, 128, 128))[None]
    conv_s = cat("conv_s", (NSEQ, 3, D)).reshape(NCORES * NSEQ, 3, D)[None]
    C_s = cat("C_s", (NSEQ, 4, 128, 128)).reshape(NCORES * NSEQ, 4, 128, 128)[None]
    n_s = cat("n_s", (NSEQ, 4, 128)).reshape(NCORES * NSEQ, 4, 128)[None]
    m_s = cat("m_s", (NSEQ, 4)).reshape(NCORES * NSEQ, 4)[None]
    S_s = cat("S_s", (NSEQ, 4, 128, 128)).reshape(NCORES * NSEQ, 4, 128, 128)[None]
    return (y_prompt, y_sample, conv_p, C_p, n_p, m_p, S_p, conv_s, C_s, n_s, m_s, S_s)
```

```python
import numpy as np
from contextlib import ExitStack
import concourse.bass as bass
import concourse.mybir as mybir
from concourse.bass_utils import run_bass_kernel_spmd

F32 = mybir.dt.float32
BF16 = mybir.dt.bfloat16
AF = mybir.ActivationFunctionType
ALU = mybir.AluOpType
AX = mybir.AxisListType

D = 1024
PIN = 4104
DFF = 2816
NFC = 22
T = 2048
NSEQ = 16
TSQ = 4
NS = NSEQ * TSQ
EPS = 1e-6
NCORES = 8
LNQ = float(np.log(128.0 ** -0.5))

C_QK, C_VA, C_OA, C_G, C_QB, C_FB, C_VB, C_GB = 0, 1024, 1536, 2048, 2056, 2568, 3080, 3592

K_ID, K_TRI, K_ONE, K_TRIS, K_BM, K_SEL, K_RP, K_RS, K_BMQ, K_NH, K_END = (
    0, 128, 256, 384, 512, 528, 530, 1042, 1106, 2130, 2131)


def make_consts():
    c = np.zeros((128, K_END), np.float32)
    c[:, K_ID:K_ID + 128] = np.eye(128)
    s = np.arange(128)
    c[:, K_TRI:K_TRI + 128] = (s[:, None] <= s[None, :])
    c[:, K_ONE:K_ONE + 128] = 1.0
    t64 = np.arange(64)
    c[:64, K_TRIS:K_TRIS + 64] = ((t64[:, None] // 4 == t64[None, :] // 4) & (t64[:, None] <= t64[None, :]))
    c[:64, K_BM:K_BM + 16] = (t64[:, None] // 4 == np.arange(16)[None, :])
    c[0, K_SEL] = 1.0
    c[1, K_SEL + 1] = 1.0
    r = np.ones(512, np.float32); r[::128] = 0.0
    c[:, K_RP:K_RP + 512] = r
    r = np.ones(64, np.float32); r[::4] = 0.0
    c[:, K_RS:K_RS + 64] = r
    bmq = (np.arange(16)[:, None] == (t64[None, :] // 4)).astype(np.float32).reshape(-1)
    c[:, K_BMQ:K_BMQ + 1024] = bmq
    c[:, K_NH] = -0.5
    return c


class Prog:
    ENG = ["pe", "act", "dve", "pool", "sp"]
    NDMA = {"sp": 64, "pool": 24}
    STRICT_SAME_ENGINE = False
    PRIO_INDEX_WEIGHT = 0.25

    def __init__(self, nc, stack):
        self.nc = nc
        self.sem = {e: stack.enter_context(nc.semaphore("s_" + e)) for e in self.ENG}
        self.dsem = {q: [stack.enter_context(nc.semaphore("d%s%d" % (q, i))) for i in range(n)] for q, n in self.NDMA.items()}
        self.cnt = {e: 0 for e in self.ENG}
        self.dcnt = {q: [0] * n for q, n in self.NDMA.items()}
        self.rr = {q: 0 for q in self.NDMA}
        self.waited = {e: {} for e in self.ENG}
        self.lastw = {}
        self.readers = {}
        self.ops = []
        self.dhist = {}
        self.reset_streams()

    def reset_streams(self):
        self.stream = {e: [] for e in self.ENG}

    def op(self, eng, fns, r=(), w=(), cost=0.3, tbl=None):
        if not isinstance(fns, (list, tuple)):
            fns = [fns]
        self.ops.append((eng, list(fns), tuple(r), tuple(w), False, cost, 0, tbl))

    def dma(self, q, fn, r=(), w=(), cost=3.0, ndesc=128):
        self.ops.append((q, [fn], tuple(r), tuple(w), True, cost, ndesc, None))

    def _schedule(self):
        import heapq
        ops = self.ops
        n = len(ops)
        succ = [[] for _ in range(n)]
        indeg = [0] * n
        lastw = {}
        readers = {}
        for i, (eng, fns, r, w, isd, cost, nd, tb) in enumerate(ops):
            deps = set()
            for k in r:
                if k in lastw:
                    deps.add(lastw[k])
            for k in w:
                if k in lastw:
                    deps.add(lastw[k])
                for x in readers.get(k, ()):
                    deps.add(x)
            deps.discard(i)
            for d in deps:
                succ[d].append(i)
            indeg[i] = len(deps)
            for k in r:
                readers.setdefault(k, []).append(i)
            for k in w:
                lastw[k] = i
                readers[k] = []
        rank = [0.0] * n
        for i in range(n - 1, -1, -1):
            m = 0.0
            for j in succ[i]:
                if rank[j] > m:
                    m = rank[j]
            rank[i] = m + ops[i][5]
        order_key = sorted(range(n), key=lambda i: (i * self.PRIO_INDEX_WEIGHT - rank[i], i))
        prio = [0] * n
        for pos, i in enumerate(order_key):
            prio[i] = pos
        inv = order_key
        ready = {e: [] for e in self.ENG}
        for i in range(n):
            if indeg[i] == 0:
                heapq.heappush(ready[ops[i][0]], prio[i])
        free = {e: 0.0 for e in self.ENG}
        events = []
        start = [0.0] * n
        t = 0.0
        done = 0
        dma_bw_free = 0.0
        cur_tbl = [None]
        while done < n:
            progressed = False
            for e in self.ENG:
                if free[e] <= t and ready[e]:
                    if e == "act" and cur_tbl[0] is not None:
                        cand = heapq.nsmallest(12, ready[e])
                        pick = None
                        for c in cand:
                            if ops[inv[c]][7] is None or ops[inv[c]][7] == cur_tbl[0]:
                                pick = c
                                break
                        if pick is None:
                            pick = cand[0]
                        ready[e].remove(pick)
                        heapq.heapify(ready[e])
                        i = inv[pick]
                    else:
                        i = inv[heapq.heappop(ready[e])]
                    eng, fns, r, w, isd, cost, nd, tb = ops[i]
                    if e == "act" and tb is not None:
                        if cur_tbl[0] is not None and cur_tbl[0] != tb:
                            cost = cost + 1.3
                        cur_tbl[0] = tb
                    start[i] = t
                    if isd:
                        occ = 0.15 if e == "sp" else 3.0
                        xfer = max(cost - 2.0, 0.0)
                        s0 = max(t, dma_bw_free)
                        dma_bw_free = s0 + xfer
                        fin = s0 + xfer + 2.0
                    else:
                        occ = cost
                        fin = t + cost
                    free[e] = t + occ
                    heapq.heappush(events, (fin, i))
                    progressed = True
            if progressed:
                continue
            cands = []
            if events:
                cands.append(events[0][0])
            for e in self.ENG:
                if ready[e] and free[e] > t:
                    cands.append(free[e])
            assert cands, "scheduler deadlock"
            t = max(t, min(cands))
            while events and events[0][0] <= t:
                fin, i = heapq.heappop(events)
                done += 1
                for j in succ[i]:
                    indeg[j] -= 1
                    if indeg[j] == 0:
                        heapq.heappush(ready[ops[j][0]], prio[j])
        order = sorted(range(n), key=lambda i: (start[i], i))
        self.est_time = t
        tot = {}
        for o in ops:
            tot[o[0]] = tot.get(o[0], 0.0) + (o[5] if not o[4] else 0.15)
        print('[sched] ops=%d est=%.1f us' % (n, t), {k: round(v) for k, v in tot.items()})
        return order

    def _semobj(self, k):
        return self.dsem[k[1]][k[2]] if isinstance(k, tuple) else self.sem[k]

    def _need(self, eng, ev):
        k, v, _ = ev
        if self.waited[eng].get(k, 0) >= v:
            return
        self.waited[eng][k] = v
        self.stream[eng].append(("wait", k, v))

    def _deps(self, eng, r, w, is_dma):
        for key in r:
            ev = self.lastw.get(key)
            if ev is not None:
                self._need(eng, ev)
        for key in w:
            ev = self.lastw.get(key)
            strict = is_dma or eng == "pool" or self.STRICT_SAME_ENGINE
            if ev is not None and (strict or ev[2] != eng):
                self._need(eng, ev)
            for rv in self.readers.get(key, []):
                if strict or rv[2] != eng:
                    self._need(eng, rv)

    def _commit(self, ev, r, w):
        for key in r:
            lst = self.readers.setdefault(key, [])
            if ev[2] != "dma":
                lst[:] = [x for x in lst if x[2] != ev[2]]
            lst.append(ev)
        for key in w:
            self.lastw[key] = ev
            self.readers[key] = []

    def _place(self, eng, fns, r, w, isd, nd=0):
        if not isd:
            self._deps(eng, r, w, False)
            self.cnt[eng] += 1
            ev = (eng, self.cnt[eng], eng)
            for f in fns[:-1]:
                self.stream[eng].append(("op", f, None, 0))
            self.stream[eng].append(("op", fns[-1], eng, 1))
            self._commit(ev, r, w)
        else:
            q = eng
            self._deps(q, r, w, True)
            hist_ = self.dhist.setdefault(q, [])
            while hist_ and (len(hist_) >= 4 or sum(x[1] for x in hist_) + nd > 3000):
                self._need(q, hist_.pop(0)[0])
            i = self.rr[q]
            self.rr[q] = (i + 1) % self.NDMA[q]
            k = ("d", q, i)
            if self.dcnt[q][i] > 0:
                self._need(q, (k, self.dcnt[q][i], "dma"))
            self.dcnt[q][i] += 16
            ev = (k, self.dcnt[q][i], "dma")
            self.stream[q].append(("op", fns[0], k, 16))
            hist_.append((ev, nd))
            self._commit(ev, r, w)

    def barrier(self, skip_queues=()):
        evs = [(e, self.cnt[e], e) for e in self.ENG if self.cnt[e] > 0]
        for q in self.NDMA:
            if q in skip_queues:
                continue
            evs += [(("d", q, i), self.dcnt[q][i], "dma") for i in range(self.NDMA[q]) if self.dcnt[q][i] > 0]
        for e in self.ENG:
            for ev in evs:
                if ev[2] != e:
                    self._need(e, ev)

    def finish_phase(self, reorder=True, skip_queues=()):
        order = self._schedule() if reorder else list(range(len(self.ops)))
        for i in order:
            eng, fns, r, w, isd, cost, nd, tb = self.ops[i]
            self._place(eng, fns, r, w, isd, nd)
        self.ops = []
        self.barrier(skip_queues)
        self.emit()

    def emit(self):
        nc = self.nc
        streams = self.stream
        P = self

        def run(eng_obj, name):
            for it in streams[name]:
                if it[0] == "wait":
                    eng_obj.wait_ge(P._semobj(it[1]), it[2])
                else:
                    ins = it[1](eng_obj)
                    if it[2] is not None:
                        ins.then_inc(P._semobj(it[2]), it[3])

        with nc.Block() as block:
            @block.tensor
            def _(e):
                run(e, "pe")

            @block.scalar
            def _(e):
                run(e, "act")

            @block.vector
            def _(e):
                run(e, "dve")

            @block.gpsimd
            def _(e):
                run(e, "pool")

            @block.sync
            def _(e):
                run(e, "sp")
        self.reset_streams()


def build_program():
    nc = bass.Bass("TRN2", target_bir_lowering=False)

    def din(name, shape):
        return nc.dram_tensor(name, list(shape), F32, kind="ExternalInput").ap()

    def dout(name, shape):
        return nc.dram_tensor(name, list(shape), F32, kind="ExternalOutput").ap()

    xp_d = din("xp", [T, D]); xs_d = din("xs", [NS, D]); sconv_d = din("sconv", [NSEQ * 3, D])
    sC_d = din("sC", [NSEQ * 4, 128, 128]); sn_d = din("sn", [NSEQ * 4, 128]); sm_d = din("sm", [NSEQ, 4])
    sS_d = din("sS", [NSEQ * 4, 128, 128])
    w_in_d = din("w_in", [D, PIN]); b_in_d = din("b_in", [1, PIN]); nmix_d = din("norm_mix", [D])
    cw_d = din("conv_w", [4, D]); cb_d = din("conv_b", [D]); gA_d = din("mlstm_norm", [1, 512])
    lbl_d = din("lb_logits", [2, 512]); gB_d = din("hgrn_norm", [1, 512]); w_out_d = din("w_out", [D, D])
    nffn_d = din("norm_ffn", [D]); wg_d = din("w_gate", [D, DFF]); wu_d = din("w_up", [D, DFF])
    wd_d = din("w_down", [DFF, D]); nfin_d = din("norm_final", [1, D]); cst_d = din("cst", [128, K_END]); pcol_d = din("pcol", [128, 80])

    yp_d = dout("y_prompt", [T, D]); ys_d = dout("y_sample", [NS, D])
    convp_d = dout("conv_p", [3, D]); Cp_d = dout("C_p", [4, 128, 128]); np_d = dout("n_p", [4, 128])
    mp_d = dout("m_p", [4, 1]); Sp_d = dout("S_p", [4, 128, 128])
    convs_d = dout("conv_s", [NSEQ * 3, D]); Cs_d = dout("C_s", [NSEQ * 4, 128, 128])
    ns_d = dout("n_s", [NSEQ * 4, 128]); ms_d = dout("m_s", [NSEQ, 4]); Ss_d = dout("S_s", [NSEQ * 4, 128, 128])
    x2_d = nc.dram_tensor("x2_scratch", [T + NS, D], F32, kind="Internal").ap()

    with ExitStack() as top:
        P = Prog(nc, top)
        nc_allow = top.enter_context(nc.allow_non_contiguous_dma(reason="small param layouts"))

        def sb(stack, name, shape, dt=F32):
            return stack.enter_context(nc.sbuf_tensor("sb_" + name, list(shape), dt))

        cst = sb(top, "cst", [128, K_END])
        ident_bf = sb(top, "ident_bf", [128, 128], BF16)
        ones2 = sb(top, "ones2", [2, 128], BF16)
        gfin_bc = sb(top, "gfin_bc", [128, D])
        pcol = sb(top, "pcol", [128, 80])
        gf8 = pcol[:, 8:16]
        psum = [top.enter_context(nc.psum_tensor("ps%d" % i, [128, 512], F32)) for i in range(8)]

        def psv(i, dt=F32):
            return psum[i][:] if dt == F32 else psum[i][:].bitcast(dt)

        ident = cst[:, K_ID:K_ID + 128]
        tri = cst[:, K_TRI:K_TRI + 128]
        onesf = cst[:, K_ONE:K_ONE + 128]
        triS = cst[0:64, K_TRIS:K_TRIS + 64]
        BM = cst[0:64, K_BM:K_BM + 16]

        def fsz(ap):
            n = 1
            for d in ap.shape[1:]:
                n *= int(d)
            return n

        def ecost(eng, ap, extra=0.0):
            n = fsz(ap)
            if eng == "act":
                return 0.22 + n * 0.00085 + extra
            if eng == "dve":
                return 0.12 + n * 0.00105 + extra
            return 0.35 + n * 0.0022 + extra

        def act(out, in_, func, r, w, scale=1.0, bias=0.0, accum=None):
            tb = "A" if func in (AF.Silu, AF.Tanh) else ("B" if func in (AF.Exp, AF.Ln) else None)
            if accum is None:
                P.op("act", lambda e: e.activation(out=out, in_=in_, func=func, scale=scale, bias=bias), r, w, ecost("act", out), tb)
            else:
                P.op("act", lambda e: e.activation(out=out, in_=in_, func=func, scale=scale, bias=bias,
                                                   accum_out=accum), r, w, ecost("act", out, 0.1), tb)

        def ts(eng, out, in0, s1, s2, op0, op1, r, w):
            if s2 is None:
                if eng == "pool":
                    assert op0 == ALU.mult
                    one_col = cst[0:int(out.shape[0]), K_ONE:K_ONE + 1]
                    P.op(eng, lambda e: e.tensor_scalar(out=out, in0=in0, scalar1=s1, scalar2=one_col, op0=ALU.mult, op1=ALU.mult),
                         tuple(r) + ("cst",), w, ecost(eng, out))
                else:
                    P.op(eng, lambda e: e.tensor_scalar(out=out, in0=in0, scalar1=s1, scalar2=None, op0=op0), r, w, ecost(eng, out))
            else:
                P.op(eng, lambda e: e.tensor_scalar(out=out, in0=in0, scalar1=s1, scalar2=s2, op0=op0, op1=op1), r, w, ecost(eng, out))

        def tt(eng, out, in0, in1, op, r, w):
            P.op(eng, lambda e: e.tensor_tensor(out=out, in0=in0, in1=in1, op=op), r, w, ecost(eng, out))

        def stt(out, in0, scalar, in1, op0, op1, r, w):
            P.op("dve", lambda e: e.scalar_tensor_tensor(out=out, in0=in0, scalar=scalar, in1=in1, op0=op0, op1=op1), r, w,
                 ecost("dve", out))

        def cp(eng, out, in_, r, w):
            if eng == "act":
                P.op("act", lambda e: e.copy(out=out, in_=in_), r, w, ecost("act", out))
            else:
                P.op(eng, lambda e: e.tensor_copy(out=out, in_=in_), r, w, ecost(eng, out))

        def mm(items, r, w):
            fns = []
            c = 0.05
            for (o, l, rh, st, sp_) in items:
                fns.append(lambda e, o=o, l=l, rh=rh, st=st, sp_=sp_: e.matmul(o, lhsT=l, rhs=rh, start=st, stop=sp_))
                c += (max(fsz(rh), 64) / 2400.0 + 0.012) * (4.0 if l.dtype == F32 else 1.0)
            P.op("pe", fns, r, w, c)

        def trp(items, idn, r, w):
            fns = []
            c = 0.05
            for (o, i) in items:
                fns.append(lambda e, o=o, i=i: e.transpose(o, i, idn))
                c += 0.07
            P.op("pe", fns, r, w, c)

        def ndesc_of(ap):
            tot = 1
            for d in ap.shape:
                tot *= int(d)
            last = ap.ap[-1]
            run = int(last[1]) if int(last[0]) == 1 else 1
            return max(1, tot // max(run, 1))

        def dma(q, out, in_, r, w, **kw):
            nbytes = out.shape[0] * fsz(out) * 4
            nd = max(ndesc_of(out), ndesc_of(in_))
            P.dma(q, lambda e: e.dma_start(out=out, in_=in_, **kw), r, w, 2.0 + nbytes / 150e3 + nd * 0.002, nd)

        def memset(eng, ap, val, w):
            P.op(eng, lambda e: e.memset(ap, val), (), w, ecost(eng, ap))

        dma("sp", cst[:], cst_d[:, :], [], ["cst"])
        cp("dve", ident_bf[:], ident, ["cst"], ["ident_bf"])
        memset("dve", ones2[:], 1.0, ["ones2"])
        for i in range(8):
            memset("dve", psum[i][:], 0.0, ["P%d" % i])
        dma("sp", gfin_bc[:], nfin_d.partition_broadcast(128), [], ["gfin_bc"])
        dma("sp", pcol[:], pcol_d[:, :], [], ["gf8", "g8", "bcol", "cw", "cb", "lbl"])

        with ExitStack() as A:
            win = sb(A, "win", [128, 8, PIN], BF16)
            wout = sb(A, "wout", [128, 8, D], BF16)
            def wload(blocks, with_out=False):
                for (c0, cn) in blocks:
                    for kc in range(8):
                        dma("pool", win[:, kc, c0:c0 + cn], w_in_d[kc * 128:(kc + 1) * 128, c0:c0 + cn], [], ["win%d_%d" % (c0, kc)],
                            max_dma_last_dim=4096)
                if with_out:
                    for kc in range(8):
                        dma("pool", wout[:, kc, :], w_out_d[kc * 128:(kc + 1) * 128, :], [], ["wout_%d" % kc], max_dma_last_dim=4096)
            wload([(C_QK, 1024)])

            def wk(c0):
                return ["win%d_%d" % (c0, kc) for kc in range(8)]
            WOUTK = ["wout_%d" % kc for kc in range(8)]

            g8 = pcol[:, 0:8]; bcol = pcol[:, 16:32]; bcolh = sb(A, "bcolh", [128, 16])
            cw = pcol[:, 32:64].rearrange("p (c j) -> p c j", j=4); cb = pcol[:, 64:72]
            lbl = pcol[:, 72:80].rearrange("p (l h) -> p l h", h=4); c0t = sb(A, "c0t", [128, 4]); c1t = sb(A, "c1t", [128, 4]); nc1t = sb(A, "nc1t", [128, 4])
            gA_bc = sb(A, "gA_bc", [128, 512]); gB_bc = sb(A, "gB_bc", [128, 512])
            bhl = sb(A, "bhl", [2, 2056], BF16)
            NTA = 256
            xin = [sb(A, "xin%d" % i, [128, D]) for i in range(2)]
            xres = [sb(A, "xres%d" % i, [128, D]) for i in range(2)]
            xn = [sb(A, "xn%d" % i, [128, D], BF16) for i in range(2)]
            ssA = sb(A, "ssA", [128, 8]); rsT = sb(A, "rsT", [128, 8])
            hT = sb(A, "hT", [128, 8, NTA], BF16)
            raw = [sb(A, "raw%d" % i, [128, NTA + 3]) for i in range(2)]
            cacc = [sb(A, "cacc%d" % i, [128, NTA]) for i in range(2)]
            qkT = sb(A, "qkT", [128, 8, NTA], BF16)
            qsh = [sb(A, "qsh%d" % i, [128, NTA]) for i in range(2)]; thh = [sb(A, "thh%d" % i, [128, NTA]) for i in range(2)]
            lfh = [sb(A, "lfh%d" % i, [128, NTA]) for i in range(2)]; bh = [sb(A, "bh%d" % i, [128, NTA]) for i in range(2)]
            eh = [sb(A, "eh%d" % i, [128, NTA]) for i in range(2)]
            qmid = sb(A, "qmid", [128, 4, NTA], BF16); kmid = sb(A, "kmid", [128, 4, NTA], BF16)
            e2 = sb(A, "e2", [128, 4, 16])
            vaug = sb(A, "vaug", [128, 4, 130], BF16); vsc = sb(A, "vsc", [128, 4, 130], BF16)
            go = sb(A, "go", [128, 512]); gt = sb(A, "gt", [128, 8])
            vB = sb(A, "vB", [128, 512], BF16); gg = sb(A, "gg", [128, 512])
            g1 = sb(A, "g1", [128, 4]); g2 = sb(A, "g2", [128, 4]); g3 = sb(A, "g3", [128, 4]); lfA = sb(A, "lfA", [128, 4])
            eq = sb(A, "eq", [128, 4]); ek = sb(A, "ek", [128, 4]); bet = sb(A, "bet", [128, 4])
            eDG = sb(A, "eDG", [128, 64])
            mrun = sb(A, "mrun", [4, NSEQ]); bmx = sb(A, "bmx", [4, NSEQ]); dG4 = sb(A, "dG4", [4, NSEQ])
            PT = sb(A, "PT", [128, 4, 128], BF16); ktok = sb(A, "ktok", [128, 4, 128], BF16)
            dd = sb(A, "dd", [128, 4]); rr = sb(A, "rr", [128, 4]); ssq = sb(A, "ssq", [128, 4]); sc = sb(A, "sc", [128, 4]); t4 = sb(A, "t4", [128, 4])
            junk2 = sb(A, "junk2", [128, 128], BF16)
            mix = sb(A, "mix", [128, D], BF16); mixT = sb(A, "mixT", [128, 8, 128], BF16)
            emS = sb(A, "emS", [128, 64]); mblk = sb(A, "mblk", [4, 4, NSEQ])
            scv = xin[0]; cso = xres[1]
            nS = sb(A, "nS", [128, 64]); nO = sb(A, "nO", [128, 64])
            EARLY = []
            LATE = []
            hist = sb(A, "hist", [128, 8, 3]); Chat = sb(A, "Chat", [128, 4, 130]); S32 = sb(A, "S32", [128, 4, 128])
            Ap = ExitStack()
            HT = [hT, sb(Ap, "hT_1", [128, 8, NTA], BF16)]; QKT = [qkT, sb(Ap, "qkT_1", [128, 8, NTA], BF16)]
            QMT = [qmid, sb(Ap, "qmid_1", [128, 4, NTA], BF16)]; KMT = [kmid, sb(Ap, "kmid_1", [128, 4, NTA], BF16)]
            BMt = [sb(Ap, "bm_%d" % i, [128, 4, 16]) for i in range(2)]; NBMt = [sb(Ap, "nbm_%d" % i, [128, 4, 16]) for i in range(2)]
            E1t = [sb(Ap, "e1_%d" % i, [128, 4, 16]) for i in range(2)]; E2t = [e2, sb(Ap, "e2_1", [128, 4, 16])]
            Chb = sb(Ap, "Chb", [128, 4, 130], BF16)
            Smb = sb(Ap, "Smb", [128, 4, 128], BF16); Stmp = sb(Ap, "Stmp", [128, 4, 128])

            dma("sp", gA_bc[:], gA_d.partition_broadcast(128), [], ["gA_bc"])
            dma("sp", gB_bc[:], gB_d.partition_broadcast(128), [], ["gB_bc"])
            ts("dve", bcolh[:], bcol[:], 0.5, None, ALU.mult, None, ["bcol"], ["bcolh"])
            ts("dve", gA_bc[:], gA_bc[:], 0.5, None, ALU.mult, None, ["gA_bc"], ["gA_bc"])
            tt("dve", c0t[:], lbl[:, 0, :], lbl[:, 1, :], ALU.subtract, ["lbl"], ["c0t"])
            act(c1t[:], c0t[:], AF.Tanh, ["c0t"], ["c1t"], scale=0.5)
            ts("dve", c0t[:], c1t[:], 0.25, 0.75, ALU.mult, ALU.add, ["c1t"], ["c0t"])
            ts("dve", nc1t[:], c1t[:], 0.25, -0.25, ALU.mult, ALU.add, ["c1t"], ["nc1t"])
            ts("dve", c1t[:], nc1t[:], -1.0, None, ALU.mult, None, ["nc1t", "c1t"], ["c1t"])
            def bofs(col):
                return col - 1024 if col < 2056 else col - 3080 + 1032
            for (c0_, cn_) in [(1024, 1024), (2048, 8), (3080, 1024)]:
                bfull = xin[0][0:2, 0:cn_]; bhif = xin[1][0:2, 0:cn_]; bhi = xn[0][0:2, 0:cn_]
                dma("sp", bfull, b_in_d[:, c0_:c0_ + cn_].partition_broadcast(2), [], ["xin0"])
                cp("dve", bhi, bfull, ["xin0"], ["xn0"])
                cp("dve", bhif, bhi, ["xn0"], ["xin1"])
                tt("dve", bfull, bfull, bhif, ALU.subtract, ["xin0", "xin1"], ["xin0"])
                ts("dve", bhif, bhif, cst[0:2, K_SEL:K_SEL + 1], None, ALU.mult, None, ["xin1", "cst"], ["xin1"])
                stt(bhl[:, bofs(c0_):bofs(c0_) + cn_], bfull, cst[0:2, K_SEL + 1:K_SEL + 2], bhif, ALU.mult, ALU.add,
                    ["xin0", "xin1", "cst"], ["bhl"])

            memset("dve", vaug[:], 1.0, ["vaug"])
            memset("dve", Chat[:], 0.0, ["Chat"])
            memset("dve", S32[:], 0.0, ["S32"])
            memset("dve", mrun[:], 0.0, ["mrun"])
            memset("pool", hist[:], 0.0, ["hist"])

            def bcast_hb(dst, func_scale, src_key):
                mm([(psum[3][:, 0:64], onesf[0:4, :], mblk[:].rearrange("k h b -> k (h b)"), True, True)], ["mblk", "cst"], ["P3"])
                act(dst[:, 0:64].rearrange("p (b h) -> p h b", h=4), psum[3][:, 0:64].rearrange("p (h b) -> p h b", h=4),
                    AF.Exp, ["P3"], [src_key], scale=func_scale)

            def phaseA_super(x_src, row0, NT, TS, sample, stage="XY"):
                nsub = NT // TS
                sp_ = 0 if sample else (row0 // NT) % 2
                hT, qkT, qmid, kmid = HT[sp_], QKT[sp_], QMT[sp_], KMT[sp_]
                e2 = E2t[sp_]
                if not sample:
                    bm, nbm, e1 = BMt[sp_], NBMt[sp_], E1t[sp_]
                kHT, kQK, kQM, kKM = "hT%d" % sp_, "qkT%d" % sp_, "qmid%d" % sp_, "kmid%d" % sp_
                kBM, kNBM, kE1, kE2 = "bm%d" % sp_, "nbm%d" % sp_, "e1%d" % sp_, "e2%d" % sp_
                if "X" in stage:
                    for j in range(nsub):
                        xi = xin[j % 2]; xb_ = xn[j % 2]
                        rows = slice(row0 + j * TS, row0 + (j + 1) * TS)
                        dma("sp", xi[0:TS, :], x_src[rows, :], [], ["xin%d" % (j % 2)])
                        act(xb_[0:TS, :], xi[0:TS, :], AF.Square, ["xin%d" % (j % 2)], ["xn%d" % (j % 2), "ssA"], accum=ssA[0:TS, j:j + 1])
                        ts("dve", rsT[0:TS, j:j + 1], ssA[0:TS, j:j + 1], 1.0 / D, EPS, ALU.mult, ALU.add, ["ssA"], ["rsT"])
                        tt("pool", ssA[0:TS, j:j + 1], rsT[0:TS, j:j + 1], cst[0:TS, K_NH:K_NH + 1], ALU.pow, ["rsT", "cst"], ["ssA"])
                        act(xb_[0:TS, :], xi[0:TS, :], AF.Copy, ["ssA", "xin%d" % (j % 2)], ["xn%d" % (j % 2)], scale=ssA[0:TS, j:j + 1])
                        pT = psv(2, BF16).rearrange("p (c t) -> p c t", c=8)
                        trp([(pT[:, c, 0:TS], xb_[0:TS, c * 128:(c + 1) * 128]) for c in range(8)], ident_bf[0:TS, 0:TS],
                            ["xn%d" % (j % 2), "ident_bf"], ["P2"])
                        tt("dve", hT[:, :, j * TS:(j + 1) * TS], pT[:, :, 0:TS],
                           g8[:].unsqueeze(2).broadcast_to([128, 8, TS]), ALU.mult, ["P2", "g8"], [kHT])

                    if sample:
                        dma("sp", scv[0:48, :], sconv_d[:, :], [], ["xin0"])
                        for hf_ in range(2):
                            dma("sp", CS32[:, 8 * hf_:8 * hf_ + 8, 0:128], sC_d.rearrange("(b h) k v -> h k b v", h=4)[0][:, 8 * hf_:8 * hf_ + 8, :],
                                [], ["CS32_%d" % hf_])
                        for hf_ in range(2):
                            dma("sp", SSh[:, 8 * hf_:8 * hf_ + 8, :], sS_d.rearrange("(b h) k v -> h k b v", h=4)[0][:, 8 * hf_:8 * hf_ + 8, :],
                                [], ["SSh_%d" % hf_])
                        dma("sp", nS[:], sn_d.rearrange("g k -> k g"), [], ["nS"])
                        for f_ in EARLY:
                            f_()
                    for c in range(8):
                        pb = c % 2
                        col = C_QK + c * 128
                        mm([(psum[pb][:, 0:NT], win[:, kc, col:col + 128], hT[:, kc, 0:NT], kc == 0, kc == 7) for kc in range(8)],
                           wk(C_QK) + [kHT], ["P%d" % pb])
                        rw = raw[c % 2]; rk = "raw%d" % (c % 2)
                        ca = cacc[c % 2]; ck = "cacc%d" % (c % 2)
                        if sample:
                            rv = rw[:, 0:112].rearrange("p (b j) -> p b j", j=7)
                            trp([(psum[3][:, 0:48], scv[0:48, c * 128:(c + 1) * 128])], cst[0:48, K_ID:K_ID + 48], ["xin0", "cst"], ["P3"])
                            cp("dve", rv[:, :, 0:3], psum[3][:, 0:48].rearrange("p (b j) -> p b j", j=3), ["P3"], [rk])
                            act(rv[:, :, 3:7], psum[pb][:, 0:NT].rearrange("p (b t) -> p b t", t=4), AF.Identity,
                                ["P%d" % pb, "bcol"], [rk], bias=bcol[:, c:c + 1])
                            src = lambda j: rv[:, :, j:j + 4]
                            cav = ca[:, 0:NT].rearrange("p (b t) -> p b t", t=4)
                        else:
                            cp("pool", rw[:, 0:3], hist[:, c, :], ["hist"], [rk])
                            act(rw[:, 3:3 + NT], psum[pb][:, 0:NT], AF.Identity, ["P%d" % pb, "bcol"], [rk], bias=bcol[:, c:c + 1])
                            src = lambda j: rw[:, j:j + NT]
                            cav = ca[:, 0:NT]
                        ts("dve", cav, src(0), cw[:, c, 0:1], cb[:, c:c + 1], ALU.mult, ALU.add, [rk, "cw", "cb"], [ck])
                        for j in (1, 2, 3):
                            stt(cav, src(j), cw[:, c, j:j + 1], cav, ALU.mult, ALU.add, [rk, "cw", ck], [ck])
                        act(qkT[:, c, 0:NT], ca[:, 0:NT], AF.Silu, [ck], [kQK])
                        if sample:
                            cp("dve", scvT[:, :].rearrange("p (b j) -> p b j", j=3), rv[:, :, 4:7], [rk], ["scvT"])
                            trp([(psum[3][0:48, 0:128], scvT[:, 0:48])], ident, ["scvT", "cst"], ["P3"])
                            cp("act", cso[0:48, c * 128:(c + 1) * 128], psum[3][0:48, 0:128], ["P3"], ["xres1"])
                        else:
                            cp("pool", hist[:, c, :], rw[:, NT:NT + 3], [rk], ["hist"])
                    if sample:
                        LATE.append(lambda: dma("sp", convs_d[:, :], cso[0:48, :], ["xres1"], ["o_convs"]))

                    rvec = cst[:, K_RS:K_RS + 64] if sample else cst[:, K_RP:K_RP + 512]
                    for h in range(4):
                        q_ = qsh[h % 2]; t_ = thh[h % 2]; l_ = lfh[h % 2]; b_ = bh[h % 2]; e_ = eh[h % 2]
                        qk_, tk_, lk_, bk_, ekk_ = ["%s%d" % (n, h % 2) for n in ("qsh", "thh", "lfh", "bh", "eh")]
                        colq = C_QB + h * 128; colf = C_FB + h * 128
                        mm([(psum[0][:, 0:NT], win[:, kc, colq:colq + 128], hT[:, kc, 0:NT], kc == 0, kc == 7) for kc in range(8)],
                           wk(C_QB) + [kHT], ["P0"])
                        act(q_[:, 0:NT], psum[0][:, 0:NT], AF.Silu, ["P0", "bcol"], [qk_], bias=bcol[:, 8 + h:9 + h])
                        mm([(psum[1][:, 0:NT], win[:, kc, colf:colf + 128], hT[:, kc, 0:NT], kc == 0, kc == 7) for kc in range(8)],
                           wk(C_QB) + [kHT], ["P1"])
                        act(t_[:, 0:NT], psum[1][:, 0:NT], AF.Tanh, ["P1", "bcolh"], [tk_], scale=0.5, bias=bcolh[:, 12 + h:13 + h])
                        act(l_[:, 0:NT], t_[:, 0:NT], AF.Ln, [tk_, "c0t", "c1t"], [lk_], scale=c1t[:, h:h + 1], bias=c0t[:, h:h + 1])
                        ts("pool", t_[:, 0:NT], t_[:, 0:NT], nc1t[:, h:h + 1], c1t[:, h:h + 1], ALU.mult, ALU.add, [tk_, "nc1t", "c1t"], [tk_])
                        P.op("dve", lambda e, b_=b_, l_=l_: e.tensor_tensor_scan(out=b_[:, 0:NT], data0=rvec[:, 0:NT], data1=l_[:, 0:NT],
                                                                                 initial=0.0, op0=ALU.mult, op1=ALU.add), [lk_, "cst"], [bk_])
                        if sample:
                            act(e_[:, 0:NT], b_[:, 0:NT], AF.Exp, [bk_], [ekk_])
                            tt("pool", qmid[:, h, 0:NT], q_[:, 0:NT], e_[:, 0:NT], ALU.mult, [qk_, ekk_], [kQM])
                            act(e_[:, 0:NT], b_[:, 0:NT], AF.Exp, [bk_], [ekk_], scale=-1.0)
                            tt("dve", kmid[:, h, 0:NT], t_[:, 0:NT], e_[:, 0:NT], ALU.mult, [tk_, ekk_], [kKM])
                            act(e2[:, h, 0:NSEQ], b_[:, 3:NT:4], AF.Exp, [bk_], [kE2])
                        else:
                            cp("dve", bm[:, h, 0:nsub], b_[:, 63:NT:128], [bk_], [kBM])
                            ts("dve", nbm[:, h, 0:nsub], b_[:, 63:NT:128], -1.0, None, ALU.mult, None, [bk_], [kNBM])
                            for j in range(nsub):
                                act(e_[:, j * TS:(j + 1) * TS], b_[:, j * TS:(j + 1) * TS], AF.Exp, [bk_, kNBM], [ekk_], bias=nbm[:, h, j:j + 1])
                            tt("pool", qmid[:, h, 0:NT], q_[:, 0:NT], e_[:, 0:NT], ALU.mult, [qk_, ekk_], [kQM])
                            for j in range(nsub):
                                act(e_[:, j * TS:(j + 1) * TS], b_[:, j * TS:(j + 1) * TS], AF.Exp, [bk_, kBM], [ekk_], scale=-1.0,
                                    bias=bm[:, h, j:j + 1])
                            tt("dve", kmid[:, h, 0:NT], t_[:, 0:NT], e_[:, 0:NT], ALU.mult, [tk_, ekk_], [kKM])
                            act(e1[:, h, 0:nsub], bm[:, h, 0:nsub], AF.Exp, [kBM], [kE1])
                            tt("dve", e2[:, h, 0:nsub], b_[:, 127:NT:128], bm[:, h, 0:nsub], ALU.subtract, [bk_, kBM], [kE2])
                            act(e2[:, h, 0:nsub], e2[:, h, 0:nsub], AF.Exp, [kE2], [kE2])

                if "Y" in stage:
                    if (not sample) and row0 == 0:
                        wload([(C_VA, 1032), (C_VB, 1024)], with_out=True)
                    for j in range(nsub):
                        js = slice(j * TS, (j + 1) * TS)
                        rows = slice(row0 + j * TS, row0 + (j + 1) * TS)
                        first = (not sample) and row0 == 0 and j == 0
                        xr = xres[j % 2]; xrk = "xres%d" % (j % 2)
                        dma("sp", xr[0:TS, :], x_src[rows, :], [], [xrk])

                        def tokmm(pb, col, n, wkey):
                            items = [(psum[pb][0:TS, 0:n], hT[:, kc, js], win[:, kc, col:col + n], kc == 0, False) for kc in range(8)]
                            items.append((psum[pb][0:TS, 0:n], ones2[0:2, 0:TS], bhl[0:2, bofs(col):bofs(col) + n], False, True))
                            mm(items, wkey + [kHT, "bhl", "ones2"], ["P%d" % pb])

                        wA = wk(C_VA); wB = wk(C_VB)
                        tokmm(0, C_VA, 512, wA)
                        cp("act", vaug[0:TS, :, 0:128], psum[0][0:TS, :].rearrange("p (h v) -> p h v", h=4), ["P0"], ["vaug"])
                        tokmm(1, C_OA, 512, wA)
                        act(go[0:TS, :], psum[1][0:TS, :], AF.Tanh, ["P1"], ["go"], scale=0.5)
                        stt(go[0:TS, :], go[0:TS, :], 1.0, gA_bc[0:TS, :], ALU.add, ALU.mult, ["go", "gA_bc"], ["go"])
                        tokmm(0, C_G, 8, wA)
                        cp("dve", gt[0:TS, :], psum[0][0:TS, 0:8], ["P0"], ["gt"])
                        tokmm(1, C_VB, 512, wB)
                        cp("act", vB[0:TS, :], psum[1][0:TS, :], ["P1"], ["vB"])
                        tokmm(0, C_GB, 512, wB)
                        act(gg[0:TS, :], psum[0][0:TS, :], AF.Silu, ["P0"], ["gg"])
                        tt("dve", gg[0:TS, :], gg[0:TS, :], gB_bc[0:TS, :], ALU.mult, ["gg", "gB_bc"], ["gg"])

                        fr = gt[0:TS, 4:8]; ig = gt[0:TS, 0:4]
                        act(g1[0:TS, :], fr, AF.Abs, ["gt"], ["g1"])
                        act(g2[0:TS, :], g1[0:TS, :], AF.Exp, ["g1"], ["g2"], scale=-1.0)
                        act(g2[0:TS, :], g2[0:TS, :], AF.Ln, ["g2"], ["g2"], bias=1.0)
                        ts("dve", g3[0:TS, :], fr, 0.0, None, ALU.min, None, ["gt"], ["g3"])
                        tt("dve", lfA[0:TS, :], g3[0:TS, :], g2[0:TS, :], ALU.subtract, ["g3", "g2"], ["lfA"])
                        cum = triS if sample else tri[0:TS, 0:TS]
                        mm([(psum[3][0:TS, 0:4], cum, lfA[0:TS, :], True, True)], ["lfA", "cst"], ["P3"])
                        act(eq[0:TS, :], psum[3][0:TS, 0:4], AF.Exp, ["P3"], ["eq"], bias=LNQ)
                        tt("dve", bet[0:TS, :], ig, psum[3][0:TS, 0:4], ALU.subtract, ["gt", "P3"], ["bet"])
                        act(ek[0:TS, :], bet[0:TS, :], AF.Exp, ["bet"], ["ek"])
                        if sample:
                            tt("dve", lfm[:], lfA[0:64, :].unsqueeze(1).broadcast_to([64, NSEQ, 4]),
                               BM.unsqueeze(2).broadcast_to([64, NSEQ, 4]), ALU.mult, ["lfA", "cst"], ["lfm"])
                            mm([(psum[3][:, 64:128], onesf[0:64, :], lfm[:].rearrange("s b h -> s (b h)"), True, True)],
                               ["lfm", "cst"], ["P3"])
                            act(eDG[:, 0:64], psum[3][:, 64:128], AF.Exp, ["P3"], ["eDG"])
                        else:
                            mm([(psum[3][:, 64:68], onesf[0:TS, :], lfA[0:TS, :], True, True)], ["lfA", "cst"], ["P3"])
                            act(eDG[:, 0:4], psum[3][:, 64:68], AF.Exp, ["P3"], ["eDG"])
                        trp([(psum[3][0:4, 128:128 + TS], bet[0:TS, :])], cst[0:TS, K_ID:K_ID + TS], ["bet", "cst"], ["P3"])
                        trp([(psum[3][0:4, 256:256 + TS], lfA[0:TS, :])], cst[0:TS, K_ID:K_ID + TS], ["lfA", "cst"], ["P3"])
                        if sample:
                            P.op("dve", lambda e: e.tensor_reduce(out=bmx[:, :], in_=psum[3][0:4, 128:192].rearrange("h (b t) -> h b t", t=4),
                                                                   axis=AX.X, op=ALU.max), ["P3"], ["bmx"])
                            P.op("dve", lambda e: e.tensor_reduce(out=dG4[:, :], in_=psum[3][0:4, 256:320].rearrange("h (b t) -> h b t", t=4),
                                                                   axis=AX.X, op=ALU.add), ["P3"], ["dG4"])
                            tt("dve", mfin[:], m0T[:], bmx[:], ALU.max, ["m0T", "bmx"], ["mfin"])
                            tt("dve", mfin[:], mfin[:], dG4[:], ALU.add, ["mfin", "dG4"], ["mfin"])
                            LATE.append(lambda: dma("sp", ms_d.rearrange("b h -> h b"), mfin[:], ["mfin"], ["o_ms"]))
                            tt("dve", mblk[:], mfin[:].unsqueeze(1).broadcast_to([4, 4, NSEQ]),
                               cst[0:4, K_ID:K_ID + 4].unsqueeze(2).broadcast_to([4, 4, NSEQ]), ALU.mult, ["mfin", "cst"], ["mblk"])
                            bcast_hb(emF, -1.0, "emF")
                            tt("dve", emF[:, 0:64], emF[:, 0:64], eDG[:, 0:64], ALU.mult, ["emF", "eDG"], ["emF"])
                        else:
                            P.op("dve", lambda e: e.tensor_reduce(out=bmx[:, 0:1], in_=psum[3][0:4, 128:128 + TS], axis=AX.X, op=ALU.max),
                                 ["P3"], ["bmx"])
                            P.op("dve", lambda e: e.tensor_reduce(out=dG4[:, 0:1], in_=psum[3][0:4, 256:256 + TS], axis=AX.X, op=ALU.add),
                                 ["P3"], ["dG4"])
                            tt("dve", mrun[:, 0:1], mrun[:, 0:1], bmx[:, 0:1], ALU.max, ["mrun", "bmx"], ["mrun"])
                            tt("dve", mrun[:, 0:1], mrun[:, 0:1], dG4[:, 0:1], ALU.add, ["mrun", "dG4"], ["mrun"])
                        for h in range(4):
                            ts("pool", vsc[0:TS, h, :], vaug[0:TS, h, :], ek[0:TS, h:h + 1], None, ALU.mult, None, ["vaug", "ek"], ["vsc"])

                        maskA = triS if sample else tri[0:TS, 0:TS]
                        pS = psum[3][:].rearrange("p (h t) -> p h t", h=4)
                        mm([(pS[0:TS, h, 0:TS], qkT[:, 4 + h, js], qkT[:, h, js], True, True) for h in range(4)], [kQK], ["P3"])
                        tt("dve", PT[0:TS, :, 0:TS], pS[0:TS, :, 0:TS], maskA.unsqueeze(1).broadcast_to([TS, 4, TS]), ALU.mult,
                           ["P3", "cst"], ["PT"])
                        pTk = psv(2, BF16)[:, 0:512].rearrange("p (h k) -> p h k", h=4)
                        trp([(pTk[0:TS, h, :], qkT[:, 4 + h, js]) for h in range(4)], ident_bf[:], [kQK, "ident_bf"], ["P2"])
                        cp("act", ktok[0:TS, :, :], pTk[0:TS, :, :], ["P2"], ["ktok"])
                        pH = [psum[4][:].rearrange("p (h x) -> p h x", h=2), psum[5][:].rearrange("p (h x) -> p h x", h=2)]
                        pU = [psum[6][:].rearrange("p (h x) -> p h x", h=2), psum[7][:].rearrange("p (h x) -> p h x", h=2)]
                        if sample:
                            Cs_v = Cs_d.rearrange("(b h) k v -> h k b v", h=4)
                            ns_v = ns_d.rearrange("(b h) k -> h k b", h=4)
                            sC_v = sC_d.rearrange("(b h) k v -> h k b v", h=4)
                            sn_v = sn_d.rearrange("(b h) k -> h k b", h=4)
                            def c_load(h_, hf_):
                                dma("sp", CS32[:, 8 * hf_:8 * hf_ + 8, 0:128], sC_v[h_][:, 8 * hf_:8 * hf_ + 8, :], [], ["CS32_%d" % hf_])
                            for h in range(4):
                                tt("dve", qm[:], qkT[:, h, 0:64].unsqueeze(1).broadcast_to([128, NSEQ, 64]),
                                   cst[:, K_BMQ:K_BMQ + 1024].rearrange("p (b t) -> p b t", b=NSEQ), ALU.mult, [kQK, "cst"], ["qm"])
                                tt("pool", vm[:], vsc[0:64, h, :].unsqueeze(1).broadcast_to([64, NSEQ, 130]),
                                   BM.unsqueeze(2).broadcast_to([64, NSEQ, 130]), ALU.mult, ["vsc", "cst"], ["vm"])
                                for hf in range(2):
                                    bs = slice(8 * hf, 8 * hf + 8)
                                    gs = slice(h + 32 * hf, 32 * hf + 32, 4)
                                    ck_ = "CS32_%d" % hf; cbk_ = "CSb_%d" % hf
                                    cp("pool", CS32[:, bs, 128], nS[:, gs], ["nS"], [ck_])
                                    tt("dve", CS32[:, bs, 0:129], CS32[:, bs, 0:129], emS[:, gs].unsqueeze(2).broadcast_to([128, 8, 129]),
                                       ALU.mult, [ck_, "emS"], [ck_])
                                    cp("act", CSb[:, bs, 0:129], CS32[:, bs, 0:129], [ck_], [cbk_])
                                    items = []
                                    if hf == 0:
                                        items.append((pH[h // 2][0:TS, h % 2, 0:129], PT[0:TS, h, 0:TS], vsc[0:TS, h, 0:129], True, False))
                                    for b in range(8 * hf, 8 * hf + 8):
                                        items.append((pH[h // 2][0:TS, h % 2, 0:129], qm[:, b, :], CSb[:, b, 0:129], False, b == NSEQ - 1))
                                    mm(items, ["PT", "vsc", "qm", cbk_], ["P%d" % (4 + h // 2)])
                                    for b0 in range(8 * hf, 8 * hf + 8, 2):
                                        pu = pU[(b0 // 2) % 2]; puk = "P%d" % (6 + (b0 // 2) % 2)
                                        mm([(pu[:, bb, 0:129], ktok[0:64, h, :], vm[:, b0 + bb, 0:129], True, True) for bb in range(2)],
                                           ["ktok", "vm"], [puk])
                                        tt("dve", CS32[:, b0:b0 + 2, 0:129], CS32[:, b0:b0 + 2, 0:129], pu[:, :, 0:129], ALU.add, [ck_, puk], [ck_])
                                    tt("dve", CS32[:, bs, 0:129], CS32[:, bs, 0:129], emF[:, gs].unsqueeze(2).broadcast_to([128, 8, 129]),
                                       ALU.mult, [ck_, "emF"], [ck_])
                                    cp("pool", nO[:, gs], CS32[:, bs, 128], [ck_], ["nO"])
                                    dma("sp", Cs_v[h][:, bs, :], CS32[:, bs, 0:128], [ck_], ["o_Cs"])
                                    if h < 3:
                                        c_load(h + 1, hf)
                            LATE.append(lambda: dma("sp", ns_d.rearrange("g k -> k g"), nO[:], ["nO"], ["o_ns"]))
                        else:
                            for h in range(4):
                                items = [(pH[h // 2][0:TS, h % 2, 0:129], PT[0:TS, h, 0:TS], vsc[0:TS, h, 0:129], True, first)]
                                if not first:
                                    items.append((pH[h // 2][0:TS, h % 2, 0:129], qkT[:, h, js], Chb[:, h, 0:129], False, True))
                                mm(items, ["PT", "vsc", kQK, "Chb"], ["P%d" % (4 + h // 2)])
                        for hp in range(2):
                            tt("dve", dd[0:TS, 2 * hp:2 * hp + 2], pH[hp][0:TS, :, 128], eq[0:TS, 2 * hp:2 * hp + 2], ALU.mult,
                               ["P%d" % (4 + hp), "eq"], ["dd"])
                        act(dd[0:TS, :], dd[0:TS, :], AF.Abs, ["dd"], ["dd"])
                        ts("dve", dd[0:TS, :], dd[0:TS, :], 1.0, None, ALU.max, None, ["dd"], ["dd"])
                        P.op("dve", lambda e: e.reciprocal(out=rr[0:TS, :], in_=dd[0:TS, :]), ["dd"], ["rr"])
                        tt("dve", rr[0:TS, :], rr[0:TS, :], eq[0:TS, :], ALU.mult, ["rr", "eq"], ["rr"])
                        for h in range(4):
                            act(junk2[0:TS, :], pH[h // 2][0:TS, h % 2, 0:128], AF.Square, ["P%d" % (4 + h // 2)], ["junk2", "ssq"],
                                accum=ssq[0:TS, h:h + 1])
                        tt("dve", t4[0:TS, :], rr[0:TS, :], rr[0:TS, :], ALU.mult, ["rr"], ["t4"])
                        tt("dve", t4[0:TS, :], t4[0:TS, :], ssq[0:TS, :], ALU.mult, ["t4", "ssq"], ["t4"])
                        ts("dve", t4[0:TS, :], t4[0:TS, :], 1.0 / 128, EPS, ALU.mult, ALU.add, ["t4"], ["t4"])
                        tt("pool", sc[0:TS, :], t4[0:TS, :], cst[0:TS, K_NH:K_NH + 1].broadcast_to([TS, 4]), ALU.pow, ["t4", "cst"], ["sc"])
                        tt("dve", sc[0:TS, :], sc[0:TS, :], rr[0:TS, :], ALU.mult, ["sc", "rr"], ["sc"])
                        for h in range(4):
                            stt(mix[0:TS, h * 128:(h + 1) * 128], pH[h // 2][0:TS, h % 2, 0:128], sc[0:TS, h:h + 1],
                                go[0:TS, h * 128:(h + 1) * 128], ALU.mult, ALU.mult, ["P%d" % (4 + h // 2), "sc", "go"], ["mix"])
                        if not sample:
                            for h in range(4):
                                mm([(pU[h // 2][:, h % 2, 0:129], ktok[0:TS, h, :], vsc[0:TS, h, 0:129], True, True)],
                                   ["ktok", "vsc"], ["P%d" % (6 + h // 2)])
                            for h in range(4):
                                tt("dve", Chat[:, h, 0:129], Chat[:, h, 0:129], pU[h // 2][:, h % 2, 0:129], ALU.add,
                                   ["Chat", "P%d" % (6 + h // 2)], ["Chat"])
                                ts("dve", Chat[:, h, 0:130], Chat[:, h, 0:130], eDG[:, h:h + 1], None, ALU.mult, None, ["Chat", "eDG"], ["Chat"])
                            cp("act", Chb[:], Chat[:], ["Chat"], ["Chb"])

                        pA = psum[3][:].rearrange("p (h t) -> p h t", h=4)
                        if sample:
                            mm([(pA[0:64, h, 0:64], kmid[:, h, 0:64], qmid[:, h, 0:64], True, True) for h in range(4)], [kKM, kQM], ["P3"])
                        else:
                            items = []
                            for h in range(4):
                                items.append((pA[0:64, h, 0:128], kmid[:, h, j * 128:j * 128 + 64], qmid[:, h, js], True, True))
                                items.append((pA[64:128, h, 64:128], kmid[:, h, j * 128 + 64:j * 128 + 128], qmid[:, h, j * 128 + 64:j * 128 + 128], True, True))
                            mm(items, [kKM, kQM], ["P3"])
                        tt("dve", PT[0:TS, :, 0:TS], pA[0:TS, :, 0:TS], maskA.unsqueeze(1).broadcast_to([TS, 4, TS]), ALU.mult,
                           ["P3", "cst"], ["PT"])
                        trp([(pTk[0:TS, h, :], kmid[:, h, js]) for h in range(4)], ident_bf[:], [kKM, "ident_bf"], ["P2"])
                        cp("act", ktok[0:TS, :, :], pTk[0:TS, :, :], ["P2"], ["ktok"])
                        pO = psum[4][:].rearrange("p (h v) -> p h v", h=4)
                        if sample:
                            Ss_v = Ss_d.rearrange("(b h) k v -> h k b v", h=4)
                            sS_v = sS_d.rearrange("(b h) k v -> h k b v", h=4)
                            def s_load(h_, hf_):
                                dma("sp", SSh[:, 8 * hf_:8 * hf_ + 8, :], sS_v[h_][:, 8 * hf_:8 * hf_ + 8, :], [], ["SSh_%d" % hf_])
                            for h in range(4):
                                tt("dve", qm[:], qmid[:, h, 0:64].unsqueeze(1).broadcast_to([128, NSEQ, 64]),
                                   cst[:, K_BMQ:K_BMQ + 1024].rearrange("p (b t) -> p b t", b=NSEQ), ALU.mult, [kQM, "cst"], ["qm"])
                                tt("pool", vm[:, :, 0:128], vB[0:64, h * 128:(h + 1) * 128].unsqueeze(1).broadcast_to([64, NSEQ, 128]),
                                   BM.unsqueeze(2).broadcast_to([64, NSEQ, 128]), ALU.mult, ["vB", "cst"], ["vm"])
                                for hf in range(2):
                                    bs = slice(8 * hf, 8 * hf + 8)
                                    sk_ = "SSh_%d" % hf; sbk_ = "SSb_%d" % hf
                                    cp("act", SSb[:, bs, :], SSh[:, bs, :], [sk_], [sbk_])
                                    items = []
                                    if hf == 0:
                                        items.append((pO[0:TS, h, :], PT[0:TS, h, 0:TS], vB[0:TS, h * 128:(h + 1) * 128], True, False))
                                    for b in range(8 * hf, 8 * hf + 8):
                                        items.append((pO[0:TS, h, :], qm[:, b, :], SSb[:, b, :], False, b == NSEQ - 1))
                                    mm(items, ["PT", "vB", "qm", sbk_], ["P4"])
                                    for b0 in range(8 * hf, 8 * hf + 8, 4):
                                        pbk = 6 + (b0 // 4) % 2
                                        pu = psum[pbk][:].rearrange("p (b v) -> p b v", b=4)
                                        mm([(pu[:, bb, :], ktok[0:64, h, :], vm[:, b0 + bb, 0:128], True, True) for bb in range(4)],
                                           ["ktok", "vm"], ["P%d" % pbk])
                                        tt("dve", SSh[:, b0:b0 + 4, :], SSh[:, b0:b0 + 4, :], pu[:, :, :], ALU.add, [sk_, "P%d" % pbk], [sk_])
                                    tt("dve", SSh[:, bs, :], SSh[:, bs, :], e2[:, h, 8 * hf:8 * hf + 8].unsqueeze(2).broadcast_to([128, 8, 128]), ALU.mult,
                                       [sk_, kE2], [sk_])
                                    dma("sp", Ss_v[h][:, bs, :], SSh[:, bs, :], [sk_], ["o_Ss"])
                                    if h < 3:
                                        s_load(h + 1, hf)
                        else:
                            if not first:
                                for h in range(4):
                                    act(Smb[:, h, :], S32[:, h, :], AF.Copy, ["S32", kE1], ["Smb"], scale=e1[:, h, j:j + 1])
                            for h in range(4):
                                items = [(pO[0:TS, h, :], PT[0:TS, h, 0:TS], vB[0:TS, h * 128:(h + 1) * 128], True, first)]
                                if not first:
                                    items.append((pO[0:TS, h, :], qmid[:, h, js], Smb[:, h, :], False, True))
                                mm(items, ["PT", "vB", kQM, "Smb"], ["P4"])
                        for h in range(4):
                            act(junk2[0:TS, :], pO[0:TS, h, :], AF.Square, ["P4"], ["junk2", "ssq"], accum=ssq[0:TS, h:h + 1])
                        ts("dve", t4[0:TS, :], ssq[0:TS, :], 1.0 / 128, EPS, ALU.mult, ALU.add, ["ssq"], ["t4"])
                        tt("pool", sc[0:TS, :], t4[0:TS, :], cst[0:TS, K_NH:K_NH + 1].broadcast_to([TS, 4]), ALU.pow, ["t4", "cst"], ["sc"])
                        for h in range(4):
                            stt(mix[0:TS, 512 + h * 128:512 + (h + 1) * 128], pO[0:TS, h, :], sc[0:TS, h:h + 1],
                                gg[0:TS, h * 128:(h + 1) * 128], ALU.mult, ALU.mult, ["P4", "sc", "gg"], ["mix"])
                        if not sample:
                            pUB = psum[6][:].rearrange("p (h v) -> p h v", h=4)
                            mm([(pUB[:, h, :], ktok[0:TS, h, :], vB[0:TS, h * 128:(h + 1) * 128], True, True) for h in range(4)],
                               ["ktok", "vB"], ["P6"])
                            for h in range(4):
                                stt(Stmp[:, h, :], S32[:, h, :], e1[:, h, j:j + 1], pUB[:, h, :], ALU.mult, ALU.add, ["S32", kE1, "P6"], ["Stmp"])
                                ts("dve", S32[:, h, :], Stmp[:, h, :], e2[:, h, j:j + 1], None, ALU.mult, None, ["Stmp", kE2], ["S32"])

                        pTm = psv(2, BF16).rearrange("p (c t) -> p c t", c=8)
                        trp([(pTm[:, c, 0:TS], mix[0:TS, c * 128:(c + 1) * 128]) for c in range(8)], ident_bf[0:TS, 0:TS],
                            ["mix", "ident_bf"], ["P2"])
                        cp("act", mixT[:, :, 0:TS], pTm[:, :, 0:TS], ["P2"], ["mixT"])
                        for dh in range(2):
                            mm([(psum[dh][0:TS, :], mixT[:, kc, 0:TS], wout[:, kc, dh * 512:(dh + 1) * 512], kc == 0, kc == 7) for kc in range(8)],
                               ["mixT"] + WOUTK, ["P%d" % dh])
                            tt("dve", xr[0:TS, dh * 512:(dh + 1) * 512], xr[0:TS, dh * 512:(dh + 1) * 512], psum[dh][0:TS, :], ALU.add,
                               [xrk, "P%d" % dh], [xrk])
                        x2rows = slice((T if sample else 0) + row0 + j * TS, (T if sample else 0) + row0 + (j + 1) * TS)
                        dma("sp", x2_d[x2rows, :], xr[0:TS, :], [xrk], ["x2d"])

            P.finish_phase(reorder=False, skip_queues=("pool",))
            wload([(C_QB, 1024)])
            NST = T // NTA
            for st_ in range(NST):
                phaseA_super(xp_d, st_ * NTA, NTA, 128, False, "XY")
            P.finish_phase(reorder=True)
            Ap.close()
            As = ExitStack()
            CS32 = sb(As, "CS32", [128, NSEQ, 130]); CSb = sb(As, "CSb", [128, NSEQ, 130], BF16)
            SSh = sb(As, "SSh", [128, NSEQ, 128]); SSb = sb(As, "SSb", [128, NSEQ, 128], BF16)
            qm = sb(As, "qm", [128, NSEQ, 64], BF16); vm = sb(As, "vm", [64, NSEQ, 130], BF16)
            scvT = sb(As, "scvT", [128, 48]); lfm = sb(As, "lfm", [64, NSEQ, 4])
            m0T = sb(As, "m0T", [4, NSEQ]); emF = sb(As, "emF", [128, 64]); mfin = sb(As, "mfin", [4, NSEQ])
            for j_ in range(3):
                EARLY.append(lambda j_=j_: dma("sp", convp_d[j_].rearrange("(c p) -> p c", p=128), hist[:, :, j_], ["hist"], ["o_convp"]))
            EARLY.append(lambda: dma("sp", Sp_d.rearrange("h k v -> k h v"), S32[:], ["S32"], ["o_Sp"]))
            EARLY.append(lambda: dma("sp", mp_d[:, :], mrun[:, 0:1], ["mrun"], ["o_mp"]))
            tt("dve", mblk[:, :, 0:1], mrun[:, 0:1].unsqueeze(1).broadcast_to([4, 4, 1]),
               cst[0:4, K_ID:K_ID + 4].unsqueeze(2), ALU.mult, ["mrun", "cst"], ["mblk"])
            mm([(psum[3][:, 0:4], onesf[0:4, :], mblk[:, :, 0], True, True)], ["mblk", "cst"], ["P3"])
            act(emS[:, 0:4], psum[3][:, 0:4], AF.Exp, ["P3"], ["emS"], scale=-1.0)
            for h in range(4):
                ts("dve", Chat[:, h, 0:129], Chat[:, h, 0:129], emS[:, h:h + 1], None, ALU.mult, None, ["Chat", "emS"], ["Chat"])
            EARLY.append(lambda: dma("sp", Cp_d.rearrange("h k v -> k h v"), Chat[:, :, 0:128], ["Chat"], ["o_Cp"]))
            EARLY.append(lambda: dma("sp", np_d.rearrange("h k -> k h"), Chat[:, :, 128], ["Chat"], ["o_np"]))
            dma("sp", m0T[:], sm_d.rearrange("b h -> h b"), [], ["m0T"])
            tt("dve", mblk[:], m0T[:].unsqueeze(1).broadcast_to([4, 4, NSEQ]),
               cst[0:4, K_ID:K_ID + 4].unsqueeze(2).broadcast_to([4, 4, NSEQ]), ALU.mult, ["m0T", "cst"], ["mblk"])
            bcast_hb(emS, 1.0, "emS")
            phaseA_super(xs_d, 0, 64, 64, True)
            for f_ in LATE:
                f_()

            P.finish_phase(reorder=False)
            As.close()

        with ExitStack() as B:
            wg = sb(B, "wg", [128, 8, DFF], BF16); wu = sb(B, "wu", [128, 8, DFF], BF16); wd = sb(B, "wd", [128, NFC, D], BF16)
            FCB = 6
            NCB = (NFC + FCB - 1) // FCB
            for cb in range(NCB):
                c0_, c1_ = cb * FCB * 128, min(DFF, (cb + 1) * FCB * 128)
                for kc in range(8):
                    dma("pool", wg[:, kc, c0_:c1_], wg_d[kc * 128:(kc + 1) * 128, c0_:c1_], [], ["wg_%d_%d" % (cb, kc)], max_dma_last_dim=4096)
                for kc in range(8):
                    dma("pool", wu[:, kc, c0_:c1_], wu_d[kc * 128:(kc + 1) * 128, c0_:c1_], [], ["wu_%d_%d" % (cb, kc)], max_dma_last_dim=4096)
            for fc in range(NFC):
                dma("pool", wd[:, fc, :], wd_d[fc * 128:(fc + 1) * 128, :], [], ["wd_%d" % fc], max_dma_last_dim=4096)
            WGK = lambda fc: ["wg_%d_%d" % (fc // FCB, kc) for kc in range(8)]
            WUK = lambda fc: ["wu_%d_%d" % (fc // FCB, kc) for kc in range(8)]
            WDK = ["wd_%d" % fc for fc in range(NFC)]
            NTB = 512
            xa = [sb(B, "xa%d" % i, [128, D]) for i in range(2)]
            xrb = [sb(B, "xrb%d" % i, [128, D]) for i in range(2)]
            junkB = sb(B, "junkB", [128, D], BF16)
            xnB = [sb(B, "xnB%d" % i, [128, D], BF16) for i in range(2)]
            ssB = sb(B, "ssB", [128, 8]); rsB = sb(B, "rsB", [128, 8])
            h2T = sb(B, "h2T", [128, 8, NTB], BF16)
            ffT = sb(B, "ffT", [128, NFC, NTB], BF16)
            sgt = [sb(B, "sgt%d" % i, [128, NTB]) for i in range(2)]

            def phaseB_super(row0, NT, TS, y_dst, yrow0):
                nsub = NT // TS
                for j in range(nsub):
                    xi = xa[j % 2]; xb_ = xnB[j % 2]
                    rows = slice(row0 + j * TS, row0 + (j + 1) * TS)
                    dma("sp", xi[0:TS, :], x2_d[rows, :], ["x2d"], ["xa%d" % (j % 2)])
                    act(junkB[0:TS, :], xi[0:TS, :], AF.Square, ["xa%d" % (j % 2)], ["junkB", "ssB"], accum=ssB[0:TS, j:j + 1])
                    ts("dve", rsB[0:TS, j:j + 1], ssB[0:TS, j:j + 1], 1.0 / D, EPS, ALU.mult, ALU.add, ["ssB"], ["rsB"])
                    tt("pool", ssB[0:TS, j:j + 1], rsB[0:TS, j:j + 1], cst[0:TS, K_NH:K_NH + 1], ALU.pow, ["rsB", "cst"], ["ssB"])
                    act(xb_[0:TS, :], xi[0:TS, :], AF.Copy, ["ssB", "xa%d" % (j % 2)], ["xnB%d" % (j % 2)], scale=ssB[0:TS, j:j + 1])
                    pT = psv(6, BF16).rearrange("p (c t) -> p c t", c=8)
                    trp([(pT[:, c, 0:TS], xb_[0:TS, c * 128:(c + 1) * 128]) for c in range(8)], ident_bf[0:TS, 0:TS],
                        ["xnB%d" % (j % 2), "ident_bf"], ["P6"])
                    tt("dve", h2T[:, :, j * TS:(j + 1) * TS], pT[:, :, 0:TS],
                       gf8[:].unsqueeze(2).broadcast_to([128, 8, TS]), ALU.mult, ["P6", "gf8"], ["h2T"])
                for fc in range(NFC):
                    pg = 2 * (fc % 2); pu = pg + 1
                    mm([(psum[pg][:, 0:NT], wg[:, kc, fc * 128:(fc + 1) * 128], h2T[:, kc, 0:NT], kc == 0, kc == 7) for kc in range(8)],
                       WGK(fc) + ["h2T"], ["P%d" % pg])
                    mm([(psum[pu][:, 0:NT], wu[:, kc, fc * 128:(fc + 1) * 128], h2T[:, kc, 0:NT], kc == 0, kc == 7) for kc in range(8)],
                       WUK(fc) + ["h2T"], ["P%d" % pu])
                    s_ = sgt[fc % 2]; sk = "sgt%d" % (fc % 2)
                    act(s_[:, 0:NT], psum[pg][:, 0:NT], AF.Silu, ["P%d" % pg], [sk])
                    tt("dve", ffT[:, fc, 0:NT], s_[:, 0:NT], psum[pu][:, 0:NT], ALU.mult, [sk, "P%d" % pu], ["ffT"])
                for j in range(nsub):
                    xr = xrb[j % 2]; xrk = "xrb%d" % (j % 2)
                    rows = slice(row0 + j * TS, row0 + (j + 1) * TS)
                    dma("sp", xr[0:TS, :], x2_d[rows, :], ["x2d"], [xrk])
                    for dh in range(2):
                        pb = 4 + dh
                        mm([(psum[pb][0:TS, :], ffT[:, fc, j * TS:(j + 1) * TS], wd[:, fc, dh * 512:(dh + 1) * 512], fc == 0, fc == NFC - 1)
                            for fc in range(NFC)], ["ffT"] + WDK, ["P%d" % pb])
                        tt("dve", xr[0:TS, dh * 512:(dh + 1) * 512], xr[0:TS, dh * 512:(dh + 1) * 512], psum[pb][0:TS, :], ALU.add,
                           [xrk, "P%d" % pb], [xrk])
                    act(junkB[0:TS, :], xr[0:TS, :], AF.Square, [xrk], ["junkB", "ssB"], accum=ssB[0:TS, 4 + j:5 + j])
                    ts("dve", rsB[0:TS, 4 + j:5 + j], ssB[0:TS, 4 + j:5 + j], 1.0 / D, EPS, ALU.mult, ALU.add, ["ssB"], ["rsB"])
                    tt("pool", ssB[0:TS, 4 + j:5 + j], rsB[0:TS, 4 + j:5 + j], cst[0:TS, K_NH:K_NH + 1], ALU.pow, ["rsB", "cst"], ["ssB"])
                    stt(xr[0:TS, :], xr[0:TS, :], ssB[0:TS, 4 + j:5 + j], gfin_bc[0:TS, :], ALU.mult, ALU.mult, [xrk, "ssB", "gfin_bc"], [xrk])
                    orow = slice(yrow0 + j * TS, yrow0 + (j + 1) * TS)
                    dma("sp", y_dst[orow, :], xr[0:TS, :], [xrk], ["o_y"])

            for st_ in range(T // NTB):
                phaseB_super(st_ * NTB, NTB, 128, yp_d, st_ * NTB)
            phaseB_super(T, 64, 64, ys_d, 0)
            P.finish_phase()
    return nc


_NC_CACHE = {}


def kernel(**inputs):
    f = lambda a: np.ascontiguousarray(np.asarray(a, dtype=np.float32))
    if "nc" not in _NC_CACHE:
        _NC_CACHE["nc"] = build_program()
    nc = _NC_CACHE["nc"]
    cst = make_consts()
    shared = {
        "w_in": f(inputs["w_in"][0]), "b_in": f(inputs["b_in"][0]).reshape(1, PIN), "norm_mix": f(inputs["norm_mix"][0]),
        "conv_w": f(inputs["conv_w"][0]), "conv_b": f(inputs["conv_b"][0]), "mlstm_norm": f(inputs["mlstm_norm"][0]).reshape(1, 512),
        "lb_logits": f(inputs["hgrn_lb_logits"]), "hgrn_norm": f(inputs["hgrn_norm"][0]).reshape(1, 512),
        "w_out": f(inputs["w_out"][0]), "norm_ffn": f(inputs["norm_ffn"][0]), "w_gate": f(inputs["w_gate"][0]),
        "w_up": f(inputs["w_up"][0]), "w_down": f(inputs["w_down"][0]), "norm_final": f(inputs["norm_final"]).reshape(1, D),
        "cst": cst,
    }
    colp = lambda v: f(v).reshape(-1, 128).T
    b_in0 = f(inputs["b_in"][0])
    cwp = np.stack([colp(inputs["conv_w"][0][j]) for j in range(4)], axis=2).reshape(128, 32)
    lbp = np.stack([colp(inputs["hgrn_lb_logits"][l]) for l in range(2)], axis=1).reshape(128, 8)
    shared["pcol"] = np.ascontiguousarray(np.concatenate([
        colp(inputs["norm_mix"][0]), colp(inputs["norm_ffn"][0]), colp(b_in0[C_QK:C_QK + 1024]), colp(b_in0[C_QB:C_QB + 1024]),
        cwp, colp(inputs["conv_b"][0]), lbp], axis=1).astype(np.float32))
    in_maps = []
    for c in range(NCORES):
        sl = slice(c * NSEQ, (c + 1) * NSEQ)
        m = dict(shared)
        m["xp"] = f(inputs["x_prompt"][c])
        m["xs"] = f(inputs["x_sample"][sl]).reshape(NS, D)
        m["sconv"] = f(inputs["state_conv"][0, sl]).reshape(NSEQ * 3, D)
        m["sC"] = f(inputs["state_mlstm_C"][0, sl]).reshape(NSEQ * 4, 128, 128)
        m["sn"] = f(inputs["state_mlstm_n"][0, sl]).reshape(NSEQ * 4, 128)
        m["sm"] = f(inputs["state_mlstm_m"][0, sl]).reshape(NSEQ, 4)
        m["sS"] = f(inputs["state_hgrn_S"][0, sl]).reshape(NSEQ * 4, 128, 128)
        in_maps.append(m)
    res = run_bass_kernel_spmd(nc, in_maps, core_ids=list(range(NCORES)))
    R = res.results
    cat = lambda k, shp: np.stack([np.asarray(R[c][k], dtype=np.float32).reshape(shp) for c in range(NCORES)], axis=0)
    y_prompt = cat("y_prompt", (T, D))
    y_sample = cat("y_sample", (NSEQ, TSQ, D)).reshape(NCORES * NSEQ, TSQ, D)
    conv_p = cat("conv_p", (3, D))[None]
    C_p = cat("C_p", (4, 128, 128))[None]
    n_p = cat("n_p", (4, 128))[None]
    m_p = cat("m_p", (4,))[None]
    S_p = cat("S_p", (4, 128, 128))[None]
    conv_s = cat("conv_s", (NSEQ, 3, D)).reshape(NCORES * NSEQ, 3, D)[None]
    C_s = cat("C_s", (NSEQ, 4, 128, 128)).reshape(NCORES * NSEQ, 4, 128, 128)[None]
    n_s = cat("n_s", (NSEQ, 4, 128)).reshape(NCORES * NSEQ, 4, 128)[None]
    m_s = cat("m_s", (NSEQ, 4)).reshape(NCORES * NSEQ, 4)[None]
    S_s = cat("S_s", (NSEQ, 4, 128, 128)).reshape(NCORES * NSEQ, 4, 128, 128)[None]
    return (y_prompt, y_sample, conv_p, C_p, n_p, m_p, S_p, conv_s, C_s, n_s, m_s, S_s)
```
